# Optimizing a Trainium2 kernel written in Bass

```python
import math
import jax, jax.numpy as jnp
from jax import lax
import numpy as np

D_MODEL = 1024
BATCH = 16
SEQ = 2048
DEPTH = 4

CHUNK = 64
N_MIXERS = 2

MLA_HEADS = 8
MLA_Q_LORA = 512
MLA_KV_LORA = 256
MLA_NOPE = 128
MLA_ROPE = 64
MLA_V = 128
ROPE_THETA = 10000.0
Q_BLOCK = 128

ML_HEADS = 4
ML_DV = D_MODEL // ML_HEADS
ML_DK = ML_DV // 2

D_FF = 2816
CONV_W = 3

EPS = 1e-6
N_MLA_LAYERS = (DEPTH + N_MIXERS - 1) // N_MIXERS
N_ML_LAYERS = DEPTH // N_MIXERS

kernel_name = "hybrid_mla_mlstm_convffn_adaln"


def rms_norm(x, gain=None):
    x32 = x.astype(jnp.float32)
    y = x32 * lax.rsqrt(jnp.mean(x32 * x32, axis=-1, keepdims=True) + EPS)
    if gain is not None:
        y = y * gain.astype(jnp.float32)
    return y.astype(x.dtype)


def rope_tables(positions):
    inv_freq = 1.0 / (ROPE_THETA ** (jnp.arange(0, MLA_ROPE, 2, dtype=jnp.float32) / MLA_ROPE))
    ang = positions.astype(jnp.float32)[..., None] * inv_freq
    return jnp.cos(ang), jnp.sin(ang)


def apply_rope(t, cos, sin):
    t32 = t.astype(jnp.float32)
    t1, t2 = jnp.split(t32, 2, axis=-1)
    out = jnp.concatenate([t1 * cos - t2 * sin, t1 * sin + t2 * cos], axis=-1)
    return out.astype(t.dtype)


def mla_mixer(h, cos, sin, w_in, q_norm, w_q_up, kv_norm, w_kv_up, w_out):
    B, S, _ = h.shape
    proj = h @ w_in
    cq, ckv, k_rope = jnp.split(proj, [MLA_Q_LORA, MLA_Q_LORA + MLA_KV_LORA], axis=-1)
    q = (rms_norm(cq, q_norm) @ w_q_up).reshape(B, S, MLA_HEADS, MLA_NOPE + MLA_ROPE)
    q_nope = q[..., :MLA_NOPE]
    q_rope = apply_rope(q[..., MLA_NOPE:], cos[:, :, None, :], sin[:, :, None, :])
    k_rope = apply_rope(k_rope, cos, sin)
    kv = (rms_norm(ckv, kv_norm) @ w_kv_up).reshape(B, S, MLA_HEADS, MLA_NOPE + MLA_V)
    k_nope, v = kv[..., :MLA_NOPE], kv[..., MLA_NOPE:]
    scale = (MLA_NOPE + MLA_ROPE) ** -0.5
    n_blk = S // Q_BLOCK
    k_chunk = jnp.arange(S) // CHUNK

    def to_blocks(t):
        return t.reshape(B, n_blk, Q_BLOCK, *t.shape[2:]).swapaxes(0, 1)

    def attend(args):
        qn, qr, blk = args
        s = (jnp.einsum('bqhd,bkhd->bhqk', qn, k_nope)
             + jnp.einsum('bqhr,bkr->bhqk', qr, k_rope)).astype(jnp.float32) * scale
        q_chunk = (blk * Q_BLOCK + jnp.arange(Q_BLOCK)) // CHUNK
        allowed = k_chunk[None, :] <= q_chunk[:, None]
        s = jnp.where(allowed, s, -jnp.inf)
        p = jax.nn.softmax(s, axis=-1).astype(v.dtype)
        return jnp.einsum('bhqk,bkhd->bqhd', p, v)

    o = lax.map(attend, (to_blocks(q_nope), to_blocks(q_rope), jnp.arange(n_blk)))
    o = o.swapaxes(0, 1).reshape(B, S, MLA_HEADS * MLA_V)
    return o @ w_out


def mlstm_mixer(h, w_in, b_gates, head_norm, w_out):
    B, S, _ = h.shape
    NC = S // CHUNK
    H = ML_HEADS
    qk_w, v_w = H * ML_DK, H * ML_DV
    proj = h @ w_in
    q, k, v, o, gates = jnp.split(proj, [qk_w, 2 * qk_w, 2 * qk_w + v_w, 2 * qk_w + 2 * v_w], axis=-1)
    gates = gates.astype(jnp.float32) + b_gates.astype(jnp.float32)
    i_pre, f_pre = gates[..., :H], gates[..., H:]
    log_f = jax.nn.log_sigmoid(f_pre)

    def to_chunks(t, d):
        return t.astype(jnp.float32).reshape(B, NC, CHUNK, H, d).transpose(1, 0, 3, 2, 4)

    def gate_chunks(t):
        return t.reshape(B, NC, CHUNK, H).transpose(1, 0, 3, 2)

    qc = to_chunks(q, ML_DK)
    kc = to_chunks(k, ML_DK) * (ML_DK ** -0.5)
    vc = to_chunks(v, ML_DV)
    ic = gate_chunks(i_pre)
    bc = lax.cumsum(gate_chunks(log_f), axis=3)
    causal = jnp.tril(jnp.ones((CHUNK, CHUNK), dtype=bool))

    def step(carry, inp):
        C, n, m = carry
        qt, kt, vt, it, bt = inp
        dmat = jnp.where(causal, bt[..., :, None] - bt[..., None, :] + it[..., None, :], -jnp.inf)
        inter_log = bt + m[..., None]
        m_t = jnp.maximum(inter_log, jnp.max(dmat, axis=-1))
        inter_w = jnp.exp(inter_log - m_t)
        s_mat = jnp.einsum('bhtd,bhsd->bhts', qt, kt) * jnp.exp(dmat - m_t[..., None])
        num = (inter_w[..., None] * jnp.einsum('bhvd,bhtd->bhtv', C, qt)
               + jnp.einsum('bhts,bhsv->bhtv', s_mat, vt))
        den = inter_w * jnp.einsum('bhd,bhtd->bht', n, qt) + jnp.sum(s_mat, axis=-1)
        h_out = num / jnp.maximum(jnp.abs(den), jnp.exp(-m_t))[..., None]
        b_last = bt[..., -1]
        w_log = b_last[..., None] - bt + it
        m_new = jnp.maximum(b_last + m, jnp.max(w_log, axis=-1))
        decay = jnp.exp(b_last + m - m_new)
        ws = jnp.exp(w_log - m_new[..., None])
        C_new = decay[..., None, None] * C + jnp.einsum('bhs,bhsv,bhsd->bhvd', ws, vt, kt)
        n_new = decay[..., None] * n + jnp.einsum('bhs,bhsd->bhd', ws, kt)
        return (C_new, n_new, m_new), h_out

    init = (jnp.zeros((B, H, ML_DV, ML_DK), jnp.float32),
            jnp.zeros((B, H, ML_DK), jnp.float32),
            jnp.zeros((B, H), jnp.float32))
    _, hc = lax.scan(step, init, (qc, kc, vc, ic, bc))
    hs = hc.transpose(1, 0, 3, 2, 4).reshape(B, S, H, ML_DV)
    hs = hs * lax.rsqrt(jnp.mean(hs * hs, axis=-1, keepdims=True) + EPS)
    hs = hs * head_norm.astype(jnp.float32).reshape(H, ML_DV)
    y = hs.reshape(B, S, v_w) * jax.nn.sigmoid(o.astype(jnp.float32))
    return y.astype(h.dtype) @ w_out


def conv_ffn(h, w_up, conv_w, conv_b, w_down):
    a, g = jnp.split(h @ w_up, 2, axis=-1)
    a = lax.conv_general_dilated(a, conv_w[:, None, :], window_strides=(1,),
                                 padding=[(CONV_W - 1, 0)],
                                 dimension_numbers=('NWC', 'WIO', 'NWC'),
                                 feature_group_count=D_FF) + conv_b
    return (jax.nn.gelu(a, approximate=False) * g) @ w_down


def setup_inputs(seed: int = 0) -> dict:
    key = jax.random.key(seed)
    ks = iter(jax.random.split(key, 32))
    D = D_MODEL

    def nrm(shape, fan_in, mult=1.0):
        return jax.random.normal(next(ks), shape, jnp.float32) * (mult * fan_in ** -0.5)

    def gain(shape):
        return 1.0 + 0.02 * jax.random.normal(next(ks), shape, jnp.float32)

    x = jax.random.normal(next(ks), (BATCH, SEQ, D), jnp.float32)
    c = jax.random.normal(next(ks), (BATCH, D), jnp.float32)
    offset = jax.random.randint(next(ks), (BATCH,), 0, 4096, dtype=jnp.int32)
    positions = offset[:, None] + jnp.arange(SEQ, dtype=jnp.int32)[None, :]

    mod_w = nrm((DEPTH, D, 6 * D), D, 0.5)
    mod_b = 0.02 * jax.random.normal(next(ks), (DEPTH, 6 * D), jnp.float32)

    NA = N_MLA_LAYERS
    mla_w_in = nrm((NA, D, MLA_Q_LORA + MLA_KV_LORA + MLA_ROPE), D)
    mla_q_norm = gain((NA, MLA_Q_LORA))
    mla_w_q_up = nrm((NA, MLA_Q_LORA, MLA_HEADS * (MLA_NOPE + MLA_ROPE)), MLA_Q_LORA)
    mla_kv_norm = gain((NA, MLA_KV_LORA))
    mla_w_kv_up = nrm((NA, MLA_KV_LORA, MLA_HEADS * (MLA_NOPE + MLA_V)), MLA_KV_LORA)
    mla_w_out = nrm((NA, MLA_HEADS * MLA_V, D), MLA_HEADS * MLA_V)

    NB = N_ML_LAYERS
    ml_w_in = nrm((NB, D, 2 * ML_HEADS * ML_DK + 2 * ML_HEADS * ML_DV + 2 * ML_HEADS), D)
    i_bias = 0.1 * jax.random.normal(next(ks), (NB, ML_HEADS), jnp.float32)
    f_bias = (jnp.linspace(3.0, 6.0, ML_HEADS, dtype=jnp.float32)[None, :]
              + 0.1 * jax.random.normal(next(ks), (NB, ML_HEADS), jnp.float32))
    ml_b_gates = jnp.concatenate([i_bias, f_bias], axis=-1)
    ml_head_norm = gain((NB, ML_HEADS * ML_DV))
    ml_w_out = nrm((NB, ML_HEADS * ML_DV, D), ML_HEADS * ML_DV)

    ffn_w_up = nrm((DEPTH, D, 2 * D_FF), D)
    ffn_conv_w = nrm((DEPTH, CONV_W, D_FF), CONV_W)
    ffn_conv_b = 0.02 * jax.random.normal(next(ks), (DEPTH, D_FF), jnp.float32)
    ffn_w_down = nrm((DEPTH, D_FF, D), D_FF)

    final_norm = gain((D,))

    return {"x": x, "c": c, "positions": positions,
            "mod_w": mod_w, "mod_b": mod_b,
            "mla_w_in": mla_w_in, "mla_q_norm": mla_q_norm, "mla_w_q_up": mla_w_q_up,
            "mla_kv_norm": mla_kv_norm, "mla_w_kv_up": mla_w_kv_up, "mla_w_out": mla_w_out,
            "ml_w_in": ml_w_in, "ml_b_gates": ml_b_gates, "ml_head_norm": ml_head_norm,
            "ml_w_out": ml_w_out,
            "ffn_w_up": ffn_w_up, "ffn_conv_w": ffn_conv_w, "ffn_conv_b": ffn_conv_b,
            "ffn_w_down": ffn_w_down, "final_norm": final_norm}


def reference(x, c, positions, mod_w, mod_b,
              mla_w_in, mla_q_norm, mla_w_q_up, mla_kv_norm, mla_w_kv_up, mla_w_out,
              ml_w_in, ml_b_gates, ml_head_norm, ml_w_out,
              ffn_w_up, ffn_conv_w, ffn_conv_b, ffn_w_down, final_norm):
    cos, sin = rope_tables(positions)
    c_act = jax.nn.silu(c)
    for i in range(DEPTH):
        mod = c_act @ mod_w[i] + mod_b[i]
        sh_a, sc_a, g_a, sh_f, sc_f, g_f = [t[:, None, :] for t in jnp.split(mod, 6, axis=-1)]
        h = rms_norm(x) * (1 + sc_a) + sh_a
        j = i // N_MIXERS
        if i % N_MIXERS == 0:
            y = mla_mixer(h, cos, sin, mla_w_in[j], mla_q_norm[j], mla_w_q_up[j],
                          mla_kv_norm[j], mla_w_kv_up[j], mla_w_out[j])
        else:
            y = mlstm_mixer(h, ml_w_in[j], ml_b_gates[j], ml_head_norm[j], ml_w_out[j])
        x = x + g_a * y
        h = rms_norm(x) * (1 + sc_f) + sh_f
        x = x + g_f * conv_ffn(h, ffn_w_up[i], ffn_conv_w[i], ffn_conv_b[i], ffn_w_down[i])
    return rms_norm(x, final_norm)
```

```python
import numpy as np
import ml_dtypes
from contextlib import ExitStack
import concourse.bass as bass
import concourse.mybir as mybir
from concourse.bass_utils import run_bass_kernel_spmd

F32 = mybir.dt.float32
BF16 = mybir.dt.bfloat16
F32R = mybir.dt.float32r
I32 = mybir.dt.int32
AF = mybir.ActivationFunctionType
ALU = mybir.AluOpType
AX = mybir.AxisListType

D = 1024
S = 2048
DEPTH = 4
DFF = 2816
NFC = 22
FF_SPLIT = (11, 11)
EPS = 1e-6
TB = 512
NTB = S // TB

PP_MODB = 0
PP_QN = 192
PP_KVN = 200
PP_CW = 204
PP_CB = 468
PP_FN = 556
PP_GB = 564
PP_IF = 568
NPP = 576
CF_ID = 0
CF_ONE = 128
CF_TRI = 256
CF_SEL = 384
CF_SELA = 896
CF_SELB = 1408
NCF = 1920
CB_ID = 0
CB_ONE = 128
CB_MROW = 256
NCB = 384


class Res:
    __slots__ = ("name", "writers", "readers")

    def __init__(self, name):
        self.name = name
        self.writers = []
        self.readers = []


class Prog:
    ENG = ("pe", "act", "dve", "pool", "sp")

    def __init__(self, nc, stack):
        self.nc = nc
        self.stack = stack
        self.q = {e: [] for e in self.ENG}
        self.n = {e: 0 for e in self.ENG}
        self.seen = {e: {} for e in self.ENG}
        self.needed = {e: set() for e in self.ENG}
        self.csem = {e: stack.enter_context(nc.semaphore("cs_" + e)) for e in self.ENG}
        self.dsem = {}
        self.dtot = {}

    def _prune(self, eng, waits, raw_set):
        best = {}
        for ev in waits:
            if ev[0] == "c":
                if ev[1] == eng and id(ev) not in raw_set:
                    continue
                key = ("c", ev[1])
            else:
                key = ("d", ev[1])
            if ev[2] > best.get(key, 0):
                best[key] = ev[2]
        out = []
        seen = self.seen[eng]
        for key, val in best.items():
            if val <= seen.get(key, 0):
                continue
            seen[key] = val
            out.append((key[0], key[1], val))
            if key[0] == "c":
                self.needed[key[1]].add(val)
        return out

    @staticmethod
    def _deps(reads, writes, cowrites):
        waits = []
        raw = set()
        for r in reads:
            for ev in r.writers:
                waits.append(ev)
                raw.add(id(ev))
        for r in writes:
            waits += r.writers
            waits += r.readers
        for r in cowrites:
            waits += r.readers
        return waits, raw

    @staticmethod
    def _update(ev, reads, writes, cowrites):
        for r in reads:
            r.readers.append(ev)
        for r in writes:
            r.writers = [ev]
            r.readers = []
        for r in cowrites:
            r.writers.append(ev)

    def op(self, eng, fn, reads=(), writes=(), cowrites=()):
        waits, raw = self._deps(reads, writes, cowrites)
        w = self._prune(eng, waits, raw)
        self.n[eng] += 1
        ev = ("c", eng, self.n[eng])
        self.q[eng].append((fn, w, ev))
        self._update(ev, reads, writes, cowrites)
        return ev

    def dma(self, eng, sem, out, in_, reads=(), writes=(), cowrites=(), **kw):
        if sem not in self.dsem:
            self.dsem[sem] = self.stack.enter_context(self.nc.semaphore("ds_" + sem))
            self.dtot[sem] = 0
        waits, raw = self._deps(reads, writes, cowrites)
        w = self._prune(eng, waits, raw)
        self.dtot[sem] += 16
        ev = ("d", sem, self.dtot[sem])
        self.q[eng].append((lambda e: e.dma_start(out=out, in_=in_, **kw), w, ev))
        self._update(ev, reads, writes, cowrites)
        return ev

    def barrier(self, clear=()):
        evs = []
        for e in self.ENG:
            if self.n[e] > 0:
                evs.append(("c", e, self.n[e]))
        for s, t in self.dtot.items():
            if t > 0:
                evs.append(("d", s, t))
        for e in self.ENG:
            w = self._prune(e, [ev for ev in evs if not (ev[0] == "c" and ev[1] == e)], set())
            if w:
                self.q[e].append((None, w, None))
        for r in clear:
            r.writers = []
            r.readers = []

    def finish(self, final_events):
        w = self._prune("sp", list(final_events), set(id(e) for e in final_events))
        self.q["sp"].append((None, w, None))
        nc = self.nc
        rank = {}
        for e in self.ENG:
            rank[e] = {idx: i + 1 for i, idx in enumerate(sorted(self.needed[e]))}
        self.stats = {e: len(self.q[e]) for e in self.ENG}
        self.stats["incs"] = {e: len(rank[e]) for e in self.ENG}

        def replay(ename, eobj):
            for fn, waits, ev in self.q[ename]:
                for (kind, key, val) in waits:
                    if kind == "c":
                        eobj.wait_ge(self.csem[key], rank[key][val])
                    else:
                        eobj.wait_ge(self.dsem[key], val)
                if fn is None:
                    continue
                ins = fn(eobj)
                if ev is None:
                    continue
                if ev[0] == "c":
                    if ev[2] in rank[ename]:
                        ins.then_inc(self.csem[ename], 1)
                else:
                    ins.then_inc(self.dsem[ev[1]], 16)

        with nc.Block() as block:
            @block.tensor
            def _(t):
                replay("pe", t)

            @block.scalar
            def _(t):
                replay("act", t)

            @block.vector
            def _(t):
                replay("dve", t)

            @block.gpsimd
            def _(t):
                replay("pool", t)

            @block.sync
            def _(t):
                replay("sp", t)


class Builder:
    def __init__(self, n_seq=2, stages=None, dbg=False):
        self.n_seq = n_seq
        self.stages = stages
        self.dbg = dbg
        self.nc = bass.Bass("TRN2", target_bir_lowering=False)
        self.stack = ExitStack()

    def dram_in(self, name, shape, dt):
        return self.nc.dram_tensor(name, list(shape), dt, kind="ExternalInput").ap()

    def dump(self, name, ap, shape, dt):
        if not self.dbg:
            return
        d = self.nc.dram_tensor("dbg_" + name, list(shape), dt, kind="ExternalOutput").ap()
        self.P.barrier()
        self.P.dma("sp", "dbg", d, ap)
        self.P.barrier()

    def sb(self, name, shape, dt):
        return self.stack.enter_context(self.nc.sbuf_tensor(name, list(shape), dt))

    def build(self):
        nc = self.nc
        with self.stack:
            self.P = Prog(nc, self.stack)
            self._declare()
            self._program()
        return nc

    def _declare(self):
        nc = self.nc
        ns = self.n_seq
        self.d_xT = self.dram_in("xT", [ns, 128, 8, S], F32)
        self.d_cT = self.dram_in("cT", [128, 8, ns], F32)
        self.d_pos = self.dram_in("pos", [ns, S], I32)
        self.d_modw = self.dram_in("modw", [DEPTH, 12, 128, 8, 512], F32)
        self.d_pp = self.dram_in("pp", [128, NPP], F32)
        self.d_cf = self.dram_in("cf", [128, NCF], F32)
        self.d_cb = self.dram_in("cb", [128, NCB], BF16)
        self.d_hn = self.dram_in("hn", [128, 2, 1024], F32)
        self.d_fup = self.dram_in("fup", [DEPTH, NFC, 128, 8, 256], F32)
        self.d_fdn = self.dram_in("fdn", [DEPTH, 2, 8, 128, 11, 128], F32)
        self.d_win = self.dram_in("mwin", [2, 128, 8 * 896], F32)
        self.d_wq = self.dram_in("mwq", [2, 8, 128, 4 * 256], F32)
        self.d_wkv = self.dram_in("mwkv", [2, 8, 128, 2 * 256], F32)
        self.d_wo = self.dram_in("mwo", [2, 128, 8 * 1024], F32)
        self.d_mlw = self.dram_in("mlw", [2, 4, 128, 8 * 896], F32)
        self.d_mlg = self.dram_in("mlg", [2, 2, 128, 8 * 128], F32)
        self.d_mlo = self.dram_in("mlo", [2, 128, 8 * 1024], F32)
        self.d_out = nc.dram_tensor("outT", [ns, 128, 8, S], F32, kind="ExternalOutput").ap()

        self.xT = self.sb("xT_sb", [128, 8, S], F32)
        self.hT = self.sb("hT_sb", [128, 8, S], BF16)
        self.pp = self.sb("pp_sb", [128, NPP], F32)
        self.cf = self.sb("cf_sb", [128, NCF], F32)
        self.cb = self.sb("cb_sb", [128, NCB], BF16)
        self.modT = self.sb("modT", [128, DEPTH, 48, 2], F32)
        self.cact = self.sb("cact", [128, 8, 2], F32)
        self.sqr = self.sb("sqr", [128, 2, 512], BF16)
        self.tab = self.sb("ropetab", [128, S], BF16)
        self.r_tab = Res("tab")
        self.WB = self.sb("WB", [128, 12288], BF16)
        self.AR = self.sb("AR", [128, 17792], F32)
        self.ps = [self.stack.enter_context(nc.psum_tensor("ps%d" % i, [128, 512], F32)) for i in range(8)]
        self.r_ps = [Res("ps%d" % i) for i in range(8)]
        self.r_x = [[Res("x%d_%d" % (k, t)) for t in range(NTB)] for k in range(8)]
        self.r_h = [Res("h%d" % t) for t in range(NTB)]
        self.r_const = Res("const")
        self.r_mod = Res("mod")
        self.ident_f = self.cf[:, CF_ID:CF_ID + 128]
        self.ones_f = self.cf[:, CF_ONE:CF_ONE + 128]
        self.ident_b = self.cb[:, CB_ID:CB_ID + 128]
        self.ones_b = self.cb[:, CB_ONE:CB_ONE + 128]

    def ar_f32(self, off_kib, shape):
        n = int(np.prod(shape[1:]))
        o = int(off_kib * 256)
        ap = self.AR[0:shape[0], o:o + n]
        if len(shape) == 3:
            ap = ap.rearrange("p (a b) -> p a b", a=shape[1])
        return ap

    def ar_bf16(self, off_kib, shape):
        n = int(np.prod(shape[1:]))
        o = int(off_kib * 512)
        ap = self.AR.bitcast(BF16)[0:shape[0], o:o + n]
        if len(shape) == 3:
            ap = ap.rearrange("p (a b) -> p a b", a=shape[1])
        return ap

    def areset(self):
        self._acur = 0

    def af(self, shape):
        n = int(np.prod(shape[1:]))
        n = (n + 7) // 8 * 8
        o = self._acur
        self._acur += n
        assert self._acur <= 17792, self._acur
        ap = self.AR[0:shape[0], o:o + int(np.prod(shape[1:]))]
        if len(shape) == 3:
            ap = ap.rearrange("p (a b) -> p a b", a=shape[1])
        return ap

    def ab(self, shape):
        n = int(np.prod(shape[1:]))
        nf = (n + 15) // 16 * 8
        o = self._acur * 2
        self._acur += nf
        assert self._acur <= 17792, self._acur
        ap = self.AR.bitcast(BF16)[0:shape[0], o:o + n]
        if len(shape) == 3:
            ap = ap.rearrange("p (a b) -> p a b", a=shape[1])
        return ap

    def mod_ap(self, l, kind, i, s):
        return self.modT[:, l, kind * 8 + i, s:s + 1]

    def _program(self):
        P = self.P
        st = self.stages
        self.load_x(0)
        self.prologue()
        P.barrier()
        last = []
        for s in range(self.n_seq):
            if s > 0:
                self.load_x(s)
                P.barrier()
            for l in range(DEPTH):
                if st is None or ("mix%d" % l) in st:
                    self.norm_mod(s, l, 1, 0)
                    P.barrier()
                    if l % 2 == 0:
                        self.mla(s, l)
                    else:
                        self.mlstm(s, l)
                    P.barrier()
                if st is None or ("ffn%d" % l) in st:
                    self.norm_mod(s, l, 4, 3)
                    P.barrier()
                    self.ffn(s, l)
                    P.barrier()
            last += self.final_store(s)
            P.barrier()
        P.finish(last)

    def prologue(self):
        P = self.P
        rc = self.r_const
        P.dma("sp", "c0", self.pp[:, :], self.d_pp[:, :], cowrites=[rc])
        P.dma("sp", "c0", self.cf[:, :], self.d_cf[:, :], cowrites=[rc])
        P.dma("sp", "c0", self.cb[:, :], self.d_cb[:, :], cowrites=[rc])
        P.dma("sp", "c0", self.cact[:, :, :], self.d_cT[:, :, 0:2], cowrites=[rc])
        r_ca = Res("cact")
        cact2 = self.cact[:, :, :]
        P.op("act", lambda e: e.activation(out=cact2, in_=cact2, func=AF.Silu), reads=[rc], writes=[r_ca])
        NSTG = 4
        stg = [self.ar_bf16(8 * i, [128, 8, 512]) for i in range(NSTG)]
        r_stg = [Res("stg%d" % i) for i in range(NSTG)]
        cact_b = self.ar_bf16(40, [128, 8, 2])
        P.op("dve", lambda e: e.tensor_copy(out=cact_b, in_=self.cact[:, :, :]), reads=[r_ca], writes=[r_ca])
        it = 0
        for l in range(DEPTH):
            pst = self.ps[l % 2]
            r_p = self.r_ps[l % 2]
            for cbk in range(12):
                sl = it % NSTG
                it += 1
                for k2 in range(2):
                    kw = dict(writes=[r_stg[sl]]) if k2 == 0 else dict(cowrites=[r_stg[sl]])
                    P.dma("pool", "stg%d" % sl, stg[sl][:, k2 * 4:(k2 + 1) * 4, :], self.d_modw[l, cbk, :, k2 * 4:(k2 + 1) * 4, :], **kw)

                def mm(e, sl=sl, cbk=cbk, pst=pst):
                    ins = None
                    for j in range(4):
                        col = (cbk * 4 + j) * 2
                        for k in range(8):
                            ins = e.matmul(pst[:, col:col + 2], lhsT=stg[sl][:, k, j * 128:(j + 1) * 128],
                                           rhs=cact_b[:, k, 0:2], start=(k == 0), stop=(k == 7))
                    return ins
                if cbk == 0:
                    P.op("pe", mm, reads=[r_stg[sl], r_ca], writes=[r_p])
                else:
                    P.op("pe", mm, reads=[r_stg[sl], r_ca], cowrites=[r_p])
            mo = self.modT[:, l, :, :]
            pin = pst[:, 0:96].rearrange("p (a b) -> p a b", a=48)
            bb = self.pp[:, PP_MODB + l * 48:PP_MODB + (l + 1) * 48].unsqueeze(2).broadcast_to([128, 48, 2])
            P.op("dve", lambda e, mo=mo, pin=pin, bb=bb: e.tensor_tensor(out=mo, in0=pin, in1=bb, op=ALU.add),
                 reads=[r_p, rc], cowrites=[self.r_mod])
            for kind in (1, 4):
                m1 = self.modT[:, l, kind * 8:(kind + 1) * 8, :]
                P.op("dve", lambda e, m1=m1: e.tensor_scalar(out=m1, in0=m1, scalar1=1.0, scalar2=None, op0=ALU.add),
                     reads=[self.r_mod], cowrites=[self.r_mod])

    def load_x(self, s):
        P = self.P
        for k in range(8):
            P.dma("sp", "xld%d" % k, self.xT[:, k, :], self.d_xT[s, :, k, :],
                  writes=[self.r_x[k][t] for t in range(NTB)])

    def final_store(self, s):
        P = self.P
        evs = []
        self._norm_core(s, final=True)
        return self._final_evs

    def norm_mod(self, s, l, kind_sc, kind_sh):
        self._norm_core(s, l=l, kind_sc=kind_sc, kind_sh=kind_sh)

    def _norm_core(self, s, l=None, kind_sc=None, kind_sh=None, final=False):
        P = self.P
        sq = [self.sqr[:, 0, :], self.sqr[:, 1, :]]
        r_sq = [Res("sq0"), Res("sq1")]
        lnv = [self.ar_f32(4, [128, 512]), self.ar_f32(6, [128, 512])]
        r_ln = [Res("ln0"), Res("ln1")]
        rstd = [self.ar_f32(8, [128, 512]), self.ar_f32(10, [128, 512])]
        r_rs = [Res("rs0"), Res("rs1")]
        tmp = [self.ar_f32(12 + 2 * i, [128, 512]) for i in range(4)]
        r_tmp = [Res("tmp%d" % i) for i in range(4)]
        ones_r = self.ones_b
        self._final_evs = []
        it = 0
        for tb in range(NTB):
            tsl = slice(tb * TB, (tb + 1) * TB)
            pb = tb % 2
            pst = self.ps[pb]
            r_p = self.r_ps[pb]
            for k in range(8):
                b = it % 2
                it += 1
                xin = self.xT[:, k, tsl]
                P.op("act", lambda e, o=sq[b], i=xin: e.activation(out=o, in_=i, func=AF.Square),
                     reads=[self.r_x[k][tb]], writes=[r_sq[b]])
                kw = dict(reads=[r_sq[b], self.r_const])
                if k == 0:
                    kw["writes"] = [r_p]
                else:
                    kw["cowrites"] = [r_p]
                P.op("pe", lambda e, pst=pst, i=sq[b], k=k: e.matmul(pst[:, :], lhsT=ones_r, rhs=i,
                                                                  start=(k == 0), stop=(k == 7)), **kw)
            P.op("act", lambda e, o=lnv[pb], pst=pst: e.activation(out=o, in_=pst[:, :], func=AF.Ln, scale=1.0 / D, bias=EPS),
                 reads=[r_p], writes=[r_ln[pb]])
            P.op("act", lambda e, o=rstd[pb], i=lnv[pb]: e.activation(out=o, in_=i, func=AF.Exp, scale=-0.5),
                 reads=[r_ln[pb]], writes=[r_rs[pb]])
            for k in range(8):
                tbuf = (tb * 8 + k) % 4
                xin = self.xT[:, k, tsl]
                if not final:
                    sc = self.mod_ap(l, kind_sc, k, s)
                    sh = self.mod_ap(l, kind_sh, k, s)
                    P.op("dve", lambda e, o=tmp[tbuf], x=xin, r=rstd[pb], sc=sc: e.scalar_tensor_tensor(
                        out=o, in0=x, scalar=sc, in1=r, op0=ALU.mult, op1=ALU.mult),
                        reads=[self.r_x[k][tb], r_rs[pb], self.r_mod], writes=[r_tmp[tbuf]])
                    ho = self.hT[:, k, tsl]
                    kw = dict(reads=[r_tmp[tbuf], self.r_mod])
                    if k == 0:
                        kw["writes"] = [self.r_h[tb]]
                    else:
                        kw["cowrites"] = [self.r_h[tb]]
                    if k % 2 == 0:
                        P.op("dve", lambda e, o=ho, i=tmp[tbuf], sh=sh: e.tensor_scalar(
                            out=o, in0=i, scalar1=sh, scalar2=None, op0=ALU.add), **kw)
                    else:
                        P.op("act", lambda e, o=ho, i=tmp[tbuf], sh=sh: e.activation(
                            out=o, in_=i, func=AF.Identity, bias=sh), **kw)
                else:
                    g = self.pp[:, PP_FN + k:PP_FN + k + 1]
                    P.op("dve", lambda e, o=tmp[tbuf], x=xin, r=rstd[pb], g=g: e.scalar_tensor_tensor(
                        out=o, in0=x, scalar=g, in1=r, op0=ALU.mult, op1=ALU.mult),
                        reads=[self.r_x[k][tb], r_rs[pb], self.r_const], writes=[r_tmp[tbuf]])
                    ev = P.dma("sp", "ost%d" % tbuf, self.d_out[s, :, k, tsl], tmp[tbuf], reads=[r_tmp[tbuf]])
                    self._final_evs.append(ev)

    def ffn(self, s, l):
        P = self.P
        u = self.ar_bf16(0, [128, 11, S])
        r_u = [[Res("u%d_%d" % (c, t)) for t in range(NTB)] for c in range(11)]
        AF_W = 2 + S + 2
        a_full = [self.ar_f32(44, [128, AF_W]), self.ar_f32(44 + 8.25, [128, AF_W])]
        r_a = [[Res("a%d_%d" % (b, t)) for t in range(NTB)] for b in range(2)]
        r_az = Res("az")
        tt = [self.ar_f32(61, [128, 512]), self.ar_f32(63, [128, 512])]
        r_t = [Res("t0"), Res("t1")]
        ge = [self.ar_f32(65, [128, 512]), self.ar_f32(67, [128, 512])]
        r_g = [Res("g0"), Res("g1")]
        for b in range(2):
            P.op("dve", lambda e, o=a_full[b][:, 0:2]: e.memset(o, 0.0), cowrites=[r_az])
        NWS = 6
        wsl = [self.WB[:, i * 2048:(i + 1) * 2048] for i in range(NWS)]
        r_w = [Res("w%d" % i) for i in range(NWS)]
        wi = 0
        cwb = PP_CW + l * 66
        cbb = PP_CB + l * 22
        it = 0
        c_glob = 0
        for half, nch in enumerate(FF_SPLIT):
            for cl in range(nch):
                c = c_glob + cl
                sl = wi % NWS
                wi += 1
                wv = wsl[sl].rearrange("p (k n) -> p k n", k=8)
                P.dma("pool", "w%d" % sl, wsl[sl], self.d_fup[l, c].rearrange("p k n -> p (k n)"), writes=[r_w[sl]])
                ab = c % 2
                w0 = self.pp[:, cwb + 0 * 22 + c:cwb + 0 * 22 + c + 1]
                w1 = self.pp[:, cwb + 1 * 22 + c:cwb + 1 * 22 + c + 1]
                w2 = self.pp[:, cwb + 2 * 22 + c:cwb + 2 * 22 + c + 1]
                bia = self.pp[:, cbb + c:cbb + c + 1]
                for tb in range(NTB):
                    tsl = slice(tb * TB, (tb + 1) * TB)
                    pa = self.ps[(it % 2) * 2]
                    pg = self.ps[(it % 2) * 2 + 1]
                    r_pa = self.r_ps[(it % 2) * 2]
                    r_pg = self.r_ps[(it % 2) * 2 + 1]
                    tbuf = it % 2
                    it += 1

                    def mm_a(e, wv=wv, pa=pa, tsl=tsl):
                        ins = None
                        for k in range(8):
                            ins = e.matmul(pa[:, :], lhsT=wv[:, k, 0:128], rhs=self.hT[:, k, tsl], start=(k == 0), stop=(k == 7))
                        return ins

                    def mm_g(e, wv=wv, pg=pg, tsl=tsl):
                        ins = None
                        for k in range(8):
                            ins = e.matmul(pg[:, :], lhsT=wv[:, k, 128:256], rhs=self.hT[:, k, tsl], start=(k == 0), stop=(k == 7))
                        return ins
                    P.op("pe", mm_a, reads=[r_w[sl], self.r_h[tb]], writes=[r_pa])
                    P.op("pe", mm_g, reads=[r_w[sl], self.r_h[tb]], writes=[r_pg])
                    af = a_full[ab]
                    o0 = 2 + tb * TB
                    P.op("act", lambda e, o=af[:, o0:o0 + TB], pa=pa: e.activation(out=o, in_=pa[:, :], func=AF.Copy),
                         reads=[r_pa], writes=[r_a[ab][tb]])
                    P.op("act", lambda e, o=tt[tbuf], pa=pa, w2=w2, bia=bia: e.activation(
                        out=o, in_=pa[:, :], func=AF.Identity, scale=w2, bias=bia),
                        reads=[r_pa, self.r_const], writes=[r_t[tbuf]])
                    rd = [r_a[ab][tb], r_az, self.r_const, r_t[tbuf]]
                    if tb > 0:
                        rd.append(r_a[ab][tb - 1])
                    P.op("dve", lambda e, o=tt[tbuf], a1=af[:, o0 - 1:o0 - 1 + TB], w1=w1: e.scalar_tensor_tensor(
                        out=o, in0=a1, scalar=w1, in1=o, op0=ALU.mult, op1=ALU.add), reads=rd, writes=[r_t[tbuf]])
                    P.op("dve", lambda e, o=tt[tbuf], a0=af[:, o0 - 2:o0 - 2 + TB], w0=w0: e.scalar_tensor_tensor(
                        out=o, in0=a0, scalar=w0, in1=o, op0=ALU.mult, op1=ALU.add), reads=rd, writes=[r_t[tbuf]])
                    P.op("act", lambda e, o=ge[tbuf], i=tt[tbuf]: e.activation(out=o, in_=i, func=AF.Gelu),
                         reads=[r_t[tbuf]], writes=[r_g[tbuf]])
                    P.op("dve", lambda e, o=u[:, cl, tsl], g=ge[tbuf], pg=pg: e.tensor_tensor(out=o, in0=g, in1=pg[:, :], op=ALU.mult),
                         reads=[r_g[tbuf], r_pg], writes=[r_u[cl][tb]])
            for d in range(8):
                sl = wi % NWS
                wi += 1
                wv = wsl[sl][:, 0:nch * 128].rearrange("p (c n) -> p c n", c=nch)
                P.dma("pool", "w%d" % sl, wsl[sl][:, 0:nch * 128], self.d_fdn[l, half, d].rearrange("p c n -> p (c n)"),
                      writes=[r_w[sl]])
                gf = self.mod_ap(l, 5, d, s)
                for tb in range(NTB):
                    tsl = slice(tb * TB, (tb + 1) * TB)
                    po = self.ps[4 + it % 2]
                    r_po = self.r_ps[4 + it % 2]
                    it += 1

                    def mm_d(e, wv=wv, po=po, tsl=tsl, nch=nch):
                        ins = None
                        for cc in range(nch):
                            ins = e.matmul(po[:, :], lhsT=wv[:, cc, :], rhs=u[:, cc, tsl], start=(cc == 0), stop=(cc == nch - 1))
                        return ins
                    P.op("pe", mm_d, reads=[r_w[sl]] + [r_u[cc][tb] for cc in range(nch)], writes=[r_po])
                    xo = self.xT[:, d, tsl]
                    P.op("dve", lambda e, xo=xo, po=po, gf=gf: e.scalar_tensor_tensor(
                        out=xo, in0=po[:, :], scalar=gf, in1=xo, op0=ALU.mult, op1=ALU.add),
                        reads=[r_po, self.r_mod], writes=[self.r_x[d][tb]])
            c_glob += nch


    def mla(self, s, l):
        P = self.P
        a = l // 2
        SCALE = 192.0 ** -0.5
        tab = self.tab[:, :]
        cosT = tab[0:64, :]
        sinT = tab[64:128, :]
        cqn = self.ar_bf16(8, [128, 4, S])
        ckvn = self.ar_bf16(24, [128, 2, S])
        krT = self.ar_bf16(32, [128, S])
        knT2 = [self.ar_bf16(36, [128, S]), self.ar_bf16(40, [128, S])]
        Vh2 = [self.ar_bf16(44, [128, 16, 128]), self.ar_bf16(48, [128, 16, 128])]
        qnT2 = [self.ar_bf16(52, [128, 512]), self.ar_bf16(53, [128, 512])]
        qrT2 = [self.ar_bf16(54, [128, 512]), self.ar_bf16(55, [128, 512])]
        PT = [self.ar_bf16(56, [128, 512]), self.ar_bf16(57, [128, 512]), self.ar_bf16(4, [128, 512])]
        scr = [self.ar_f32(58 + 2 * i, [128, 512]) for i in range(4)]
        rinv = self.ar_f32(66, [128, 512])
        sqb = [self.ar_bf16(0, [128, 512]), self.ar_bf16(1, [128, 512])]
        small = self.ar_f32(5, [128, 64])
        r_cos = self.r_tab; r_sin = self.r_tab
        r_cqn = [Res("cqn%d" % t) for t in range(NTB)]
        r_ckvn = [Res("ckvn%d" % t) for t in range(NTB)]
        r_kr = [Res("kr%d" % t) for t in range(NTB)]
        r_scr = [Res("scr%d" % i) for i in range(4)]
        r_sq = [Res("sqA"), Res("sqB")]
        r_small = Res("small")
        rc = self.r_const
        w_in = self.WB[:, 0:8 * 896].rearrange("p (k n) -> p k n", k=8)
        r_wbig = Res("wbig")
        for k2 in range(4):
            P.dma("pool", "wbig", self.WB[:, k2 * 1792:(k2 + 1) * 1792], self.d_win[a, :, k2 * 1792:(k2 + 1) * 1792],
                  cowrites=[r_wbig])
        if l == 0:
            pos_i = self.ar_f32(36, [64, S]).bitcast(I32)
            tf = self.ar_f32(44, [64, S])
            tg = self.ar_f32(52, [64, S])
            r_pi = Res("posi"); r_tf = Res("tf"); r_tg = Res("tg")
            P.dma("sp", "pos", pos_i, self.d_pos[s:s + 1, :].broadcast_to([64, S]), writes=[r_pi])
            invf = self.pp[0:64, PP_IF:PP_IF + 1]
            TWO_PI = 2.0 * np.pi
            c1 = 6.28125
            rem = TWO_PI - c1
            c2 = float(np.frombuffer(np.array([np.frombuffer(np.float32(rem).tobytes(), np.uint32)[0] & 0xFFFFF000], np.uint32).tobytes(), np.float32)[0])
            c3 = float(np.float32(rem - c2))
            P.op("dve", lambda e: e.tensor_copy(out=tf, in_=pos_i), reads=[r_pi], writes=[r_tf])
            P.op("dve", lambda e: e.tensor_scalar(out=tf, in0=tf, scalar1=invf, scalar2=None, op0=ALU.mult),
                 reads=[r_tf, rc], writes=[r_tf])
            P.op("dve", lambda e: e.tensor_scalar(out=tg, in0=tf, scalar1=float(1.0 / TWO_PI), scalar2=None, op0=ALU.mult),
                 reads=[r_tf], writes=[r_tg])
            P.op("dve", lambda e: e.tensor_copy(out=pos_i, in_=tg), reads=[r_tg], writes=[r_pi])
            P.op("dve", lambda e: e.tensor_copy(out=tg, in_=pos_i), reads=[r_pi], writes=[r_tg])
            for cc in (c1, c2, c3):
                P.op("dve", lambda e, cc=cc: e.scalar_tensor_tensor(out=tf, in0=tg, scalar=-float(cc), in1=tf, op0=ALU.mult, op1=ALU.add),
                     reads=[r_tf, r_tg], writes=[r_tf])
            PI = float(np.pi)
            th = self.ar_f32(60, [64, S])
            r_th = Res("th")
            for (shift, dstT, r_d) in ((0.0, sinT, r_sin), (PI / 2, cosT, r_cos)):
                P.op("dve", lambda e, shift=shift: e.tensor_scalar(out=tg, in0=tf, scalar1=float(shift), scalar2=None, op0=ALU.add),
                     reads=[r_tf], writes=[r_tg])
                P.op("dve", lambda e: e.tensor_scalar(out=th, in0=tg, scalar1=-PI, scalar2=TWO_PI, op0=ALU.is_lt, op1=ALU.mult),
                     reads=[r_tg], writes=[r_th])
                P.op("dve", lambda e: e.tensor_tensor(out=tg, in0=tg, in1=th, op=ALU.add), reads=[r_tg, r_th], writes=[r_tg])
                P.op("dve", lambda e: e.tensor_scalar(out=th, in0=tg, scalar1=PI, scalar2=-TWO_PI, op0=ALU.is_gt, op1=ALU.mult),
                     reads=[r_tg], writes=[r_th])
                P.op("dve", lambda e: e.tensor_tensor(out=tg, in0=tg, in1=th, op=ALU.add), reads=[r_tg, r_th], writes=[r_tg])
                P.op("act", lambda e, dstT=dstT: e.activation(out=dstT, in_=tg, func=AF.Sin), reads=[r_tg], writes=[r_d])
            P.op("dve", lambda e: e.tensor_scalar(out=tab[64:96, :], in0=tab[64:96, :], scalar1=-1.0, scalar2=None, op0=ALU.mult),
                 reads=[r_sin], writes=[r_sin])
        P.barrier()
        r_zero = Res("zero")
        P.op("dve", lambda e: e.memset(krT[64:128, :], 0.0), cowrites=[r_zero])
        for qq in range(2):
            P.op("dve", lambda e, qq=qq: e.memset(qrT2[qq][64:128, :], 0.0), cowrites=[r_zero])

        ones_r = self.ones_b
        sqr = [self.sqr[:, 0, :], self.sqr[:, 1, :]]
        it = 0
        sqi = 0

        rope_i = [0]

        def rope(dst, tsl, nm):
            b2 = (rope_i[0] % 2) * 2
            rope_i[0] += 1
            s0 = scr[b2][0:64, :]
            s1 = scr[b2 + 1][0:64, :]
            P.op("dve", lambda e: e.tensor_tensor(out=s0, in0=self.ps[6][0:64, :], in1=cosT[:, tsl], op=ALU.mult),
                 reads=[self.r_ps[6], r_cos], writes=[r_scr[b2]])
            P.op("dve", lambda e: e.tensor_tensor(out=s1, in0=self.ps[6][64:128, :], in1=sinT[:, tsl], op=ALU.mult),
                 reads=[self.r_ps[6], r_sin], writes=[r_scr[b2 + 1]])
            P.op("dve", lambda e: e.tensor_tensor(out=dst, in0=s0, in1=s1, op=ALU.add),
                 reads=[r_scr[b2], r_scr[b2 + 1], r_zero], writes=[nm])

        for tb in range(NTB):
            tsl = slice(tb * TB, (tb + 1) * TB)
            for (nch, col0, dst, r_dst, gcol) in ((4, 0, cqn, r_cqn, PP_QN + a * 4), (2, 512, ckvn, r_ckvn, PP_KVN + a * 2)):
                pss = self.ps[4 + it % 2]
                r_pss = self.r_ps[4 + it % 2]
                it += 1
                for j in range(nch):
                    pj = self.ps[j]
                    r_pj = self.r_ps[j]

                    def mm(e, pj=pj, c0=col0 + j * 128, tsl=tsl):
                        ins = None
                        for k in range(8):
                            ins = e.matmul(pj[:, :], lhsT=w_in[:, k, c0:c0 + 128], rhs=self.hT[:, k, tsl], start=(k == 0), stop=(k == 7))
                        return ins
                    P.op("pe", mm, reads=[r_wbig, self.r_h[tb]], writes=[r_pj])
                    P.op("act", lambda e, o=scr[j], pj=pj: e.activation(out=o, in_=pj[:, :], func=AF.Copy),
                         reads=[r_pj], writes=[r_scr[j]])
                    b = sqi % 2
                    sqi += 1
                    P.op("act", lambda e, o=sqr[b], i=scr[j]: e.activation(out=o, in_=i, func=AF.Square),
                         reads=[r_scr[j]], writes=[r_sq[b]])
                    kw = dict(reads=[r_sq[b], rc])
                    if j == 0:
                        kw["writes"] = [r_pss]
                    else:
                        kw["cowrites"] = [r_pss]
                    P.op("pe", lambda e, pss=pss, i=sqr[b], j=j, nch=nch: e.matmul(pss[:, :], lhsT=ones_r, rhs=i, start=(j == 0), stop=(j == nch - 1)), **kw)
                P.op("act", lambda e, pss=pss, nch=nch: e.activation(out=rinv, in_=pss[:, :], func=AF.Ln, scale=1.0 / (nch * 128), bias=EPS),
                     reads=[r_pss], writes=[r_small])
                P.op("act", lambda e: e.activation(out=rinv, in_=rinv, func=AF.Exp, scale=-0.5), reads=[r_small], writes=[r_small])
                for j in range(nch):
                    g = self.pp[:, gcol + j:gcol + j + 1]
                    kw = dict(reads=[r_scr[j], r_small, rc])
                    if j == 0:
                        kw["writes"] = [r_dst[tb]]
                    else:
                        kw["cowrites"] = [r_dst[tb]]
                    P.op("dve", lambda e, o=dst[:, j, tsl], i=scr[j], g=g: e.scalar_tensor_tensor(
                        out=o, in0=i, scalar=g, in1=rinv, op0=ALU.mult, op1=ALU.mult), **kw)
            def mmk(e, tsl=tsl):
                ins = None
                for k in range(8):
                    ins = e.matmul(self.ps[6][:, :], lhsT=w_in[:, k, 768:896], rhs=self.hT[:, k, tsl], start=(k == 0), stop=(k == 7))
                return ins
            P.op("pe", mmk, reads=[r_wbig, self.r_h[tb]], writes=[self.r_ps[6]])
            rope(krT[0:64, tsl], tsl, r_kr[tb])
        P.barrier()
        if s == 0 and l == 0:
            self.dump("cos", cosT, [64, S], BF16)
            self.dump("sin", sinT, [64, S], BF16)
            self.dump("cqn", cqn, [128, 4, S], BF16)
            self.dump("ckvn", ckvn, [128, 2, S], BF16)
            self.dump("krT", krT[0:64, :], [64, S], BF16)

        wo = self.WB[:, 0:8192].rearrange("p (h n) -> p h n", h=8)
        r_wo = Res("wo")
        for h2 in range(4):
            P.dma("pool", "wbig", self.WB[:, h2 * 2048:(h2 + 1) * 2048], self.d_wo[a, :, h2 * 2048:(h2 + 1) * 2048], cowrites=[r_wo])

        r_qn = [Res("qn0"), Res("qn1")]
        r_qr = [Res("qr0"), Res("qr1")]
        r_kn = [[Res("kn%d_%d" % (b, t)) for t in range(NTB)] for b in range(2)]
        r_v = [[Res("v%d_%d" % (b, t)) for t in range(NTB)] for b in range(2)]
        r_pt = [Res("pt0"), Res("pt1"), Res("pt2")]
        r_rinv = Res("rinv")
        r_o = [[Res("o%d_%d" % (h, t)) for t in range(NTB)] for h in range(8)]
        r_hw = [Res("hw0"), Res("hw1")]
        r_mk = Res("mk")
        r_mq = Res("mq")
        mrow128 = self.cb[:, CB_MROW:CB_MROW + 128]

        def hw_views(hb):
            wq = self.WB[:, 8192 + hb * 1536:8192 + hb * 1536 + 1024].rearrange("p (k n) -> p k n", k=4)
            wkv = self.WB[:, 8192 + hb * 1536 + 1024:8192 + hb * 1536 + 1536].rearrange("p (k n) -> p k n", k=2)
            return wq, wkv

        def load_hw(h):
            hb = h % 2
            P.dma("pool", "hw%d" % hb, self.WB[:, 8192 + hb * 1536:8192 + hb * 1536 + 1024], self.d_wq[a, h], writes=[r_hw[hb]])
            P.dma("pool", "hw%d" % hb, self.WB[:, 8192 + hb * 1536 + 1024:8192 + hb * 1536 + 1536], self.d_wkv[a, h], cowrites=[r_hw[hb]])

        def sumsq_max(srcA, rA, srcB, rB, outcol, r_out):
            P.op("dve", lambda e: e.tensor_tensor(out=sqb[0], in0=srcA, in1=srcA, op=ALU.mult), reads=[rA], writes=[r_sq[0]])
            P.op("dve", lambda e: e.tensor_tensor(out=sqb[1], in0=srcB, in1=srcB, op=ALU.mult), reads=[rB, r_zero], writes=[r_sq[1]])

            def mms(e):
                e.matmul(self.ps[7][:, :], lhsT=self.ones_b, rhs=sqb[0], start=True, stop=False)
                return e.matmul(self.ps[7][:, :], lhsT=self.ones_b, rhs=sqb[1], start=False, stop=True)
            P.op("pe", mms, reads=[r_sq[0], r_sq[1], rc], writes=[self.r_ps[7]])
            P.op("dve", lambda e: e.reduce_max(out=outcol, in_=self.ps[7][:, :], axis=AX.X), reads=[self.r_ps[7]], cowrites=[r_out])

        def kprep(h, tbs):
            hb = h % 2
            wq, wkv = hw_views(hb)
            knT = knT2[hb]
            Vh = Vh2[hb]
            for tb in tbs:
                tsl = slice(tb * TB, (tb + 1) * TB)

                def mmk2(e, tsl=tsl, wkv=wkv):
                    ins = None
                    for k in range(2):
                        ins = e.matmul(self.ps[6][:, :], lhsT=wkv[:, k, 0:128], rhs=ckvn[:, k, tsl], start=(k == 0), stop=(k == 1))
                    return ins
                P.op("pe", mmk2, reads=[r_hw[hb], r_ckvn[tb]], writes=[self.r_ps[6]])
                P.op("act", lambda e, o=knT[:, tsl]: e.activation(out=o, in_=self.ps[6][:, :], func=AF.Copy),
                     reads=[self.r_ps[6]], writes=[r_kn[hb][tb]])

                def mmv(e, tb=tb, wkv=wkv):
                    ins = None
                    for t4 in range(4):
                        t0 = tb * TB + t4 * 128
                        for k in range(2):
                            ins = e.matmul(self.ps[7][:, t4 * 128:(t4 + 1) * 128], lhsT=ckvn[:, k, t0:t0 + 128], rhs=wkv[:, k, 128:256],
                                           start=(k == 0), stop=(k == 1))
                    return ins
                P.op("pe", mmv, reads=[r_hw[hb], r_ckvn[tb]], writes=[self.r_ps[7]])
                P.op("dve", lambda e, o=Vh[:, tb * 4:(tb + 1) * 4, :]: e.tensor_copy(
                    out=o, in_=self.ps[7][:, :].rearrange("p (a b) -> p a b", a=4)),
                    reads=[self.r_ps[7]], writes=[r_v[hb][tb]])
                sumsq_max(knT[:, tsl], r_kn[hb][tb], krT[:, tsl], r_kr[tb], small[:, hb * 4 + tb:hb * 4 + tb + 1], r_mk)
            if tbs[-1] == NTB - 1:
                P.op("dve", lambda e: e.reduce_max(out=small[:, 8 + hb:9 + hb], in_=small[:, hb * 4:hb * 4 + 4], axis=AX.X),
                     reads=[r_mk], cowrites=[r_mk])

        qcnt = [0]

        def qprep(h, qb):
            hb = h % 2
            wq, wkv = hw_views(hb)
            qi = qcnt[0] % 2
            qcnt[0] += 1
            tsl = slice(qb * TB, (qb + 1) * TB)
            qnT = qnT2[qi]
            qrT = qrT2[qi]

            def mmq(e, wq=wq):
                ins = None
                for k in range(4):
                    ins = e.matmul(self.ps[6][:, :], lhsT=wq[:, k, 0:128], rhs=cqn[:, k, tsl], start=(k == 0), stop=(k == 3))
                return ins
            P.op("pe", mmq, reads=[r_hw[hb], r_cqn[qb]], writes=[self.r_ps[6]])
            P.op("act", lambda e: e.activation(out=qnT, in_=self.ps[6][:, :], func=AF.Copy), reads=[self.r_ps[6]], writes=[r_qn[qi]])

            def mmqr(e, wq=wq):
                ins = None
                for k in range(4):
                    ins = e.matmul(self.ps[6][:, :], lhsT=wq[:, k, 128:256], rhs=cqn[:, k, tsl], start=(k == 0), stop=(k == 3))
                return ins
            P.op("pe", mmqr, reads=[r_hw[hb], r_cqn[qb]], writes=[self.r_ps[6]])
            rope(qrT[0:64, :], tsl, r_qr[qi])
            mq = small[:, 10 + qi:11 + qi]
            sumsq_max(qnT, r_qn[qi], qrT, r_qr[qi], mq, r_mq)
            negc = small[:, 16 + h * 4 + qb:17 + h * 4 + qb]
            P.op("dve", lambda e: e.tensor_scalar(out=negc, in0=mq, scalar1=small[:, 8 + hb:9 + hb], scalar2=-0.5 * SCALE,
                                                  op0=ALU.add, op1=ALU.mult), reads=[r_mq, r_mk], cowrites=[r_mq])
            return qi, negc

        def blk_ctx(h, qb, qi, negc):
            hb = h % 2
            return dict(h=h, qb=qb, hb=hb, knT=knT2[hb], Vh=Vh2[hb], qnT=qnT2[qi], qrT=qrT2[qi], qi=qi, negc=negc,
                        po=self.ps[3 + qb % 2], r_po=self.r_ps[3 + qb % 2], psm=self.ps[5], r_psm=self.r_ps[5],
                        nj=4 * qb + 4)

        def step_bufs(k):
            return self.ps[k % 3], self.r_ps[k % 3], PT[k % 3], r_pt[k % 3]

        def emit_mms(k, cx, j):
            pst, r_pst, ptb, r_ptb = step_bufs(k)
            r = j - 4 * cx["qb"]
            c0 = 128 * max(r, 0)
            ksl = slice(j * 128, (j + 1) * 128)
            knT, qnT, qrT = cx["knT"], cx["qnT"], cx["qrT"]

            def mms(e):
                e.matmul(pst[:, c0:512], lhsT=knT[:, ksl], rhs=qnT[:, c0:512], start=True, stop=False)
                ins = e.matmul(pst[:, c0:512], lhsT=krT[:, ksl], rhs=qrT[:, c0:512], start=False, stop=(r < 0))
                if r >= 0:
                    ins = e.matmul(pst[:, c0:c0 + 64], lhsT=mrow128, rhs=self.ones_b[:, 0:64], start=False, stop=True)
                return ins
            tbk = j // 4
            P.op("pe", mms, reads=[r_kn[cx["hb"]][tbk], r_kr[tbk], r_qn[cx["qi"]], r_qr[cx["qi"]], rc, r_zero], writes=[r_pst])

        def emit_exp_mmo(k, cx, j):
            pst, r_pst, ptb, r_ptb = step_bufs(k)
            r = j - 4 * cx["qb"]
            c0 = 128 * max(r, 0)
            negc, Vh, po, psm, nj = cx["negc"], cx["Vh"], cx["po"], cx["psm"], cx["nj"]
            P.op("act", lambda e: e.activation(out=ptb[:, c0:512], in_=pst[:, c0:512], func=AF.Exp, scale=SCALE, bias=negc),
                 reads=[r_pst, r_mq], writes=[r_ptb])

            def mmo(e):
                e.matmul(po[:, c0:512], lhsT=Vh[:, j, :], rhs=ptb[:, c0:512], start=(j == 0), stop=(j == nj - 1))
                return e.matmul(psm[:, c0:512], lhsT=self.ones_b, rhs=ptb[:, c0:512], start=(j == 0), stop=(j == nj - 1))
            tbk = j // 4
            kw = dict(reads=[r_v[cx["hb"]][tbk], r_ptb, rc])
            if j == 0:
                kw["writes"] = [cx["r_po"], cx["r_psm"]]
            else:
                kw["cowrites"] = [cx["r_po"], cx["r_psm"]]
            P.op("pe", mmo, **kw)
            if j == nj - 1:
                h, qb = cx["h"], cx["qb"]
                P.op("act", lambda e: e.activation(out=rinv, in_=psm[:, :], func=AF.Ln), reads=[cx["r_psm"]], writes=[r_rinv])
                P.op("act", lambda e: e.activation(out=rinv, in_=rinv, func=AF.Exp, scale=-1.0), reads=[r_rinv], writes=[r_rinv])
                P.op("dve", lambda e: e.tensor_tensor(out=self.hT[:, h, qb * TB:(qb + 1) * TB], in0=po[:, :], in1=rinv, op=ALU.mult),
                     reads=[cx["r_po"], r_rinv], writes=[r_o[h][qb]])

        load_hw(0)
        load_hw(1)
        kprep(0, [0, 1, 2, 3])
        ctxs = {}
        qi0, negc0 = qprep(0, 0)
        ctxs[(0, 0)] = blk_ctx(0, 0, qi0, negc0)
        steps = [(h, qb, j) for h in range(8) for qb in range(4) for j in range(4 * qb + 4)]
        LOOK = 2
        emit_mms(0, ctxs[(0, 0)], 0)
        emit_mms(1, ctxs[(0, 0)], 1)
        for k, (h, qb, j) in enumerate(steps):
            if j == 0:
                if qb < 3:
                    qi_, negc_ = qprep(h, qb + 1)
                    ctxs[(h, qb + 1)] = blk_ctx(h, qb + 1, qi_, negc_)
                if h < 7:
                    if qb == 1:
                        kprep(h + 1, [0, 1])
                    elif qb == 2:
                        kprep(h + 1, [2, 3])
                    elif qb == 3:
                        qi_, negc_ = qprep(h + 1, 0)
                        ctxs[(h + 1, 0)] = blk_ctx(h + 1, 0, qi_, negc_)
                if qb == 3 and h + 2 < 8:
                    load_hw(h + 2)
            if k + LOOK < len(steps):
                h2, qb2, j2 = steps[k + LOOK]
                emit_mms(k + LOOK, ctxs[(h2, qb2)], j2)
            emit_exp_mmo(k, ctxs[(h, qb)], j)
        it = 0
        for d in range(8):
            ga = self.mod_ap(l, 2, d, s)
            for tb in range(NTB):
                tsl = slice(tb * TB, (tb + 1) * TB)
                po = self.ps[6 + it % 2]
                r_po = self.r_ps[6 + it % 2]
                it += 1

                def mmd(e, po=po, d=d, tsl=tsl):
                    ins = None
                    for h in range(8):
                        ins = e.matmul(po[:, :], lhsT=wo[:, h, d * 128:(d + 1) * 128], rhs=self.hT[:, h, tsl], start=(h == 0), stop=(h == 7))
                    return ins
                P.op("pe", mmd, reads=[r_wo] + [r_o[h][tb] for h in range(8)], writes=[r_po])
                xo = self.xT[:, d, tsl]
                P.op("dve", lambda e, xo=xo, po=po, ga=ga: e.scalar_tensor_tensor(
                    out=xo, in0=po[:, :], scalar=ga, in1=xo, op0=ALU.mult, op1=ALU.add),
                    reads=[r_po, self.r_mod], writes=[self.r_x[d][tb]])


    def mlstm(self, s, l):
        P = self.P
        b = l // 2
        rc = self.r_const
        LNS = float(np.log(128.0 ** -0.5))
        self.areset()
        G1 = self.af([128, S])
        G2 = self.af([128, S])
        qpT = self.ab([128, S])
        kpT = self.ab([128, S])
        yT = self.ab([128, 8, S])
        tokS = self.af([128, 16, 8])
        dec_b = self.af([128, 64])
        hn_h = self.af([128, 256])
        small2 = self.af([128, 64])
        kws = [self.ab([128, 128]) for _ in range(2)]
        vaug = [self.ab([128, 264]) for _ in range(2)]
        eo = [self.af([128, 256]) for _ in range(2)]
        smT = [self.ab([128, 128]) for _ in range(2)]
        Cf = self.af([128, 264])
        Cb = [self.ab([128, 264]) for _ in range(2)]
        yh = [self.ab([128, 256]) for _ in range(2)]
        qs = [self.af([128, 512]) for _ in range(2)]
        junk = qs[0][:, 0:256]
        tri = self.cf[:, CF_TRI:CF_TRI + 128]

        def slot(G, j):
            return G[32 * j:32 * j + 4, :]
        A = [slot(G1, j) for j in range(4)]
        B = [slot(G2, j) for j in range(4)]
        r_A = [Res("A%d" % j) for j in range(4)]
        r_B = [Res("B%d" % j) for j in range(4)]
        wg = [self.WB[:, 8192 + g * 1024:8192 + (g + 1) * 1024].rearrange("p (k n) -> p k n", k=8) for g in range(2)]
        r_wg = Res("wg")
        for g in range(2):
            P.dma("pool", "wg", self.WB[:, 8192 + g * 1024:8192 + (g + 1) * 1024], self.d_mlg[b, g], cowrites=[r_wg])
        r_gz = Res("gz")
        P.op("dve", lambda e: e.memset(G1, 0.0), writes=[r_gz])
        P.op("dve", lambda e: e.memset(G2, 0.0), cowrites=[r_gz])
        r_wh = Res("wh")
        Wh = self.WB[:, 0:7168].rearrange("p (k n) -> p k n", k=8)

        def load_head(h):
            for k2 in range(4):
                kw = dict(writes=[r_wh]) if k2 == 0 else dict(cowrites=[r_wh])
                P.dma("pool", "wh", self.WB[:, k2 * 1792:(k2 + 1) * 1792], self.d_mlw[b, h, :, k2 * 1792:(k2 + 1) * 1792], **kw)
        load_head(0)
        ib = self.pp[0:4, PP_GB + 2 * b:PP_GB + 2 * b + 1]
        fb = self.pp[0:4, PP_GB + 2 * b + 1:PP_GB + 2 * b + 2]
        for tb in range(NTB):
            tsl = slice(tb * TB, (tb + 1) * TB)
            pi = self.ps[(tb % 2) * 2]
            pf = self.ps[(tb % 2) * 2 + 1]
            r_pi = self.r_ps[(tb % 2) * 2]
            r_pf = self.r_ps[(tb % 2) * 2 + 1]

            def mmg(e, pp_=pi, g=0, tsl=tsl):
                ins = None
                for k in range(8):
                    ins = e.matmul(pp_[:, :], lhsT=wg[g][:, k, :], rhs=self.hT[:, k, tsl], start=(k == 0), stop=(k == 7))
                return ins
            P.op("pe", mmg, reads=[r_wg, self.r_h[tb]], writes=[r_pi])
            P.op("pe", lambda e, f=mmg, pf=pf, tsl=tsl: f(e, pf, 1, tsl), reads=[r_wg, self.r_h[tb]], writes=[r_pf])
            kw = dict(writes=[r_A[0]]) if tb == 0 else dict(cowrites=[r_A[0]])
            P.op("act", lambda e, pi=pi, tsl=tsl: e.activation(out=A[0][:, tsl], in_=pi[0:4, :], func=AF.Identity, bias=ib),
                 reads=[r_pi, rc, r_gz], **kw)
            kw = dict(writes=[r_A[1]]) if tb == 0 else dict(cowrites=[r_A[1]])
            P.op("act", lambda e, pf=pf, tsl=tsl: e.activation(out=A[1][:, tsl], in_=pf[0:4, :], func=AF.Identity, bias=fb),
                 reads=[r_pf, rc, r_gz], **kw)
        P.op("act", lambda e: e.activation(out=A[1], in_=A[1], func=AF.Exp, scale=-1.0), reads=[r_A[1]], writes=[r_A[1]])
        P.op("act", lambda e: e.activation(out=A[1], in_=A[1], func=AF.Ln, bias=1.0), reads=[r_A[1]], writes=[r_A[1]])
        P.op("dve", lambda e: e.tensor_tensor_scan(out=B[0], data0=A[1], data1=A[1], initial=0.0, op0=ALU.add, op1=ALU.max),
             reads=[r_A[1]], writes=[r_B[0]])
        P.op("dve", lambda e: e.tensor_tensor(out=B[1], in0=A[0], in1=B[0], op=ALU.add), reads=[r_A[0], r_B[0]], writes=[r_B[1]])
        P.op("dve", lambda e: e.tensor_tensor_scan(out=A[1], data0=B[1], data1=B[1], initial=0.0, op0=ALU.max, op1=ALU.max),
             reads=[r_B[1]], writes=[r_A[1]])
        P.op("dve", lambda e: e.tensor_tensor_scan(out=A[0], data0=B[1], data1=B[1], initial=0.0, op0=ALU.max, op1=ALU.max),
             reads=[r_B[1]], writes=[r_A[0]])

        def v3(ap):
            return ap.rearrange("p (c n) -> p c n", c=16)
        Mxa3 = v3(A[1]); Mxb3 = v3(A[0]); a3 = v3(B[1]); Gn3 = v3(B[0])

        def ref_of(M3):
            return M3[:, 0:15, 127:128].broadcast_to([4, 15, 128])

        def end_of(M3):
            return M3[:, :, 127:128].broadcast_to([4, 16, 128])
        o3 = v3(A[2])
        P.op("dve", lambda e: e.tensor_tensor(out=o3[:, 1:16, :], in0=ref_of(Mxb3), in1=Mxb3[:, 1:16, :], op=ALU.subtract),
             reads=[r_A[0]], writes=[r_A[2]])
        P.op("dve", lambda e: e.tensor_scalar(out=o3[:, 0:1, :], in0=Mxb3[:, 0:1, :], scalar1=-1.0, scalar2=None, op0=ALU.mult),
             reads=[r_A[0]], cowrites=[r_A[2]])
        P.op("act", lambda e: e.activation(out=A[2], in_=A[2], func=AF.Exp), reads=[r_A[2]], writes=[r_A[2]])
        o3b = v3(B[2])
        P.op("dve", lambda e: e.tensor_tensor(out=o3b[:, 1:16, :], in0=a3[:, 1:16, :], in1=ref_of(Mxa3), op=ALU.subtract),
             reads=[r_B[1], r_A[1]], writes=[r_B[2]])
        P.op("dve", lambda e: e.tensor_copy(out=o3b[:, 0:1, :], in_=a3[:, 0:1, :]), reads=[r_B[1]], cowrites=[r_B[2]])
        P.op("act", lambda e: e.activation(out=B[2], in_=B[2], func=AF.Exp, bias=LNS), reads=[r_B[2]], writes=[r_B[2]])
        P.op("dve", lambda e: e.tensor_tensor(out=v3(A[3]), in0=a3, in1=end_of(Mxa3), op=ALU.subtract),
             reads=[r_B[1], r_A[1]], writes=[r_A[3]])
        P.op("act", lambda e: e.activation(out=A[3], in_=A[3], func=AF.Exp, bias=LNS), reads=[r_A[3]], writes=[r_A[3]])
        P.op("dve", lambda e: e.tensor_tensor(out=B[3], in0=B[0], in1=A[0], op=ALU.subtract), reads=[r_B[0], r_A[0]], writes=[r_B[3]])
        P.op("act", lambda e: e.activation(out=B[3], in_=B[3], func=AF.Exp, scale=2.0), reads=[r_B[3]], writes=[r_B[3]])
        r_s2 = Res("small2")
        dr = small2[0:4, 0:16]
        P.op("dve", lambda e: e.memset(small2[:, 0:16], 0.0), writes=[r_s2])
        P.op("dve", lambda e: e.tensor_tensor(out=small2[0:4, 1:16].unsqueeze(2), in0=Mxb3[:, 0:15, 127:128], in1=Mxb3[:, 1:16, 127:128],
                                              op=ALU.subtract), reads=[r_A[0]], cowrites=[r_s2])
        P.op("dve", lambda e: e.tensor_scalar(out=small2[0:4, 0:1].unsqueeze(2), in0=Mxb3[:, 0:1, 127:128], scalar1=-1.0, scalar2=None,
                                              op0=ALU.mult), reads=[r_A[0]], cowrites=[r_s2])
        P.op("act", lambda e: e.activation(out=dr, in_=dr, func=AF.Exp), reads=[r_s2], writes=[r_s2])
        r_dec = Res("dec")
        for h in range(4):
            sel0 = self.cf[:, CF_SELA + h * 128:CF_SELA + (h + 1) * 128]
            P.op("pe", lambda e, sel0=sel0: e.matmul(self.ps[4][:, 0:16], lhsT=sel0, rhs=small2[:, 0:16], start=True, stop=True),
                 reads=[r_s2, rc], writes=[self.r_ps[4]])
            kw = dict(writes=[r_dec]) if h == 0 else dict(cowrites=[r_dec])
            P.op("act", lambda e, h=h: e.activation(out=dec_b[:, h * 16:(h + 1) * 16], in_=self.ps[4][:, 0:16], func=AF.Copy),
                 reads=[self.r_ps[4]], **kw)
        r_tok = Res("tok")
        for c in range(16):
            csl = slice(c * 128, (c + 1) * 128)
            for gi, (G, rG, col) in enumerate(((G1, r_A, 0), (G2, r_B, 4))):
                bk = (2 * c + gi) % 8
                P.op("pe", lambda e, G=G, csl=csl, bk=bk: e.transpose(out=self.ps[bk][:, 0:128], in_=G[:, csl], identity=self.ident_f),
                     reads=[rG[0], rG[1], rG[2], rG[3], rc], writes=[self.r_ps[bk]])
                P.op("act", lambda e, c=c, col=col, bk=bk: e.activation(out=tokS[:, c, col:col + 4], in_=self.ps[bk][:, 96:100], func=AF.Copy),
                     reads=[self.r_ps[bk]], cowrites=[r_tok])
        if s == 0 and l == 1 and self.dbg:
            self.dump("G1", G1, [128, S], F32)
            self.dump("G2", G2, [128, S], F32)
            self.dump("tokS", tokS, [128, 16, 8], F32)
            self.dump("decb", dec_b, [128, 64], F32)
        r_qp = [Res("qp%d" % t) for t in range(NTB)]
        r_kp = [Res("kp%d" % t) for t in range(NTB)]
        r_qs = [Res("qs0"), Res("qs1")]
        r_hn = Res("hn")
        r_kws = [Res("kws0"), Res("kws1")]
        r_va = [Res("va0"), Res("va1")]
        r_eo = [Res("eo0"), Res("eo1")]
        r_sm = [Res("sm0"), Res("sm1")]
        r_Cf = Res("Cf")
        r_Cb = [Res("Cb0"), Res("Cb1")]
        r_yh = [Res("yh0"), Res("yh1")]
        r_junk = r_qs[0]
        r_yT = [[Res("yT%d_%d" % (j, t)) for t in range(NTB)] for j in range(8)]
        r_st = Res("st")
        for vb in range(2):
            P.op("dve", lambda e, vb=vb: e.memset(vaug[vb][:, 256:264], 1.0), cowrites=[r_va[vb]])
        qi = 0
        for h in range(4):
            P.dma("sp", "hn", hn_h, self.d_hn[:, b, h * 256:(h + 1) * 256], writes=[r_hn])
            selh = self.cf[:, CF_SELB + h * 128:CF_SELB + (h + 1) * 128]
            for tb in range(NTB):
                tsl = slice(tb * TB, (tb + 1) * TB)
                for (c0, Gsl, rG, dstT, r_d) in ((0, G1[:, tsl], r_A[2], qpT, r_qp), (128, G2[:, tsl], r_B[2], kpT, r_kp)):
                    pb = self.ps[(qi % 2) * 2]
                    pq = self.ps[(qi % 2) * 2 + 1]
                    r_pb = self.r_ps[(qi % 2) * 2]
                    r_pq = self.r_ps[(qi % 2) * 2 + 1]
                    sc = qs[qi % 2]
                    r_sc = r_qs[qi % 2]
                    qi += 1
                    P.op("pe", lambda e, pb=pb, Gsl=Gsl, selh=selh: e.matmul(pb[:, :], lhsT=selh, rhs=Gsl, start=True, stop=True),
                         reads=[rG, rc], writes=[r_pb])

                    def mmq(e, pq=pq, c0=c0, tsl=tsl):
                        ins = None
                        for k in range(8):
                            ins = e.matmul(pq[:, :], lhsT=Wh[:, k, c0:c0 + 128], rhs=self.hT[:, k, tsl], start=(k == 0), stop=(k == 7))
                        return ins
                    P.op("pe", mmq, reads=[r_wh, self.r_h[tb]], writes=[r_pq])
                    P.op("act", lambda e, sc=sc, pq=pq: e.activation(out=sc, in_=pq[:, :], func=AF.Copy), reads=[r_pq], writes=[r_sc])
                    P.op("dve", lambda e, o=dstT[:, tsl], sc=sc, pb=pb: e.tensor_tensor(out=o, in0=sc, in1=pb[:, :], op=ALU.mult),
                         reads=[r_sc, r_pb], writes=[r_d[tb]])
            def proj(c, h=h):
                t0 = c * 128
                ia = c % 2
                io = 2 if c % 2 == 0 else 7
                pa = self.ps[ia]
                po = self.ps[io]

                def mma(e, pa=pa, t0=t0):
                    ins = None
                    for k in range(8):
                        ins = e.matmul(pa[:, 0:384], lhsT=self.hT[:, k, t0:t0 + 128], rhs=Wh[:, k, 256:640], start=(k == 0), stop=(k == 7))
                    return ins

                def mmo(e, po=po, t0=t0):
                    ins = None
                    for k in range(8):
                        ins = e.matmul(po[:, 0:256], lhsT=self.hT[:, k, t0:t0 + 128], rhs=Wh[:, k, 640:896], start=(k == 0), stop=(k == 7))
                    return ins
                P.op("pe", mma, reads=[r_wh, self.r_h[c // 4]], writes=[self.r_ps[ia]])
                P.op("pe", mmo, reads=[r_wh, self.r_h[c // 4]], writes=[self.r_ps[io]])
                cb_ = c % 2
                wsc = tokS[:, c, h:h + 1]
                P.op("act", lambda e, pa=pa, cb_=cb_, wsc=wsc: e.activation(out=kws[cb_], in_=pa[:, 0:128], func=AF.Copy, scale=wsc),
                     reads=[self.r_ps[ia], r_tok], writes=[r_kws[cb_]])
                P.op("act", lambda e, pa=pa, cb_=cb_: e.activation(out=vaug[cb_][:, 0:256], in_=pa[:, 128:384], func=AF.Copy),
                     reads=[self.r_ps[ia]], cowrites=[r_va[cb_]])
                P.op("act", lambda e, po=po, cb_=cb_: e.activation(out=eo[cb_], in_=po[:, 0:256], func=AF.Exp, scale=-1.0),
                     reads=[self.r_ps[io]], writes=[r_eo[cb_]])
                P.op("act", lambda e, cb_=cb_: e.activation(out=eo[cb_], in_=eo[cb_], func=AF.Ln, bias=1.0),
                     reads=[r_eo[cb_]], writes=[r_eo[cb_]])
                P.op("act", lambda e, cb_=cb_: e.activation(out=eo[cb_], in_=eo[cb_], func=AF.Exp, scale=-1.0),
                     reads=[r_eo[cb_]], writes=[r_eo[cb_]])
                P.op("dve", lambda e, cb_=cb_: e.tensor_tensor(out=eo[cb_], in0=eo[cb_], in1=hn_h, op=ALU.mult),
                     reads=[r_eo[cb_], r_hn], writes=[r_eo[cb_]])

            pT = self.ps[3].bitcast(BF16)

            def recur(c, h=h):
                t0 = c * 128
                csl = slice(t0, t0 + 128)
                tbq = c // 4
                cb_ = c % 2
                pn = self.ps[4 + cb_]
                r_pn = self.r_ps[4 + cb_]
                P.op("pe", lambda e: e.matmul(self.ps[3][:, 0:128], lhsT=kpT[:, csl], rhs=qpT[:, csl], start=True, stop=True),
                     reads=[r_kp[tbq], r_qp[tbq]], writes=[self.r_ps[3]])
                P.op("dve", lambda e: e.tensor_tensor(out=smT[cb_], in0=self.ps[3][:, 0:128], in1=tri, op=ALU.mult),
                     reads=[self.r_ps[3], rc], writes=[r_sm[cb_]])
                Cprev = Cb[(c + 1) % 2]

                def mmn(e):
                    if c > 0:
                        e.matmul(pn[:, 0:257], lhsT=qpT[:, csl], rhs=Cprev[:, 0:257], start=True, stop=False)
                    return e.matmul(pn[:, 0:257], lhsT=smT[cb_], rhs=vaug[cb_][:, 0:257], start=(c == 0), stop=True)
                rd = [r_qp[tbq], r_sm[cb_], r_va[cb_]]
                if c > 0:
                    rd.append(r_Cb[(c + 1) % 2])
                P.op("pe", mmn, reads=rd, writes=[r_pn])
                if c < 15:
                    P.op("pe", lambda e: e.matmul(self.ps[6][:, 0:257], lhsT=kws[cb_], rhs=vaug[cb_][:, 0:257], start=True, stop=True),
                         reads=[r_kws[cb_], r_va[cb_]], writes=[self.r_ps[6]])
                    if c == 0:
                        P.op("dve", lambda e: e.tensor_copy(out=Cb[cb_][:, 0:257], in_=self.ps[6][:, 0:257]), reads=[self.r_ps[6]], writes=[r_Cb[cb_]])
                        P.op("dve", lambda e: e.tensor_copy(out=Cf[:, 0:257], in_=self.ps[6][:, 0:257]), reads=[self.r_ps[6]], writes=[r_Cf])
                    else:
                        dsc = dec_b[:, h * 16 + c:h * 16 + c + 1]
                        P.op("dve", lambda e: e.scalar_tensor_tensor(out=Cb[cb_][:, 0:257], in0=Cf[:, 0:257], scalar=dsc, in1=self.ps[6][:, 0:257],
                                                                     op0=ALU.mult, op1=ALU.add),
                             reads=[self.r_ps[6], r_Cf, r_dec], writes=[r_Cb[cb_]])
                        if c < 14:
                            P.op("dve", lambda e: e.scalar_tensor_tensor(out=Cf[:, 0:257], in0=Cf[:, 0:257], scalar=dsc, in1=self.ps[6][:, 0:257],
                                                                         op0=ALU.mult, op1=ALU.add),
                                 reads=[self.r_ps[6], r_Cf, r_dec], writes=[r_Cf])
                st = small2[:, 16 + 8 * cb_:24 + 8 * cb_]
                emt2 = tokS[:, c, 4 + h:5 + h]
                P.op("act", lambda e: e.activation(out=junk, in_=pn[:, 0:256], func=AF.Square, accum_out=st[:, 2:3]),
                     reads=[r_pn], writes=[r_junk], cowrites=[r_st])
                P.op("act", lambda e: e.activation(out=st[:, 6:7], in_=pn[:, 256:257], func=AF.Square),
                     reads=[r_pn], cowrites=[r_st])
                P.op("dve", lambda e: e.tensor_scalar(out=st[:, 0:1], in0=st[:, 6:7], scalar1=emt2, scalar2=EPS, op0=ALU.max, op1=ALU.mult),
                     reads=[r_st, r_tok], cowrites=[r_st])
                P.op("dve", lambda e: e.scalar_tensor_tensor(out=st[:, 1:2], in0=st[:, 2:3], scalar=1.0 / 256, in1=st[:, 0:1],
                                                             op0=ALU.mult, op1=ALU.add), reads=[r_st], cowrites=[r_st])
                P.op("act", lambda e: e.activation(out=st[:, 3:4], in_=st[:, 1:2], func=AF.Ln), reads=[r_st], cowrites=[r_st])
                P.op("act", lambda e: e.activation(out=st[:, 5:6], in_=st[:, 3:4], func=AF.Exp, scale=-0.5), reads=[r_st], cowrites=[r_st])
                P.op("dve", lambda e: e.scalar_tensor_tensor(out=yh[cb_], in0=pn[:, 0:256], scalar=st[:, 5:6], in1=eo[cb_],
                                                             op0=ALU.mult, op1=ALU.mult),
                     reads=[r_pn, r_st, r_eo[cb_]], writes=[r_yh[cb_]])

            def ytrans(c, h=h):
                t0 = c * 128
                csl = slice(t0, t0 + 128)
                tbq = c // 4
                cb_ = c % 2
                for j in range(2):
                    P.op("pe", lambda e, j=j: e.matmul(self.ps[3][:, 128 + j * 128:128 + (j + 1) * 128], lhsT=yh[cb_][:, j * 128:(j + 1) * 128],
                                                       rhs=self.ident_b, start=True, stop=True),
                         reads=[r_yh[cb_], rc], **(dict(writes=[self.r_ps[3]]) if j == 0 else dict(cowrites=[self.r_ps[3]])))
                P.op("act", lambda e: e.activation(out=yT[:, 2 * h:2 * h + 2, csl], in_=self.ps[3][:, 128:384].rearrange("p (a b) -> p a b", a=2), func=AF.Copy),
                     reads=[self.r_ps[3]], cowrites=[r_yT[2 * h][tbq], r_yT[2 * h + 1][tbq]])
            proj(0)
            for c in range(16):
                if c + 1 < 16:
                    proj(c + 1)
                recur(c)
                if c > 0:
                    ytrans(c - 1)
            ytrans(15)
            if h + 1 < 4:
                load_head(h + 1)
        if s == 0 and l == 1 and self.dbg == 2:
            self.dump("qpT", qpT, [128, S], BF16)
            self.dump("kpT", kpT, [128, S], BF16)
            for j in range(8):
                self.dump("yT%d" % j, yT[:, j, :], [128, S], BF16)
            self.dump("small2", small2, [128, 64], F32)
        wo = self.WB[:, 0:8192].rearrange("p (h n) -> p h n", h=8)
        r_wo = Res("wo")
        for h2 in range(4):
            kw = dict(writes=[r_wh, r_wo]) if h2 == 0 else dict(cowrites=[r_wo])
            P.dma("pool", "wh", self.WB[:, h2 * 2048:(h2 + 1) * 2048], self.d_mlo[b, :, h2 * 2048:(h2 + 1) * 2048], **kw)
        it = 0
        for d in range(8):
            ga = self.mod_ap(l, 2, d, s)
            for tb in range(NTB):
                tsl = slice(tb * TB, (tb + 1) * TB)
                po = self.ps[it % 2]
                r_po = self.r_ps[it % 2]
                it += 1

                def mmd(e, po=po, d=d, tsl=tsl):
                    ins = None
                    for j in range(8):
                        ins = e.matmul(po[:, :], lhsT=wo[:, j, d * 128:(d + 1) * 128], rhs=yT[:, j, tsl], start=(j == 0), stop=(j == 7))
                    return ins
                P.op("pe", mmd, reads=[r_wo] + [r_yT[j][tb] for j in range(8)], writes=[r_po])
                xo = self.xT[:, d, tsl]
                P.op("dve", lambda e, xo=xo, po=po, ga=ga: e.scalar_tensor_tensor(
                    out=xo, in0=po[:, :], scalar=ga, in1=xo, op0=ALU.mult, op1=ALU.add),
                    reads=[r_po, self.r_mod], writes=[self.r_x[d][tb]])


def _consts():
    cf = np.zeros((128, NCF), np.float32)
    cf[:, CF_ID:CF_ID + 128] = np.eye(128, dtype=np.float32)
    cf[:, CF_ONE:CF_ONE + 128] = 1.0
    cf[:, CF_TRI:CF_TRI + 128] = np.triu(np.ones((128, 128), np.float32))
    for h in range(4):
        cf[h, CF_SEL + h * 128:CF_SEL + (h + 1) * 128] = 1.0
        cf[64 + h, CF_SEL + h * 128:CF_SEL + (h + 1) * 128] = 1.0
        cf[h, CF_SELA + h * 128:CF_SELA + (h + 1) * 128] = 1.0
        cf[64 + h, CF_SELB + h * 128:CF_SELB + (h + 1) * 128] = 1.0
    cb = np.zeros((128, NCB), np.float32)
    cb[:, CB_ID:CB_ID + 128] = np.eye(128, dtype=np.float32)
    cb[:, CB_ONE:CB_ONE + 128] = 1.0
    cb[0, CB_MROW + 64:CB_MROW + 128] = -30000.0
    return cf, cb.astype(ml_dtypes.bfloat16)


def _col(v, nchunk):
    return np.ascontiguousarray(np.asarray(v, np.float32).reshape(nchunk, 128).T)


def prep_shared(inp):
    f32 = np.float32
    sh = {}
    pp = np.zeros((128, NPP), f32)
    for l in range(DEPTH):
        pp[:, PP_MODB + l * 48:PP_MODB + (l + 1) * 48] = _col(inp["mod_b"][l], 48)
        for tap in range(3):
            pp[:, PP_CW + l * 66 + tap * 22:PP_CW + l * 66 + (tap + 1) * 22] = _col(inp["ffn_conv_w"][l, tap], 22)
        pp[:, PP_CB + l * 22:PP_CB + (l + 1) * 22] = _col(inp["ffn_conv_b"][l], 22)
    for a in range(2):
        pp[:, PP_QN + a * 4:PP_QN + (a + 1) * 4] = _col(inp["mla_q_norm"][a], 4)
        pp[:, PP_KVN + a * 2:PP_KVN + (a + 1) * 2] = _col(inp["mla_kv_norm"][a], 2)
        pp[0:4, PP_GB + a * 2] = np.asarray(inp["ml_b_gates"][a][0:4], f32)
        pp[0:4, PP_GB + a * 2 + 1] = np.asarray(inp["ml_b_gates"][a][4:8], f32)
    pp[:, PP_FN:PP_FN + 8] = _col(inp["final_norm"], 8)
    sh["pp"] = pp
    cf, cb = _consts()
    sh["cf"] = cf
    sh["cb"] = cb
    mw = np.asarray(inp["mod_w"], f32)
    sh["modw"] = np.ascontiguousarray(mw.reshape(DEPTH, 8, 128, 12, 512).transpose(0, 3, 2, 1, 4))
    hn = np.asarray(inp["ml_head_norm"], f32)
    sh["hn"] = np.ascontiguousarray(np.broadcast_to(hn[None, :, :], (128, 2, 1024)))
    wu = np.asarray(inp["ffn_w_up"], f32)
    wa = wu[:, :, :DFF].reshape(DEPTH, 8, 128, NFC, 128)
    wg = wu[:, :, DFF:].reshape(DEPTH, 8, 128, NFC, 128)
    fup = np.concatenate([wa, wg], axis=-1)
    sh["fup"] = np.ascontiguousarray(fup.transpose(0, 3, 2, 1, 4))
    wd = np.asarray(inp["ffn_w_down"], f32)
    wd = wd.reshape(DEPTH, 2, 11, 128, 8, 128)
    sh["fdn"] = np.ascontiguousarray(wd.transpose(0, 1, 4, 3, 2, 5))
    jj = np.arange(0, 64, 2, dtype=f32) / f32(64)
    invf = (f32(1.0) / (f32(10000.0) ** jj)).astype(f32)
    pp[0:64, PP_IF] = np.concatenate([invf, invf])
    win = np.asarray(inp["mla_w_in"], f32)
    win = np.concatenate([win, win[:, :, 800:832], win[:, :, 768:800]], axis=-1)
    sh["mwin"] = np.ascontiguousarray(win.reshape(2, 8, 128, 896).transpose(0, 2, 1, 3).reshape(2, 128, 8 * 896))
    wq = np.asarray(inp["mla_w_q_up"], f32).reshape(2, 4, 128, 8, 192)
    wq = np.concatenate([wq, wq[..., 160:192], wq[..., 128:160]], axis=-1)
    sh["mwq"] = np.ascontiguousarray(wq.transpose(0, 3, 2, 1, 4).reshape(2, 8, 128, 4 * 256))
    wkv = np.asarray(inp["mla_w_kv_up"], f32).reshape(2, 2, 128, 8, 256)
    sh["mwkv"] = np.ascontiguousarray(wkv.transpose(0, 3, 2, 1, 4).reshape(2, 8, 128, 2 * 256))
    wo = np.asarray(inp["mla_w_out"], f32).reshape(2, 8, 128, 1024)
    sh["mwo"] = np.ascontiguousarray(wo.transpose(0, 2, 1, 3).reshape(2, 128, 8 * 1024))
    mw_ = np.asarray(inp["ml_w_in"], f32).reshape(2, 8, 128, 3080)
    heads = []
    for h in range(4):
        heads.append(np.concatenate([mw_[..., h * 128:(h + 1) * 128], mw_[..., 512 + h * 128:512 + (h + 1) * 128],
                                     mw_[..., 512 + h * 128:512 + (h + 1) * 128],
                                     mw_[..., 1024 + h * 256:1024 + (h + 1) * 256],
                                     mw_[..., 2048 + h * 256:2048 + (h + 1) * 256]], axis=-1))
    mlw = np.stack(heads, axis=1)
    sh["mlw"] = np.ascontiguousarray(mlw.transpose(0, 1, 3, 2, 4).reshape(2, 4, 128, 8 * 896))
    mlg = np.zeros((2, 2, 128, 8, 128), f32)
    for g in range(2):
        mlg[:, g, :, :, 0:4] = mw_[..., 3072 + 4 * g:3076 + 4 * g].transpose(0, 2, 1, 3)
    sh["mlg"] = np.ascontiguousarray(mlg.reshape(2, 2, 128, 8 * 128))
    mo = np.asarray(inp["ml_w_out"], f32).reshape(2, 8, 128, 1024)
    sh["mlo"] = np.ascontiguousarray(mo.transpose(0, 2, 1, 3).reshape(2, 128, 8 * 1024))
    return sh


def prep_core(inp, core, n_seq=2):
    b0 = core * n_seq
    x = np.asarray(inp["x"][b0:b0 + n_seq], np.float32)
    xT = np.ascontiguousarray(x.reshape(n_seq, S, 8, 128).transpose(0, 3, 2, 1))
    c = np.asarray(inp["c"][b0:b0 + n_seq], np.float32)
    cT = np.ascontiguousarray(c.reshape(n_seq, 8, 128).transpose(2, 1, 0))
    pos = np.ascontiguousarray(np.asarray(inp["positions"][b0:b0 + n_seq], np.int32))
    return {"xT": xT, "cT": cT, "pos": pos}


def unpack_out(outT):
    ns = outT.shape[0]
    return np.ascontiguousarray(outT.transpose(0, 3, 2, 1).reshape(ns, S, D))


_CACHE = {}


def get_nc(stages=None, dbg=False):
    key = (tuple(stages) if stages is not None else None, dbg)
    if key not in _CACHE:
        b = Builder(stages=stages, dbg=dbg)
        nc = b.build()
        _CACHE[key] = (nc, b)
    return _CACHE[key]


def run(inp, cores=range(8), stages=None, trace=False, dbg=False):
    nc, b = get_nc(stages, dbg)
    sh = prep_shared(inp)
    in_maps = []
    for c in cores:
        m = dict(sh)
        m.update(prep_core(inp, c))
        in_maps.append(m)
    res = run_bass_kernel_spmd(nc, in_maps, core_ids=list(range(len(in_maps))), trace=trace)
    outs = [unpack_out(r["outT"]) for r in res.results]
    return np.concatenate(outs, axis=0), res


def kernel(**inputs):
    out, _ = run(inputs)
    return out.astype(np.float32)
```

```python
import numpy as np
import ml_dtypes
from contextlib import ExitStack
import concourse.bass as bass
import concourse.mybir as mybir
from concourse.bass_utils import run_bass_kernel_spmd

F32 = mybir.dt.float32
BF16 = mybir.dt.bfloat16
F32R = mybir.dt.float32r
I32 = mybir.dt.int32
AF = mybir.ActivationFunctionType
ALU = mybir.AluOpType
AX = mybir.AxisListType

D = 1024
S = 2048
DEPTH = 4
DFF = 2816
NFC = 22
FF_SPLIT = (11, 11)
EPS = 1e-6
TB = 512
NTB = S // TB

PP_MODB = 0
PP_QN = 192
PP_KVN = 200
PP_CW = 204
PP_CB = 468
PP_FN = 556
PP_GB = 564
PP_IF = 568
NPP = 576
CF_ID = 0
CF_ONE = 128
CF_TRI = 256
CF_SEL = 384
CF_SELA = 896
CF_SELB = 1408
NCF = 1920
CB_ID = 0
CB_ONE = 128
CB_MROW = 256
NCB = 384


class Res:
    __slots__ = ("name", "writers", "readers")

    def __init__(self, name):
        self.name = name
        self.writers = []
        self.readers = []


class Prog:
    ENG = ("pe", "act", "dve", "pool", "sp")

    def __init__(self, nc, stack):
        self.nc = nc
        self.stack = stack
        self.q = {e: [] for e in self.ENG}
        self.n = {e: 0 for e in self.ENG}
        self.seen = {e: {} for e in self.ENG}
        self.needed = {e: set() for e in self.ENG}
        self.csem = {e: stack.enter_context(nc.semaphore("cs_" + e)) for e in self.ENG}
        self.dsem = {}
        self.dtot = {}

    def _prune(self, eng, waits, raw_set):
        best = {}
        for ev in waits:
            if ev[0] == "c":
                if ev[1] == eng and id(ev) not in raw_set:
                    continue
                key = ("c", ev[1])
            else:
                key = ("d", ev[1])
            if ev[2] > best.get(key, 0):
                best[key] = ev[2]
        out = []
        seen = self.seen[eng]
        for key, val in best.items():
            if val <= seen.get(key, 0):
                continue
            seen[key] = val
            out.append((key[0], key[1], val))
            if key[0] == "c":
                self.needed[key[1]].add(val)
        return out

    @staticmethod
    def _deps(reads, writes, cowrites):
        waits = []
        raw = set()
        for r in reads:
            for ev in r.writers:
                waits.append(ev)
                raw.add(id(ev))
        for r in writes:
            waits += r.writers
            waits += r.readers
        for r in cowrites:
            waits += r.readers
        return waits, raw

    @staticmethod
    def _update(ev, reads, writes, cowrites):
        for r in reads:
            r.readers.append(ev)
        for r in writes:
            r.writers = [ev]
            r.readers = []
        for r in cowrites:
            r.writers.append(ev)

    def op(self, eng, fn, reads=(), writes=(), cowrites=()):
        waits, raw = self._deps(reads, writes, cowrites)
        w = self._prune(eng, waits, raw)
        self.n[eng] += 1
        ev = ("c", eng, self.n[eng])
        self.q[eng].append((fn, w, ev))
        self._update(ev, reads, writes, cowrites)
        return ev

    def dma(self, eng, sem, out, in_, reads=(), writes=(), cowrites=(), **kw):
        if sem not in self.dsem:
            self.dsem[sem] = self.stack.enter_context(self.nc.semaphore("ds_" + sem))
            self.dtot[sem] = 0
        waits, raw = self._deps(reads, writes, cowrites)
        w = self._prune(eng, waits, raw)
        self.dtot[sem] += 16
        ev = ("d", sem, self.dtot[sem])
        self.q[eng].append((lambda e: e.dma_start(out=out, in_=in_, **kw), w, ev))
        self._update(ev, reads, writes, cowrites)
        return ev

    def barrier(self, clear=()):
        evs = []
        for e in self.ENG:
            if self.n[e] > 0:
                evs.append(("c", e, self.n[e]))
        for s, t in self.dtot.items():
            if t > 0:
                evs.append(("d", s, t))
        for e in self.ENG:
            w = self._prune(e, [ev for ev in evs if not (ev[0] == "c" and ev[1] == e)], set())
            if w:
                self.q[e].append((None, w, None))
        for r in clear:
            r.writers = []
            r.readers = []

    def finish(self, final_events):
        w = self._prune("sp", list(final_events), set(id(e) for e in final_events))
        self.q["sp"].append((None, w, None))
        nc = self.nc
        rank = {}
        for e in self.ENG:
            rank[e] = {idx: i + 1 for i, idx in enumerate(sorted(self.needed[e]))}
        self.stats = {e: len(self.q[e]) for e in self.ENG}
        self.stats["incs"] = {e: len(rank[e]) for e in self.ENG}

        def replay(ename, eobj):
            for fn, waits, ev in self.q[ename]:
                for (kind, key, val) in waits:
                    if kind == "c":
                        eobj.wait_ge(self.csem[key], rank[key][val])
                    else:
                        eobj.wait_ge(self.dsem[key], val)
                if fn is None:
                    continue
                ins = fn(eobj)
                if ev is None:
                    continue
                if ev[0] == "c":
                    if ev[2] in rank[ename]:
                        ins.then_inc(self.csem[ename], 1)
                else:
                    ins.then_inc(self.dsem[ev[1]], 16)

        with nc.Block() as block:
            @block.tensor
            def _(t):
                replay("pe", t)

            @block.scalar
            def _(t):
                replay("act", t)

            @block.vector
            def _(t):
                replay("dve", t)

            @block.gpsimd
            def _(t):
                replay("pool", t)

            @block.sync
            def _(t):
                replay("sp", t)


class Builder:
    def __init__(self, n_seq=2, stages=None, dbg=False):
        self.n_seq = n_seq
        self.stages = stages
        self.dbg = dbg
        self.nc = bass.Bass("TRN2", target_bir_lowering=False)
        self.stack = ExitStack()

    def dram_in(self, name, shape, dt):
        return self.nc.dram_tensor(name, list(shape), dt, kind="ExternalInput").ap()

    def dump(self, name, ap, shape, dt):
        if not self.dbg:
            return
        d = self.nc.dram_tensor("dbg_" + name, list(shape), dt, kind="ExternalOutput").ap()
        self.P.barrier()
        self.P.dma("sp", "dbg", d, ap)
        self.P.barrier()

    def sb(self, name, shape, dt):
        return self.stack.enter_context(self.nc.sbuf_tensor(name, list(shape), dt))

    def build(self):
        nc = self.nc
        with self.stack:
            self.P = Prog(nc, self.stack)
            self._declare()
            self._program()
        return nc

    def _declare(self):
        nc = self.nc
        ns = self.n_seq
        self.d_xT = self.dram_in("xT", [ns, 128, 8, S], F32)
        self.d_cT = self.dram_in("cT", [128, 8, ns], F32)
        self.d_pos = self.dram_in("pos", [ns, S], I32)
        self.d_modw = self.dram_in("modw", [DEPTH, 12, 128, 8, 512], F32)
        self.d_pp = self.dram_in("pp", [128, NPP], F32)
        self.d_cf = self.dram_in("cf", [128, NCF], F32)
        self.d_cb = self.dram_in("cb", [128, NCB], BF16)
        self.d_hn = self.dram_in("hn", [128, 2, 1024], F32)
        self.d_fup = self.dram_in("fup", [DEPTH, NFC, 128, 8, 256], F32)
        self.d_fdn = self.dram_in("fdn", [DEPTH, 2, 8, 128, 11, 128], F32)
        self.d_win = self.dram_in("mwin", [2, 128, 8 * 896], F32)
        self.d_wq = self.dram_in("mwq", [2, 8, 128, 4 * 256], F32)
        self.d_wkv = self.dram_in("mwkv", [2, 8, 128, 2 * 256], F32)
        self.d_wo = self.dram_in("mwo", [2, 128, 8 * 1024], F32)
        self.d_mlw = self.dram_in("mlw", [2, 4, 128, 8 * 896], F32)
        self.d_mlg = self.dram_in("mlg", [2, 2, 128, 8 * 128], F32)
        self.d_mlo = self.dram_in("mlo", [2, 128, 8 * 1024], F32)
        self.d_out = nc.dram_tensor("outT", [ns, 128, 8, S], F32, kind="ExternalOutput").ap()

        self.xT = self.sb("xT_sb", [128, 8, S], F32)
        self.hT = self.sb("hT_sb", [128, 8, S], BF16)
        self.pp = self.sb("pp_sb", [128, NPP], F32)
        self.cf = self.sb("cf_sb", [128, NCF], F32)
        self.cb = self.sb("cb_sb", [128, NCB], BF16)
        self.modT = self.sb("modT", [128, DEPTH, 48, 2], F32)
        self.cact = self.sb("cact", [128, 8, 2], F32)
        self.sqr = self.sb("sqr", [128, 2, 512], BF16)
        self.tab = self.sb("ropetab", [128, S], BF16)
        self.r_tab = Res("tab")
        self.WB = self.sb("WB", [128, 12288], BF16)
        self.AR = self.sb("AR", [128, 17792], F32)
        self.ps = [self.stack.enter_context(nc.psum_tensor("ps%d" % i, [128, 512], F32)) for i in range(8)]
        self.r_ps = [Res("ps%d" % i) for i in range(8)]
        self.r_x = [[Res("x%d_%d" % (k, t)) for t in range(NTB)] for k in range(8)]
        self.r_h = [Res("h%d" % t) for t in range(NTB)]
        self.r_const = Res("const")
        self.r_mod = Res("mod")
        self.ident_f = self.cf[:, CF_ID:CF_ID + 128]
        self.ones_f = self.cf[:, CF_ONE:CF_ONE + 128]
        self.ident_b = self.cb[:, CB_ID:CB_ID + 128]
        self.ones_b = self.cb[:, CB_ONE:CB_ONE + 128]

    def ar_f32(self, off_kib, shape):
        n = int(np.prod(shape[1:]))
        o = int(off_kib * 256)
        ap = self.AR[0:shape[0], o:o + n]
        if len(shape) == 3:
            ap = ap.rearrange("p (a b) -> p a b", a=shape[1])
        return ap

    def ar_bf16(self, off_kib, shape):
        n = int(np.prod(shape[1:]))
        o = int(off_kib * 512)
        ap = self.AR.bitcast(BF16)[0:shape[0], o:o + n]
        if len(shape) == 3:
            ap = ap.rearrange("p (a b) -> p a b", a=shape[1])
        return ap

    def areset(self):
        self._acur = 0

    def af(self, shape):
        n = int(np.prod(shape[1:]))
        n = (n + 7) // 8 * 8
        o = self._acur
        self._acur += n
        assert self._acur <= 17792, self._acur
        ap = self.AR[0:shape[0], o:o + int(np.prod(shape[1:]))]
        if len(shape) == 3:
            ap = ap.rearrange("p (a b) -> p a b", a=shape[1])
        return ap

    def ab(self, shape):
        n = int(np.prod(shape[1:]))
        nf = (n + 15) // 16 * 8
        o = self._acur * 2
        self._acur += nf
        assert self._acur <= 17792, self._acur
        ap = self.AR.bitcast(BF16)[0:shape[0], o:o + n]
        if len(shape) == 3:
            ap = ap.rearrange("p (a b) -> p a b", a=shape[1])
        return ap

    def mod_ap(self, l, kind, i, s):
        return self.modT[:, l, kind * 8 + i, s:s + 1]

    def _program(self):
        P = self.P
        st = self.stages
        self.load_x(0)
        self.prologue()
        P.barrier()
        last = []
        for s in range(self.n_seq):
            if s > 0:
                self.load_x(s)
                P.barrier()
            for l in range(DEPTH):
                if st is None or ("mix%d" % l) in st:
                    self.norm_mod(s, l, 1, 0)
                    P.barrier()
                    if l % 2 == 0:
                        self.mla(s, l)
                    else:
                        self.mlstm(s, l)
                    P.barrier()
                if st is None or ("ffn%d" % l) in st:
                    self.norm_mod(s, l, 4, 3)
                    P.barrier()
                    self.ffn(s, l)
                    P.barrier()
            last += self.final_store(s)
            P.barrier()
        P.finish(last)

    def prologue(self):
        P = self.P
        rc = self.r_const
        P.dma("sp", "c0", self.pp[:, :], self.d_pp[:, :], cowrites=[rc])
        P.dma("sp", "c0", self.cf[:, :], self.d_cf[:, :], cowrites=[rc])
        P.dma("sp", "c0", self.cb[:, :], self.d_cb[:, :], cowrites=[rc])
        P.dma("sp", "c0", self.cact[:, :, :], self.d_cT[:, :, 0:2], cowrites=[rc])
        r_ca = Res("cact")
        cact2 = self.cact[:, :, :]
        P.op("act", lambda e: e.activation(out=cact2, in_=cact2, func=AF.Silu), reads=[rc], writes=[r_ca])
        NSTG = 4
        stg = [self.ar_bf16(8 * i, [128, 8, 512]) for i in range(NSTG)]
        r_stg = [Res("stg%d" % i) for i in range(NSTG)]
        cact_b = self.ar_bf16(40, [128, 8, 2])
        P.op("dve", lambda e: e.tensor_copy(out=cact_b, in_=self.cact[:, :, :]), reads=[r_ca], writes=[r_ca])
        it = 0
        for l in range(DEPTH):
            pst = self.ps[l % 2]
            r_p = self.r_ps[l % 2]
            for cbk in range(12):
                sl = it % NSTG
                it += 1
                for k2 in range(2):
                    kw = dict(writes=[r_stg[sl]]) if k2 == 0 else dict(cowrites=[r_stg[sl]])
                    P.dma("pool", "stg%d" % sl, stg[sl][:, k2 * 4:(k2 + 1) * 4, :], self.d_modw[l, cbk, :, k2 * 4:(k2 + 1) * 4, :], **kw)

                def mm(e, sl=sl, cbk=cbk, pst=pst):
                    ins = None
                    for j in range(4):
                        col = (cbk * 4 + j) * 2
                        for k in range(8):
                            ins = e.matmul(pst[:, col:col + 2], lhsT=stg[sl][:, k, j * 128:(j + 1) * 128],
                                           rhs=cact_b[:, k, 0:2], start=(k == 0), stop=(k == 7))
                    return ins
                if cbk == 0:
                    P.op("pe", mm, reads=[r_stg[sl], r_ca], writes=[r_p])
                else:
                    P.op("pe", mm, reads=[r_stg[sl], r_ca], cowrites=[r_p])
            mo = self.modT[:, l, :, :]
            pin = pst[:, 0:96].rearrange("p (a b) -> p a b", a=48)
            bb = self.pp[:, PP_MODB + l * 48:PP_MODB + (l + 1) * 48].unsqueeze(2).broadcast_to([128, 48, 2])
            P.op("dve", lambda e, mo=mo, pin=pin, bb=bb: e.tensor_tensor(out=mo, in0=pin, in1=bb, op=ALU.add),
                 reads=[r_p, rc], cowrites=[self.r_mod])
            for kind in (1, 4):
                m1 = self.modT[:, l, kind * 8:(kind + 1) * 8, :]
                P.op("dve", lambda e, m1=m1: e.tensor_scalar(out=m1, in0=m1, scalar1=1.0, scalar2=None, op0=ALU.add),
                     reads=[self.r_mod], cowrites=[self.r_mod])

    def load_x(self, s):
        P = self.P
        for k in range(8):
            P.dma("sp", "xld%d" % k, self.xT[:, k, :], self.d_xT[s, :, k, :],
                  writes=[self.r_x[k][t] for t in range(NTB)])

    def final_store(self, s):
        P = self.P
        evs = []
        self._norm_core(s, final=True)
        return self._final_evs

    def norm_mod(self, s, l, kind_sc, kind_sh):
        self._norm_core(s, l=l, kind_sc=kind_sc, kind_sh=kind_sh)

    def _norm_core(self, s, l=None, kind_sc=None, kind_sh=None, final=False):
        P = self.P
        sq = [self.sqr[:, 0, :], self.sqr[:, 1, :]]
        r_sq = [Res("sq0"), Res("sq1")]
        lnv = [self.ar_f32(4, [128, 512]), self.ar_f32(6, [128, 512])]
        r_ln = [Res("ln0"), Res("ln1")]
        rstd = [self.ar_f32(8, [128, 512]), self.ar_f32(10, [128, 512])]
        r_rs = [Res("rs0"), Res("rs1")]
        tmp = [self.ar_f32(12 + 2 * i, [128, 512]) for i in range(4)]
        r_tmp = [Res("tmp%d" % i) for i in range(4)]
        ones_r = self.ones_b
        self._final_evs = []
        it = 0
        for tb in range(NTB):
            tsl = slice(tb * TB, (tb + 1) * TB)
            pb = tb % 2
            pst = self.ps[pb]
            r_p = self.r_ps[pb]
            for k in range(8):
                b = it % 2
                it += 1
                xin = self.xT[:, k, tsl]
                P.op("act", lambda e, o=sq[b], i=xin: e.activation(out=o, in_=i, func=AF.Square),
                     reads=[self.r_x[k][tb]], writes=[r_sq[b]])
                kw = dict(reads=[r_sq[b], self.r_const])
                if k == 0:
                    kw["writes"] = [r_p]
                else:
                    kw["cowrites"] = [r_p]
                P.op("pe", lambda e, pst=pst, i=sq[b], k=k: e.matmul(pst[:, :], lhsT=ones_r, rhs=i,
                                                                  start=(k == 0), stop=(k == 7)), **kw)
            P.op("act", lambda e, o=lnv[pb], pst=pst: e.activation(out=o, in_=pst[:, :], func=AF.Ln, scale=1.0 / D, bias=EPS),
                 reads=[r_p], writes=[r_ln[pb]])
            P.op("act", lambda e, o=rstd[pb], i=lnv[pb]: e.activation(out=o, in_=i, func=AF.Exp, scale=-0.5),
                 reads=[r_ln[pb]], writes=[r_rs[pb]])
            for k in range(8):
                tbuf = (tb * 8 + k) % 4
                xin = self.xT[:, k, tsl]
                if not final:
                    sc = self.mod_ap(l, kind_sc, k, s)
                    sh = self.mod_ap(l, kind_sh, k, s)
                    P.op("dve", lambda e, o=tmp[tbuf], x=xin, r=rstd[pb], sc=sc: e.scalar_tensor_tensor(
                        out=o, in0=x, scalar=sc, in1=r, op0=ALU.mult, op1=ALU.mult),
                        reads=[self.r_x[k][tb], r_rs[pb], self.r_mod], writes=[r_tmp[tbuf]])
                    ho = self.hT[:, k, tsl]
                    kw = dict(reads=[r_tmp[tbuf], self.r_mod])
                    if k == 0:
                        kw["writes"] = [self.r_h[tb]]
                    else:
                        kw["cowrites"] = [self.r_h[tb]]
                    if k % 2 == 0:
                        P.op("dve", lambda e, o=ho, i=tmp[tbuf], sh=sh: e.tensor_scalar(
                            out=o, in0=i, scalar1=sh, scalar2=None, op0=ALU.add), **kw)
                    else:
                        P.op("act", lambda e, o=ho, i=tmp[tbuf], sh=sh: e.activation(
                            out=o, in_=i, func=AF.Identity, bias=sh), **kw)
                else:
                    g = self.pp[:, PP_FN + k:PP_FN + k + 1]
                    P.op("dve", lambda e, o=tmp[tbuf], x=xin, r=rstd[pb], g=g: e.scalar_tensor_tensor(
                        out=o, in0=x, scalar=g, in1=r, op0=ALU.mult, op1=ALU.mult),
                        reads=[self.r_x[k][tb], r_rs[pb], self.r_const], writes=[r_tmp[tbuf]])
                    ev = P.dma("sp", "ost%d" % tbuf, self.d_out[s, :, k, tsl], tmp[tbuf], reads=[r_tmp[tbuf]])
                    self._final_evs.append(ev)

    def ffn(self, s, l):
        P = self.P
        u = self.ar_bf16(0, [128, 11, S])
        r_u = [[Res("u%d_%d" % (c, t)) for t in range(NTB)] for c in range(11)]
        AF_W = 2 + S + 2
        a_full = [self.ar_f32(44, [128, AF_W]), self.ar_f32(44 + 8.25, [128, AF_W])]
        r_a = [[Res("a%d_%d" % (b, t)) for t in range(NTB)] for b in range(2)]
        r_az = Res("az")
        tt = [self.ar_f32(61, [128, 512]), self.ar_f32(63, [128, 512])]
        r_t = [Res("t0"), Res("t1")]
        ge = [self.ar_f32(65, [128, 512]), self.ar_f32(67, [128, 512])]
        r_g = [Res("g0"), Res("g1")]
        for b in range(2):
            P.op("dve", lambda e, o=a_full[b][:, 0:2]: e.memset(o, 0.0), cowrites=[r_az])
        NWS = 6
        wsl = [self.WB[:, i * 2048:(i + 1) * 2048] for i in range(NWS)]
        r_w = [Res("w%d" % i) for i in range(NWS)]
        wi = 0
        cwb = PP_CW + l * 66
        cbb = PP_CB + l * 22
        it = 0
        c_glob = 0
        for half, nch in enumerate(FF_SPLIT):
            for cl in range(nch):
                c = c_glob + cl
                sl = wi % NWS
                wi += 1
                wv = wsl[sl].rearrange("p (k n) -> p k n", k=8)
                P.dma("pool", "w%d" % sl, wsl[sl], self.d_fup[l, c].rearrange("p k n -> p (k n)"), writes=[r_w[sl]])
                ab = c % 2
                w0 = self.pp[:, cwb + 0 * 22 + c:cwb + 0 * 22 + c + 1]
                w1 = self.pp[:, cwb + 1 * 22 + c:cwb + 1 * 22 + c + 1]
                w2 = self.pp[:, cwb + 2 * 22 + c:cwb + 2 * 22 + c + 1]
                bia = self.pp[:, cbb + c:cbb + c + 1]
                for tb in range(NTB):
                    tsl = slice(tb * TB, (tb + 1) * TB)
                    pa = self.ps[(it % 2) * 2]
                    pg = self.ps[(it % 2) * 2 + 1]
                    r_pa = self.r_ps[(it % 2) * 2]
                    r_pg = self.r_ps[(it % 2) * 2 + 1]
                    tbuf = it % 2
                    it += 1

                    def mm_a(e, wv=wv, pa=pa, tsl=tsl):
                        ins = None
                        for k in range(8):
                            ins = e.matmul(pa[:, :], lhsT=wv[:, k, 0:128], rhs=self.hT[:, k, tsl], start=(k == 0), stop=(k == 7))
                        return ins

                    def mm_g(e, wv=wv, pg=pg, tsl=tsl):
                        ins = None
                        for k in range(8):
                            ins = e.matmul(pg[:, :], lhsT=wv[:, k, 128:256], rhs=self.hT[:, k, tsl], start=(k == 0), stop=(k == 7))
                        return ins
                    P.op("pe", mm_a, reads=[r_w[sl], self.r_h[tb]], writes=[r_pa])
                    P.op("pe", mm_g, reads=[r_w[sl], self.r_h[tb]], writes=[r_pg])
                    af = a_full[ab]
                    o0 = 2 + tb * TB
                    P.op("act", lambda e, o=af[:, o0:o0 + TB], pa=pa: e.activation(out=o, in_=pa[:, :], func=AF.Copy),
                         reads=[r_pa], writes=[r_a[ab][tb]])
                    P.op("act", lambda e, o=tt[tbuf], pa=pa, w2=w2, bia=bia: e.activation(
                        out=o, in_=pa[:, :], func=AF.Identity, scale=w2, bias=bia),
                        reads=[r_pa, self.r_const], writes=[r_t[tbuf]])
                    rd = [r_a[ab][tb], r_az, self.r_const, r_t[tbuf]]
                    if tb > 0:
                        rd.append(r_a[ab][tb - 1])
                    P.op("dve", lambda e, o=tt[tbuf], a1=af[:, o0 - 1:o0 - 1 + TB], w1=w1: e.scalar_tensor_tensor(
                        out=o, in0=a1, scalar=w1, in1=o, op0=ALU.mult, op1=ALU.add), reads=rd, writes=[r_t[tbuf]])
                    P.op("dve", lambda e, o=tt[tbuf], a0=af[:, o0 - 2:o0 - 2 + TB], w0=w0: e.scalar_tensor_tensor(
                        out=o, in0=a0, scalar=w0, in1=o, op0=ALU.mult, op1=ALU.add), reads=rd, writes=[r_t[tbuf]])
                    P.op("act", lambda e, o=ge[tbuf], i=tt[tbuf]: e.activation(out=o, in_=i, func=AF.Gelu),
                         reads=[r_t[tbuf]], writes=[r_g[tbuf]])
                    P.op("dve", lambda e, o=u[:, cl, tsl], g=ge[tbuf], pg=pg: e.tensor_tensor(out=o, in0=g, in1=pg[:, :], op=ALU.mult),
                         reads=[r_g[tbuf], r_pg], writes=[r_u[cl][tb]])
            for d in range(8):
                sl = wi % NWS
                wi += 1
                wv = wsl[sl][:, 0:nch * 128].rearrange("p (c n) -> p c n", c=nch)
                P.dma("pool", "w%d" % sl, wsl[sl][:, 0:nch * 128], self.d_fdn[l, half, d].rearrange("p c n -> p (c n)"),
                      writes=[r_w[sl]])
                gf = self.mod_ap(l, 5, d, s)
                for tb in range(NTB):
                    tsl = slice(tb * TB, (tb + 1) * TB)
                    po = self.ps[4 + it % 2]
                    r_po = self.r_ps[4 + it % 2]
                    it += 1

                    def mm_d(e, wv=wv, po=po, tsl=tsl, nch=nch):
                        ins = None
                        for cc in range(nch):
                            ins = e.matmul(po[:, :], lhsT=wv[:, cc, :], rhs=u[:, cc, tsl], start=(cc == 0), stop=(cc == nch - 1))
                        return ins
                    P.op("pe", mm_d, reads=[r_w[sl]] + [r_u[cc][tb] for cc in range(nch)], writes=[r_po])
                    xo = self.xT[:, d, tsl]
                    P.op("dve", lambda e, xo=xo, po=po, gf=gf: e.scalar_tensor_tensor(
                        out=xo, in0=po[:, :], scalar=gf, in1=xo, op0=ALU.mult, op1=ALU.add),
                        reads=[r_po, self.r_mod], writes=[self.r_x[d][tb]])
            c_glob += nch


    def mla(self, s, l):
        P = self.P
        a = l // 2
        SCALE = 192.0 ** -0.5
        tab = self.tab[:, :]
        cosT = tab[0:64, :]
        sinT = tab[64:128, :]
        cqn = self.ar_bf16(8, [128, 4, S])
        ckvn = self.ar_bf16(24, [128, 2, S])
        krT = self.ar_bf16(32, [128, S])
        knT2 = [self.ar_bf16(36, [128, S]), self.ar_bf16(40, [128, S])]
        Vh2 = [self.ar_bf16(44, [128, 16, 128]), self.ar_bf16(48, [128, 16, 128])]
        qnT2 = [self.ar_bf16(52, [128, 512]), self.ar_bf16(53, [128, 512])]
        qrT2 = [self.ar_bf16(54, [128, 512]), self.ar_bf16(55, [128, 512])]
        PT = [self.ar_bf16(56, [128, 512]), self.ar_bf16(57, [128, 512]), self.ar_bf16(4, [128, 512])]
        scr = [self.ar_f32(58 + 2 * i, [128, 512]) for i in range(4)]
        rinv = self.ar_f32(66, [128, 512])
        sqb = [self.ar_bf16(0, [128, 512]), self.ar_bf16(1, [128, 512])]
        small = self.ar_f32(5, [128, 64])
        r_cos = self.r_tab; r_sin = self.r_tab
        r_cqn = [Res("cqn%d" % t) for t in range(NTB)]
        r_ckvn = [Res("ckvn%d" % t) for t in range(NTB)]
        r_kr = [Res("kr%d" % t) for t in range(NTB)]
        r_scr = [Res("scr%d" % i) for i in range(4)]
        r_sq = [Res("sqA"), Res("sqB")]
        r_small = Res("small")
        rc = self.r_const
        w_in = self.WB[:, 0:8 * 896].rearrange("p (k n) -> p k n", k=8)
        r_wbig = Res("wbig")
        for k2 in range(4):
            P.dma("pool", "wbig", self.WB[:, k2 * 1792:(k2 + 1) * 1792], self.d_win[a, :, k2 * 1792:(k2 + 1) * 1792],
                  cowrites=[r_wbig])
        if l == 0:
            pos_i = self.ar_f32(36, [64, S]).bitcast(I32)
            tf = self.ar_f32(44, [64, S])
            tg = self.ar_f32(52, [64, S])
            r_pi = Res("posi"); r_tf = Res("tf"); r_tg = Res("tg")
            P.dma("sp", "pos", pos_i, self.d_pos[s:s + 1, :].broadcast_to([64, S]), writes=[r_pi])
            invf = self.pp[0:64, PP_IF:PP_IF + 1]
            TWO_PI = 2.0 * np.pi
            c1 = 6.28125
            rem = TWO_PI - c1
            c2 = float(np.frombuffer(np.array([np.frombuffer(np.float32(rem).tobytes(), np.uint32)[0] & 0xFFFFF000], np.uint32).tobytes(), np.float32)[0])
            c3 = float(np.float32(rem - c2))
            P.op("dve", lambda e: e.tensor_copy(out=tf, in_=pos_i), reads=[r_pi], writes=[r_tf])
            P.op("dve", lambda e: e.tensor_scalar(out=tf, in0=tf, scalar1=invf, scalar2=None, op0=ALU.mult),
                 reads=[r_tf, rc], writes=[r_tf])
            P.op("dve", lambda e: e.tensor_scalar(out=tg, in0=tf, scalar1=float(1.0 / TWO_PI), scalar2=None, op0=ALU.mult),
                 reads=[r_tf], writes=[r_tg])
            P.op("dve", lambda e: e.tensor_copy(out=pos_i, in_=tg), reads=[r_tg], writes=[r_pi])
            P.op("dve", lambda e: e.tensor_copy(out=tg, in_=pos_i), reads=[r_pi], writes=[r_tg])
            for cc in (c1, c2, c3):
                P.op("dve", lambda e, cc=cc: e.scalar_tensor_tensor(out=tf, in0=tg, scalar=-float(cc), in1=tf, op0=ALU.mult, op1=ALU.add),
                     reads=[r_tf, r_tg], writes=[r_tf])
            PI = float(np.pi)
            th = self.ar_f32(60, [64, S])
            r_th = Res("th")
            for (shift, dstT, r_d) in ((0.0, sinT, r_sin), (PI / 2, cosT, r_cos)):
                P.op("dve", lambda e, shift=shift: e.tensor_scalar(out=tg, in0=tf, scalar1=float(shift), scalar2=None, op0=ALU.add),
                     reads=[r_tf], writes=[r_tg])
                P.op("dve", lambda e: e.tensor_scalar(out=th, in0=tg, scalar1=-PI, scalar2=TWO_PI, op0=ALU.is_lt, op1=ALU.mult),
                     reads=[r_tg], writes=[r_th])
                P.op("dve", lambda e: e.tensor_tensor(out=tg, in0=tg, in1=th, op=ALU.add), reads=[r_tg, r_th], writes=[r_tg])
                P.op("dve", lambda e: e.tensor_scalar(out=th, in0=tg, scalar1=PI, scalar2=-TWO_PI, op0=ALU.is_gt, op1=ALU.mult),
                     reads=[r_tg], writes=[r_th])
                P.op("dve", lambda e: e.tensor_tensor(out=tg, in0=tg, in1=th, op=ALU.add), reads=[r_tg, r_th], writes=[r_tg])
                P.op("act", lambda e, dstT=dstT: e.activation(out=dstT, in_=tg, func=AF.Sin), reads=[r_tg], writes=[r_d])
            P.op("dve", lambda e: e.tensor_scalar(out=tab[64:96, :], in0=tab[64:96, :], scalar1=-1.0, scalar2=None, op0=ALU.mult),
                 reads=[r_sin], writes=[r_sin])
        P.barrier()
        r_zero = Res("zero")
        P.op("dve", lambda e: e.memset(krT[64:128, :], 0.0), cowrites=[r_zero])
        for qq in range(2):
            P.op("dve", lambda e, qq=qq: e.memset(qrT2[qq][64:128, :], 0.0), cowrites=[r_zero])

        ones_r = self.ones_b
        sqr = [self.sqr[:, 0, :], self.sqr[:, 1, :]]
        it = 0
        sqi = 0

        rope_i = [0]

        def rope(dst, tsl, nm):
            b2 = (rope_i[0] % 2) * 2
            rope_i[0] += 1
            s0 = scr[b2][0:64, :]
            s1 = scr[b2 + 1][0:64, :]
            P.op("dve", lambda e: e.tensor_tensor(out=s0, in0=self.ps[6][0:64, :], in1=cosT[:, tsl], op=ALU.mult),
                 reads=[self.r_ps[6], r_cos], writes=[r_scr[b2]])
            P.op("dve", lambda e: e.tensor_tensor(out=s1, in0=self.ps[6][64:128, :], in1=sinT[:, tsl], op=ALU.mult),
                 reads=[self.r_ps[6], r_sin], writes=[r_scr[b2 + 1]])
            P.op("dve", lambda e: e.tensor_tensor(out=dst, in0=s0, in1=s1, op=ALU.add),
                 reads=[r_scr[b2], r_scr[b2 + 1], r_zero], writes=[nm])

        for tb in range(NTB):
            tsl = slice(tb * TB, (tb + 1) * TB)
            for (nch, col0, dst, r_dst, gcol) in ((4, 0, cqn, r_cqn, PP_QN + a * 4), (2, 512, ckvn, r_ckvn, PP_KVN + a * 2)):
                pss = self.ps[4 + it % 2]
                r_pss = self.r_ps[4 + it % 2]
                it += 1
                for j in range(nch):
                    pj = self.ps[j]
                    r_pj = self.r_ps[j]

                    def mm(e, pj=pj, c0=col0 + j * 128, tsl=tsl):
                        ins = None
                        for k in range(8):
                            ins = e.matmul(pj[:, :], lhsT=w_in[:, k, c0:c0 + 128], rhs=self.hT[:, k, tsl], start=(k == 0), stop=(k == 7))
                        return ins
                    P.op("pe", mm, reads=[r_wbig, self.r_h[tb]], writes=[r_pj])
                    P.op("act", lambda e, o=scr[j], pj=pj: e.activation(out=o, in_=pj[:, :], func=AF.Copy),
                         reads=[r_pj], writes=[r_scr[j]])
                    b = sqi % 2
                    sqi += 1
                    P.op("act", lambda e, o=sqr[b], i=scr[j]: e.activation(out=o, in_=i, func=AF.Square),
                         reads=[r_scr[j]], writes=[r_sq[b]])
                    kw = dict(reads=[r_sq[b], rc])
                    if j == 0:
                        kw["writes"] = [r_pss]
                    else:
                        kw["cowrites"] = [r_pss]
                    P.op("pe", lambda e, pss=pss, i=sqr[b], j=j, nch=nch: e.matmul(pss[:, :], lhsT=ones_r, rhs=i, start=(j == 0), stop=(j == nch - 1)), **kw)
                P.op("act", lambda e, pss=pss, nch=nch: e.activation(out=rinv, in_=pss[:, :], func=AF.Ln, scale=1.0 / (nch * 128), bias=EPS),
                     reads=[r_pss], writes=[r_small])
                P.op("act", lambda e: e.activation(out=rinv, in_=rinv, func=AF.Exp, scale=-0.5), reads=[r_small], writes=[r_small])
                for j in range(nch):
                    g = self.pp[:, gcol + j:gcol + j + 1]
                    kw = dict(reads=[r_scr[j], r_small, rc])
                    if j == 0:
                        kw["writes"] = [r_dst[tb]]
                    else:
                        kw["cowrites"] = [r_dst[tb]]
                    P.op("dve", lambda e, o=dst[:, j, tsl], i=scr[j], g=g: e.scalar_tensor_tensor(
                        out=o, in0=i, scalar=g, in1=rinv, op0=ALU.mult, op1=ALU.mult), **kw)
            def mmk(e, tsl=tsl):
                ins = None
                for k in range(8):
                    ins = e.matmul(self.ps[6][:, :], lhsT=w_in[:, k, 768:896], rhs=self.hT[:, k, tsl], start=(k == 0), stop=(k == 7))
                return ins
            P.op("pe", mmk, reads=[r_wbig, self.r_h[tb]], writes=[self.r_ps[6]])
            rope(krT[0:64, tsl], tsl, r_kr[tb])
        P.barrier()
        if s == 0 and l == 0:
            self.dump("cos", cosT, [64, S], BF16)
            self.dump("sin", sinT, [64, S], BF16)
            self.dump("cqn", cqn, [128, 4, S], BF16)
            self.dump("ckvn", ckvn, [128, 2, S], BF16)
            self.dump("krT", krT[0:64, :], [64, S], BF16)

        wo = self.WB[:, 0:8192].rearrange("p (h n) -> p h n", h=8)
        r_wo = Res("wo")
        for h2 in range(4):
            P.dma("pool", "wbig", self.WB[:, h2 * 2048:(h2 + 1) * 2048], self.d_wo[a, :, h2 * 2048:(h2 + 1) * 2048], cowrites=[r_wo])

        r_qn = [Res("qn0"), Res("qn1")]
        r_qr = [Res("qr0"), Res("qr1")]
        r_kn = [[Res("kn%d_%d" % (b, t)) for t in range(NTB)] for b in range(2)]
        r_v = [[Res("v%d_%d" % (b, t)) for t in range(NTB)] for b in range(2)]
        r_pt = [Res("pt0"), Res("pt1"), Res("pt2")]
        r_rinv = Res("rinv")
        r_o = [[Res("o%d_%d" % (h, t)) for t in range(NTB)] for h in range(8)]
        r_hw = [Res("hw0"), Res("hw1")]
        r_mk = Res("mk")
        r_mq = Res("mq")
        mrow128 = self.cb[:, CB_MROW:CB_MROW + 128]

        def hw_views(hb):
            wq = self.WB[:, 8192 + hb * 1536:8192 + hb * 1536 + 1024].rearrange("p (k n) -> p k n", k=4)
            wkv = self.WB[:, 8192 + hb * 1536 + 1024:8192 + hb * 1536 + 1536].rearrange("p (k n) -> p k n", k=2)
            return wq, wkv

        def load_hw(h):
            hb = h % 2
            P.dma("pool", "hw%d" % hb, self.WB[:, 8192 + hb * 1536:8192 + hb * 1536 + 1024], self.d_wq[a, h], writes=[r_hw[hb]])
            P.dma("pool", "hw%d" % hb, self.WB[:, 8192 + hb * 1536 + 1024:8192 + hb * 1536 + 1536], self.d_wkv[a, h], cowrites=[r_hw[hb]])

        def sumsq_max(srcA, rA, srcB, rB, outcol, r_out):
            P.op("dve", lambda e: e.tensor_tensor(out=sqb[0], in0=srcA, in1=srcA, op=ALU.mult), reads=[rA], writes=[r_sq[0]])
            P.op("dve", lambda e: e.tensor_tensor(out=sqb[1], in0=srcB, in1=srcB, op=ALU.mult), reads=[rB, r_zero], writes=[r_sq[1]])

            def mms(e):
                e.matmul(self.ps[7][:, :], lhsT=self.ones_b, rhs=sqb[0], start=True, stop=False)
                return e.matmul(self.ps[7][:, :], lhsT=self.ones_b, rhs=sqb[1], start=False, stop=True)
            P.op("pe", mms, reads=[r_sq[0], r_sq[1], rc], writes=[self.r_ps[7]])
            P.op("dve", lambda e: e.reduce_max(out=outcol, in_=self.ps[7][:, :], axis=AX.X), reads=[self.r_ps[7]], cowrites=[r_out])

        def kprep(h, tbs):
            hb = h % 2
            wq, wkv = hw_views(hb)
            knT = knT2[hb]
            Vh = Vh2[hb]
            for tb in tbs:
                tsl = slice(tb * TB, (tb + 1) * TB)

                def mmk2(e, tsl=tsl, wkv=wkv):
                    ins = None
                    for k in range(2):
                        ins = e.matmul(self.ps[6][:, :], lhsT=wkv[:, k, 0:128], rhs=ckvn[:, k, tsl], start=(k == 0), stop=(k == 1))
                    return ins
                P.op("pe", mmk2, reads=[r_hw[hb], r_ckvn[tb]], writes=[self.r_ps[6]])
                P.op("act", lambda e, o=knT[:, tsl]: e.activation(out=o, in_=self.ps[6][:, :], func=AF.Copy),
                     reads=[self.r_ps[6]], writes=[r_kn[hb][tb]])

                def mmv(e, tb=tb, wkv=wkv):
                    ins = None
                    for t4 in range(4):
                        t0 = tb * TB + t4 * 128
                        for k in range(2):
                            ins = e.matmul(self.ps[7][:, t4 * 128:(t4 + 1) * 128], lhsT=ckvn[:, k, t0:t0 + 128], rhs=wkv[:, k, 128:256],
                                           start=(k == 0), stop=(k == 1))
                    return ins
                P.op("pe", mmv, reads=[r_hw[hb], r_ckvn[tb]], writes=[self.r_ps[7]])
                P.op("dve", lambda e, o=Vh[:, tb * 4:(tb + 1) * 4, :]: e.tensor_copy(
                    out=o, in_=self.ps[7][:, :].rearrange("p (a b) -> p a b", a=4)),
                    reads=[self.r_ps[7]], writes=[r_v[hb][tb]])
                sumsq_max(knT[:, tsl], r_kn[hb][tb], krT[:, tsl], r_kr[tb], small[:, hb * 4 + tb:hb * 4 + tb + 1], r_mk)
            if tbs[-1] == NTB - 1:
                P.op("dve", lambda e: e.reduce_max(out=small[:, 8 + hb:9 + hb], in_=small[:, hb * 4:hb * 4 + 4], axis=AX.X),
                     reads=[r_mk], cowrites=[r_mk])

        qcnt = [0]

        def qprep(h, qb):
            hb = h % 2
            wq, wkv = hw_views(hb)
            qi = qcnt[0] % 2
            qcnt[0] += 1
            tsl = slice(qb * TB, (qb + 1) * TB)
            qnT = qnT2[qi]
            qrT = qrT2[qi]

            def mmq(e, wq=wq):
                ins = None
                for k in range(4):
                    ins = e.matmul(self.ps[6][:, :], lhsT=wq[:, k, 0:128], rhs=cqn[:, k, tsl], start=(k == 0), stop=(k == 3))
                return ins
            P.op("pe", mmq, reads=[r_hw[hb], r_cqn[qb]], writes=[self.r_ps[6]])
            P.op("act", lambda e: e.activation(out=qnT, in_=self.ps[6][:, :], func=AF.Copy), reads=[self.r_ps[6]], writes=[r_qn[qi]])

            def mmqr(e, wq=wq):
                ins = None
                for k in range(4):
                    ins = e.matmul(self.ps[6][:, :], lhsT=wq[:, k, 128:256], rhs=cqn[:, k, tsl], start=(k == 0), stop=(k == 3))
                return ins
            P.op("pe", mmqr, reads=[r_hw[hb], r_cqn[qb]], writes=[self.r_ps[6]])
            rope(qrT[0:64, :], tsl, r_qr[qi])
            mq = small[:, 10 + qi:11 + qi]
            sumsq_max(qnT, r_qn[qi], qrT, r_qr[qi], mq, r_mq)
            negc = small[:, 16 + h * 4 + qb:17 + h * 4 + qb]
            P.op("dve", lambda e: e.tensor_scalar(out=negc, in0=mq, scalar1=small[:, 8 + hb:9 + hb], scalar2=-0.5 * SCALE,
                                                  op0=ALU.add, op1=ALU.mult), reads=[r_mq, r_mk], cowrites=[r_mq])
            return qi, negc

        def blk_ctx(h, qb, qi, negc):
            hb = h % 2
            return dict(h=h, qb=qb, hb=hb, knT=knT2[hb], Vh=Vh2[hb], qnT=qnT2[qi], qrT=qrT2[qi], qi=qi, negc=negc,
                        po=self.ps[3 + qb % 2], r_po=self.r_ps[3 + qb % 2], psm=self.ps[5], r_psm=self.r_ps[5],
                        nj=4 * qb + 4)

        def step_bufs(k):
            return self.ps[k % 3], self.r_ps[k % 3], PT[k % 3], r_pt[k % 3]

        def emit_mms(k, cx, j):
            pst, r_pst, ptb, r_ptb = step_bufs(k)
            r = j - 4 * cx["qb"]
            c0 = 128 * max(r, 0)
            ksl = slice(j * 128, (j + 1) * 128)
            knT, qnT, qrT = cx["knT"], cx["qnT"], cx["qrT"]

            def mms(e):
                e.matmul(pst[:, c0:512], lhsT=knT[:, ksl], rhs=qnT[:, c0:512], start=True, stop=False)
                ins = e.matmul(pst[:, c0:512], lhsT=krT[:, ksl], rhs=qrT[:, c0:512], start=False, stop=(r < 0))
                if r >= 0:
                    ins = e.matmul(pst[:, c0:c0 + 64], lhsT=mrow128, rhs=self.ones_b[:, 0:64], start=False, stop=True)
                return ins
            tbk = j // 4
            P.op("pe", mms, reads=[r_kn[cx["hb"]][tbk], r_kr[tbk], r_qn[cx["qi"]], r_qr[cx["qi"]], rc, r_zero], writes=[r_pst])

        def emit_exp_mmo(k, cx, j):
            pst, r_pst, ptb, r_ptb = step_bufs(k)
            r = j - 4 * cx["qb"]
            c0 = 128 * max(r, 0)
            negc, Vh, po, psm, nj = cx["negc"], cx["Vh"], cx["po"], cx["psm"], cx["nj"]
            P.op("act", lambda e: e.activation(out=ptb[:, c0:512], in_=pst[:, c0:512], func=AF.Exp, scale=SCALE, bias=negc),
                 reads=[r_pst, r_mq], writes=[r_ptb])

            def mmo(e):
                e.matmul(po[:, c0:512], lhsT=Vh[:, j, :], rhs=ptb[:, c0:512], start=(j == 0), stop=(j == nj - 1))
                return e.matmul(psm[:, c0:512], lhsT=self.ones_b, rhs=ptb[:, c0:512], start=(j == 0), stop=(j == nj - 1))
            tbk = j // 4
            kw = dict(reads=[r_v[cx["hb"]][tbk], r_ptb, rc])
            if j == 0:
                kw["writes"] = [cx["r_po"], cx["r_psm"]]
            else:
                kw["cowrites"] = [cx["r_po"], cx["r_psm"]]
            P.op("pe", mmo, **kw)
            if j == nj - 1:
                h, qb = cx["h"], cx["qb"]
                P.op("act", lambda e: e.activation(out=rinv, in_=psm[:, :], func=AF.Ln), reads=[cx["r_psm"]], writes=[r_rinv])
                P.op("act", lambda e: e.activation(out=rinv, in_=rinv, func=AF.Exp, scale=-1.0), reads=[r_rinv], writes=[r_rinv])
                P.op("dve", lambda e: e.tensor_tensor(out=self.hT[:, h, qb * TB:(qb + 1) * TB], in0=po[:, :], in1=rinv, op=ALU.mult),
                     reads=[cx["r_po"], r_rinv], writes=[r_o[h][qb]])

        load_hw(0)
        load_hw(1)
        kprep(0, [0, 1, 2, 3])
        ctxs = {}
        qi0, negc0 = qprep(0, 0)
        ctxs[(0, 0)] = blk_ctx(0, 0, qi0, negc0)
        steps = [(h, qb, j) for h in range(8) for qb in range(4) for j in range(4 * qb + 4)]
        LOOK = 2
        emit_mms(0, ctxs[(0, 0)], 0)
        emit_mms(1, ctxs[(0, 0)], 1)
        for k, (h, qb, j) in enumerate(steps):
            if j == 0:
                if qb < 3:
                    qi_, negc_ = qprep(h, qb + 1)
                    ctxs[(h, qb + 1)] = blk_ctx(h, qb + 1, qi_, negc_)
                if h < 7:
                    if qb == 1:
                        kprep(h + 1, [0, 1])
                    elif qb == 2:
                        kprep(h + 1, [2, 3])
                    elif qb == 3:
                        qi_, negc_ = qprep(h + 1, 0)
                        ctxs[(h + 1, 0)] = blk_ctx(h + 1, 0, qi_, negc_)
                if qb == 3 and h + 2 < 8:
                    load_hw(h + 2)
            if k + LOOK < len(steps):
                h2, qb2, j2 = steps[k + LOOK]
                emit_mms(k + LOOK, ctxs[(h2, qb2)], j2)
            emit_exp_mmo(k, ctxs[(h, qb)], j)
        it = 0
        for d in range(8):
            ga = self.mod_ap(l, 2, d, s)
            for tb in range(NTB):
                tsl = slice(tb * TB, (tb + 1) * TB)
                po = self.ps[6 + it % 2]
                r_po = self.r_ps[6 + it % 2]
                it += 1

                def mmd(e, po=po, d=d, tsl=tsl):
                    ins = None
                    for h in range(8):
                        ins = e.matmul(po[:, :], lhsT=wo[:, h, d * 128:(d + 1) * 128], rhs=self.hT[:, h, tsl], start=(h == 0), stop=(h == 7))
                    return ins
                P.op("pe", mmd, reads=[r_wo] + [r_o[h][tb] for h in range(8)], writes=[r_po])
                xo = self.xT[:, d, tsl]
                P.op("dve", lambda e, xo=xo, po=po, ga=ga: e.scalar_tensor_tensor(
                    out=xo, in0=po[:, :], scalar=ga, in1=xo, op0=ALU.mult, op1=ALU.add),
                    reads=[r_po, self.r_mod], writes=[self.r_x[d][tb]])


    def mlstm(self, s, l):
        P = self.P
        b = l // 2
        rc = self.r_const
        LNS = float(np.log(128.0 ** -0.5))
        self.areset()
        G1 = self.af([128, S])
        G2 = self.af([128, S])
        qpT = self.ab([128, S])
        kpT = self.ab([128, S])
        yT = self.ab([128, 8, S])
        tokS = self.af([128, 16, 8])
        dec_b = self.af([128, 64])
        hn_h = self.af([128, 256])
        small2 = self.af([128, 64])
        kws = [self.ab([128, 128]) for _ in range(2)]
        vaug = [self.ab([128, 264]) for _ in range(2)]
        eo = [self.af([128, 256]) for _ in range(2)]
        smT = [self.ab([128, 128]) for _ in range(2)]
        Cf = self.af([128, 264])
        Cb = [self.ab([128, 264]) for _ in range(2)]
        yh = [self.ab([128, 256]) for _ in range(2)]
        qs = [self.af([128, 512]) for _ in range(2)]
        junk = qs[0][:, 0:256]
        tri = self.cf[:, CF_TRI:CF_TRI + 128]

        def slot(G, j):
            return G[32 * j:32 * j + 4, :]
        A = [slot(G1, j) for j in range(4)]
        B = [slot(G2, j) for j in range(4)]
        r_A = [Res("A%d" % j) for j in range(4)]
        r_B = [Res("B%d" % j) for j in range(4)]
        wg = [self.WB[:, 8192 + g * 1024:8192 + (g + 1) * 1024].rearrange("p (k n) -> p k n", k=8) for g in range(2)]
        r_wg = Res("wg")
        for g in range(2):
            P.dma("pool", "wg", self.WB[:, 8192 + g * 1024:8192 + (g + 1) * 1024], self.d_mlg[b, g], cowrites=[r_wg])
        r_gz = Res("gz")
        P.op("dve", lambda e: e.memset(G1, 0.0), writes=[r_gz])
        P.op("dve", lambda e: e.memset(G2, 0.0), cowrites=[r_gz])
        r_wh = Res("wh")
        Wh = self.WB[:, 0:7168].rearrange("p (k n) -> p k n", k=8)

        def load_head(h):
            for k2 in range(4):
                kw = dict(writes=[r_wh]) if k2 == 0 else dict(cowrites=[r_wh])
                P.dma("pool", "wh", self.WB[:, k2 * 1792:(k2 + 1) * 1792], self.d_mlw[b, h, :, k2 * 1792:(k2 + 1) * 1792], **kw)
        load_head(0)
        ib = self.pp[0:4, PP_GB + 2 * b:PP_GB + 2 * b + 1]
        fb = self.pp[0:4, PP_GB + 2 * b + 1:PP_GB + 2 * b + 2]
        for tb in range(NTB):
            tsl = slice(tb * TB, (tb + 1) * TB)
            pi = self.ps[(tb % 2) * 2]
            pf = self.ps[(tb % 2) * 2 + 1]
            r_pi = self.r_ps[(tb % 2) * 2]
            r_pf = self.r_ps[(tb % 2) * 2 + 1]

            def mmg(e, pp_=pi, g=0, tsl=tsl):
                ins = None
                for k in range(8):
                    ins = e.matmul(pp_[:, :], lhsT=wg[g][:, k, :], rhs=self.hT[:, k, tsl], start=(k == 0), stop=(k == 7))
                return ins
            P.op("pe", mmg, reads=[r_wg, self.r_h[tb]], writes=[r_pi])
            P.op("pe", lambda e, f=mmg, pf=pf, tsl=tsl: f(e, pf, 1, tsl), reads=[r_wg, self.r_h[tb]], writes=[r_pf])
            kw = dict(writes=[r_A[0]]) if tb == 0 else dict(cowrites=[r_A[0]])
            P.op("act", lambda e, pi=pi, tsl=tsl: e.activation(out=A[0][:, tsl], in_=pi[0:4, :], func=AF.Identity, bias=ib),
                 reads=[r_pi, rc, r_gz], **kw)
            kw = dict(writes=[r_A[1]]) if tb == 0 else dict(cowrites=[r_A[1]])
            P.op("act", lambda e, pf=pf, tsl=tsl: e.activation(out=A[1][:, tsl], in_=pf[0:4, :], func=AF.Identity, bias=fb),
                 reads=[r_pf, rc, r_gz], **kw)
        P.op("act", lambda e: e.activation(out=A[1], in_=A[1], func=AF.Exp, scale=-1.0), reads=[r_A[1]], writes=[r_A[1]])
        P.op("act", lambda e: e.activation(out=A[1], in_=A[1], func=AF.Ln, bias=1.0), reads=[r_A[1]], writes=[r_A[1]])
        P.op("dve", lambda e: e.tensor_tensor_scan(out=B[0], data0=A[1], data1=A[1], initial=0.0, op0=ALU.add, op1=ALU.max),
             reads=[r_A[1]], writes=[r_B[0]])
        P.op("dve", lambda e: e.tensor_tensor(out=B[1], in0=A[0], in1=B[0], op=ALU.add), reads=[r_A[0], r_B[0]], writes=[r_B[1]])
        P.op("dve", lambda e: e.tensor_tensor_scan(out=A[1], data0=B[1], data1=B[1], initial=0.0, op0=ALU.max, op1=ALU.max),
             reads=[r_B[1]], writes=[r_A[1]])
        P.op("dve", lambda e: e.tensor_tensor_scan(out=A[0], data0=B[1], data1=B[1], initial=0.0, op0=ALU.max, op1=ALU.max),
             reads=[r_B[1]], writes=[r_A[0]])

        def v3(ap):
            return ap.rearrange("p (c n) -> p c n", c=16)
        Mxa3 = v3(A[1]); Mxb3 = v3(A[0]); a3 = v3(B[1]); Gn3 = v3(B[0])

        def ref_of(M3):
            return M3[:, 0:15, 127:128].broadcast_to([4, 15, 128])

        def end_of(M3):
            return M3[:, :, 127:128].broadcast_to([4, 16, 128])
        o3 = v3(A[2])
        P.op("dve", lambda e: e.tensor_tensor(out=o3[:, 1:16, :], in0=ref_of(Mxb3), in1=Mxb3[:, 1:16, :], op=ALU.subtract),
             reads=[r_A[0]], writes=[r_A[2]])
        P.op("dve", lambda e: e.tensor_scalar(out=o3[:, 0:1, :], in0=Mxb3[:, 0:1, :], scalar1=-1.0, scalar2=None, op0=ALU.mult),
             reads=[r_A[0]], cowrites=[r_A[2]])
        P.op("act", lambda e: e.activation(out=A[2], in_=A[2], func=AF.Exp), reads=[r_A[2]], writes=[r_A[2]])
        o3b = v3(B[2])
        P.op("dve", lambda e: e.tensor_tensor(out=o3b[:, 1:16, :], in0=a3[:, 1:16, :], in1=ref_of(Mxa3), op=ALU.subtract),
             reads=[r_B[1], r_A[1]], writes=[r_B[2]])
        P.op("dve", lambda e: e.tensor_copy(out=o3b[:, 0:1, :], in_=a3[:, 0:1, :]), reads=[r_B[1]], cowrites=[r_B[2]])
        P.op("act", lambda e: e.activation(out=B[2], in_=B[2], func=AF.Exp, bias=LNS), reads=[r_B[2]], writes=[r_B[2]])
        P.op("dve", lambda e: e.tensor_tensor(out=v3(A[3]), in0=a3, in1=end_of(Mxa3), op=ALU.subtract),
             reads=[r_B[1], r_A[1]], writes=[r_A[3]])
        P.op("act", lambda e: e.activation(out=A[3], in_=A[3], func=AF.Exp, bias=LNS), reads=[r_A[3]], writes=[r_A[3]])
        P.op("dve", lambda e: e.tensor_tensor(out=B[3], in0=B[0], in1=A[0], op=ALU.subtract), reads=[r_B[0], r_A[0]], writes=[r_B[3]])
        P.op("act", lambda e: e.activation(out=B[3], in_=B[3], func=AF.Exp, scale=2.0), reads=[r_B[3]], writes=[r_B[3]])
        r_s2 = Res("small2")
        dr = small2[0:4, 0:16]
        P.op("dve", lambda e: e.memset(small2[:, 0:16], 0.0), writes=[r_s2])
        P.op("dve", lambda e: e.tensor_tensor(out=small2[0:4, 1:16].unsqueeze(2), in0=Mxb3[:, 0:15, 127:128], in1=Mxb3[:, 1:16, 127:128],
                                              op=ALU.subtract), reads=[r_A[0]], cowrites=[r_s2])
        P.op("dve", lambda e: e.tensor_scalar(out=small2[0:4, 0:1].unsqueeze(2), in0=Mxb3[:, 0:1, 127:128], scalar1=-1.0, scalar2=None,
                                              op0=ALU.mult), reads=[r_A[0]], cowrites=[r_s2])
        P.op("act", lambda e: e.activation(out=dr, in_=dr, func=AF.Exp), reads=[r_s2], writes=[r_s2])
        r_dec = Res("dec")
        for h in range(4):
            sel0 = self.cf[:, CF_SELA + h * 128:CF_SELA + (h + 1) * 128]
            P.op("pe", lambda e, sel0=sel0: e.matmul(self.ps[4][:, 0:16], lhsT=sel0, rhs=small2[:, 0:16], start=True, stop=True),
                 reads=[r_s2, rc], writes=[self.r_ps[4]])
            kw = dict(writes=[r_dec]) if h == 0 else dict(cowrites=[r_dec])
            P.op("act", lambda e, h=h: e.activation(out=dec_b[:, h * 16:(h + 1) * 16], in_=self.ps[4][:, 0:16], func=AF.Copy),
                 reads=[self.r_ps[4]], **kw)
        r_tok = Res("tok")
        for c in range(16):
            csl = slice(c * 128, (c + 1) * 128)
            for gi, (G, rG, col) in enumerate(((G1, r_A, 0), (G2, r_B, 4))):
                bk = (2 * c + gi) % 8
                P.op("pe", lambda e, G=G, csl=csl, bk=bk: e.transpose(out=self.ps[bk][:, 0:128], in_=G[:, csl], identity=self.ident_f),
                     reads=[rG[0], rG[1], rG[2], rG[3], rc], writes=[self.r_ps[bk]])
                P.op("act", lambda e, c=c, col=col, bk=bk: e.activation(out=tokS[:, c, col:col + 4], in_=self.ps[bk][:, 96:100], func=AF.Copy),
                     reads=[self.r_ps[bk]], cowrites=[r_tok])
        if s == 0 and l == 1 and self.dbg:
            self.dump("G1", G1, [128, S], F32)
            self.dump("G2", G2, [128, S], F32)
            self.dump("tokS", tokS, [128, 16, 8], F32)
            self.dump("decb", dec_b, [128, 64], F32)
        r_qp = [Res("qp%d" % t) for t in range(NTB)]
        r_kp = [Res("kp%d" % t) for t in range(NTB)]
        r_qs = [Res("qs0"), Res("qs1")]
        r_hn = Res("hn")
        r_kws = [Res("kws0"), Res("kws1")]
        r_va = [Res("va0"), Res("va1")]
        r_eo = [Res("eo0"), Res("eo1")]
        r_sm = [Res("sm0"), Res("sm1")]
        r_Cf = Res("Cf")
        r_Cb = [Res("Cb0"), Res("Cb1")]
        r_yh = [Res("yh0"), Res("yh1")]
        r_junk = r_qs[0]
        r_yT = [[Res("yT%d_%d" % (j, t)) for t in range(NTB)] for j in range(8)]
        r_st = Res("st")
        for vb in range(2):
            P.op("dve", lambda e, vb=vb: e.memset(vaug[vb][:, 256:264], 1.0), cowrites=[r_va[vb]])
        qi = 0
        for h in range(4):
            P.dma("sp", "hn", hn_h, self.d_hn[:, b, h * 256:(h + 1) * 256], writes=[r_hn])
            selh = self.cf[:, CF_SELB + h * 128:CF_SELB + (h + 1) * 128]
            for tb in range(NTB):
                tsl = slice(tb * TB, (tb + 1) * TB)
                for (c0, Gsl, rG, dstT, r_d) in ((0, G1[:, tsl], r_A[2], qpT, r_qp), (128, G2[:, tsl], r_B[2], kpT, r_kp)):
                    pb = self.ps[(qi % 2) * 2]
                    pq = self.ps[(qi % 2) * 2 + 1]
                    r_pb = self.r_ps[(qi % 2) * 2]
                    r_pq = self.r_ps[(qi % 2) * 2 + 1]
                    sc = qs[qi % 2]
                    r_sc = r_qs[qi % 2]
                    qi += 1
                    P.op("pe", lambda e, pb=pb, Gsl=Gsl, selh=selh: e.matmul(pb[:, :], lhsT=selh, rhs=Gsl, start=True, stop=True),
                         reads=[rG, rc], writes=[r_pb])

                    def mmq(e, pq=pq, c0=c0, tsl=tsl):
                        ins = None
                        for k in range(8):
                            ins = e.matmul(pq[:, :], lhsT=Wh[:, k, c0:c0 + 128], rhs=self.hT[:, k, tsl], start=(k == 0), stop=(k == 7))
                        return ins
                    P.op("pe", mmq, reads=[r_wh, self.r_h[tb]], writes=[r_pq])
                    P.op("act", lambda e, sc=sc, pq=pq: e.activation(out=sc, in_=pq[:, :], func=AF.Copy), reads=[r_pq], writes=[r_sc])
                    P.op("dve", lambda e, o=dstT[:, tsl], sc=sc, pb=pb: e.tensor_tensor(out=o, in0=sc, in1=pb[:, :], op=ALU.mult),
                         reads=[r_sc, r_pb], writes=[r_d[tb]])
            def proj(c, part, h=h):
                t0 = c * 128
                ia = c % 2
                io = 2
                pa = self.ps[ia]
                po = self.ps[io]

                def mma(e, pa=pa, t0=t0):
                    ins = None
                    for k in range(8):
                        ins = e.matmul(pa[:, 0:384], lhsT=self.hT[:, k, t0:t0 + 128], rhs=Wh[:, k, 256:640], start=(k == 0), stop=(k == 7))
                    return ins

                def mmo(e, po=po, t0=t0):
                    ins = None
                    for k in range(8):
                        ins = e.matmul(po[:, 0:256], lhsT=self.hT[:, k, t0:t0 + 128], rhs=Wh[:, k, 640:896], start=(k == 0), stop=(k == 7))
                    return ins
                if part == "pe":
                    P.op("pe", mma, reads=[r_wh, self.r_h[c // 4]], writes=[self.r_ps[ia]])
                    P.op("pe", mmo, reads=[r_wh, self.r_h[c // 4]], writes=[self.r_ps[io]])
                    return
                cb_ = c % 2
                wsc = tokS[:, c, h:h + 1]
                P.op("act", lambda e, pa=pa, cb_=cb_, wsc=wsc: e.activation(out=kws[cb_], in_=pa[:, 0:128], func=AF.Copy, scale=wsc),
                     reads=[self.r_ps[ia], r_tok], writes=[r_kws[cb_]])
                P.op("act", lambda e, pa=pa, cb_=cb_: e.activation(out=vaug[cb_][:, 0:256], in_=pa[:, 128:384], func=AF.Copy),
                     reads=[self.r_ps[ia]], cowrites=[r_va[cb_]])
                P.op("act", lambda e, po=po, cb_=cb_: e.activation(out=eo[cb_], in_=po[:, 0:256], func=AF.Exp, scale=-1.0),
                     reads=[self.r_ps[io]], writes=[r_eo[cb_]])
                P.op("act", lambda e, cb_=cb_: e.activation(out=eo[cb_], in_=eo[cb_], func=AF.Ln, bias=1.0),
                     reads=[r_eo[cb_]], writes=[r_eo[cb_]])
                P.op("act", lambda e, cb_=cb_: e.activation(out=eo[cb_], in_=eo[cb_], func=AF.Exp, scale=-1.0),
                     reads=[r_eo[cb_]], writes=[r_eo[cb_]])
                P.op("dve", lambda e, cb_=cb_: e.tensor_tensor(out=eo[cb_], in0=eo[cb_], in1=hn_h, op=ALU.mult),
                     reads=[r_eo[cb_], r_hn], writes=[r_eo[cb_]])

            pT = self.ps[3].bitcast(BF16)

            def recur(c, part, h=h):
                t0 = c * 128
                csl = slice(t0, t0 + 128)
                tbq = c // 4
                cb_ = c % 2
                pn = self.ps[4 + cb_]
                r_pn = self.r_ps[4 + cb_]
                if part == "a":
                    P.op("pe", lambda e: e.matmul(self.ps[3][:, 0:128], lhsT=kpT[:, csl], rhs=qpT[:, csl], start=True, stop=True),
                         reads=[r_kp[tbq], r_qp[tbq]], writes=[self.r_ps[3]])
                    return
                if part == "b":
                    P.op("dve", lambda e: e.tensor_tensor(out=smT[cb_], in0=self.ps[3][:, 0:128], in1=tri, op=ALU.mult),
                         reads=[self.r_ps[3], rc], writes=[r_sm[cb_]])
                Cprev = Cb[(c + 1) % 2]

                def mmn(e):
                    if c > 0:
                        e.matmul(pn[:, 0:257], lhsT=qpT[:, csl], rhs=Cprev[:, 0:257], start=True, stop=False)
                    return e.matmul(pn[:, 0:257], lhsT=smT[cb_], rhs=vaug[cb_][:, 0:257], start=(c == 0), stop=True)
                rd = [r_qp[tbq], r_sm[cb_], r_va[cb_]]
                if c > 0:
                    rd.append(r_Cb[(c + 1) % 2])
                if part == "b":
                    P.op("pe", mmn, reads=rd, writes=[r_pn])
                if part == "b" and c < 15:
                    P.op("pe", lambda e: e.matmul(self.ps[6][:, 0:257], lhsT=kws[cb_], rhs=vaug[cb_][:, 0:257], start=True, stop=True),
                         reads=[r_kws[cb_], r_va[cb_]], writes=[self.r_ps[6]])
                    if c == 0:
                        P.op("dve", lambda e: e.tensor_copy(out=Cb[cb_][:, 0:257], in_=self.ps[6][:, 0:257]), reads=[self.r_ps[6]], writes=[r_Cb[cb_]])
                        P.op("dve", lambda e: e.tensor_copy(out=Cf[:, 0:257], in_=self.ps[6][:, 0:257]), reads=[self.r_ps[6]], writes=[r_Cf])
                    else:
                        dsc = dec_b[:, h * 16 + c:h * 16 + c + 1]
                        P.op("dve", lambda e: e.scalar_tensor_tensor(out=Cb[cb_][:, 0:257], in0=Cf[:, 0:257], scalar=dsc, in1=self.ps[6][:, 0:257],
                                                                     op0=ALU.mult, op1=ALU.add),
                             reads=[self.r_ps[6], r_Cf, r_dec], writes=[r_Cb[cb_]])
                        if c < 14:
                            P.op("dve", lambda e: e.scalar_tensor_tensor(out=Cf[:, 0:257], in0=Cf[:, 0:257], scalar=dsc, in1=self.ps[6][:, 0:257],
                                                                         op0=ALU.mult, op1=ALU.add),
                                 reads=[self.r_ps[6], r_Cf, r_dec], writes=[r_Cf])
                if part == "b":
                    return
                st = small2[:, 16 + 8 * cb_:24 + 8 * cb_]
                emt2 = tokS[:, c, 4 + h:5 + h]
                P.op("act", lambda e: e.activation(out=junk, in_=pn[:, 0:256], func=AF.Square, accum_out=st[:, 2:3]),
                     reads=[r_pn], writes=[r_junk], cowrites=[r_st])
                P.op("act", lambda e: e.activation(out=st[:, 6:7], in_=pn[:, 256:257], func=AF.Square),
                     reads=[r_pn], cowrites=[r_st])
                P.op("dve", lambda e: e.tensor_scalar(out=st[:, 0:1], in0=st[:, 6:7], scalar1=emt2, scalar2=EPS, op0=ALU.max, op1=ALU.mult),
                     reads=[r_st, r_tok], cowrites=[r_st])
                P.op("dve", lambda e: e.scalar_tensor_tensor(out=st[:, 1:2], in0=st[:, 2:3], scalar=1.0 / 256, in1=st[:, 0:1],
                                                             op0=ALU.mult, op1=ALU.add), reads=[r_st], cowrites=[r_st])
                P.op("act", lambda e: e.activation(out=st[:, 3:4], in_=st[:, 1:2], func=AF.Ln), reads=[r_st], cowrites=[r_st])
                P.op("act", lambda e: e.activation(out=st[:, 5:6], in_=st[:, 3:4], func=AF.Exp, scale=-0.5), reads=[r_st], cowrites=[r_st])
                P.op("dve", lambda e: e.scalar_tensor_tensor(out=yh[cb_], in0=pn[:, 0:256], scalar=st[:, 5:6], in1=eo[cb_],
                                                             op0=ALU.mult, op1=ALU.mult),
                     reads=[r_pn, r_st, r_eo[cb_]], writes=[r_yh[cb_]])

            def ytrans(c, h=h):
                t0 = c * 128
                csl = slice(t0, t0 + 128)
                tbq = c // 4
                cb_ = c % 2
                for j in range(2):
                    P.op("pe", lambda e, j=j: e.matmul(self.ps[7][:, 128 + j * 128:128 + (j + 1) * 128], lhsT=yh[cb_][:, j * 128:(j + 1) * 128],
                                                       rhs=self.ident_b, start=True, stop=True),
                         reads=[r_yh[cb_], rc], **(dict(writes=[self.r_ps[7]]) if j == 0 else dict(cowrites=[self.r_ps[7]])))
                P.op("act", lambda e: e.activation(out=yT[:, 2 * h:2 * h + 2, csl], in_=self.ps[7][:, 128:384].rearrange("p (a b) -> p a b", a=2), func=AF.Copy),
                     reads=[self.r_ps[7]], cowrites=[r_yT[2 * h][tbq], r_yT[2 * h + 1][tbq]])
            proj(0, "pe")
            proj(0, "ev")
            for c in range(16):
                recur(c, "a")
                if c + 1 < 16:
                    proj(c + 1, "pe")
                recur(c, "b")
                if c + 1 < 16:
                    proj(c + 1, "ev")
                recur(c, "c")
                if c > 0:
                    ytrans(c - 1)
            ytrans(15)
            if h + 1 < 4:
                load_head(h + 1)
        if s == 0 and l == 1 and self.dbg == 2:
            self.dump("qpT", qpT, [128, S], BF16)
            self.dump("kpT", kpT, [128, S], BF16)
            for j in range(8):
                self.dump("yT%d" % j, yT[:, j, :], [128, S], BF16)
            self.dump("small2", small2, [128, 64], F32)
        wo = self.WB[:, 0:8192].rearrange("p (h n) -> p h n", h=8)
        r_wo = Res("wo")
        for h2 in range(4):
            kw = dict(writes=[r_wh, r_wo]) if h2 == 0 else dict(cowrites=[r_wo])
            P.dma("pool", "wh", self.WB[:, h2 * 2048:(h2 + 1) * 2048], self.d_mlo[b, :, h2 * 2048:(h2 + 1) * 2048], **kw)
        it = 0
        for d in range(8):
            ga = self.mod_ap(l, 2, d, s)
            for tb in range(NTB):
                tsl = slice(tb * TB, (tb + 1) * TB)
                po = self.ps[it % 2]
                r_po = self.r_ps[it % 2]
                it += 1

                def mmd(e, po=po, d=d, tsl=tsl):
                    ins = None
                    for j in range(8):
                        ins = e.matmul(po[:, :], lhsT=wo[:, j, d * 128:(d + 1) * 128], rhs=yT[:, j, tsl], start=(j == 0), stop=(j == 7))
                    return ins
                P.op("pe", mmd, reads=[r_wo] + [r_yT[j][tb] for j in range(8)], writes=[r_po])
                xo = self.xT[:, d, tsl]
                P.op("dve", lambda e, xo=xo, po=po, ga=ga: e.scalar_tensor_tensor(
                    out=xo, in0=po[:, :], scalar=ga, in1=xo, op0=ALU.mult, op1=ALU.add),
                    reads=[r_po, self.r_mod], writes=[self.r_x[d][tb]])


def _consts():
    cf = np.zeros((128, NCF), np.float32)
    cf[:, CF_ID:CF_ID + 128] = np.eye(128, dtype=np.float32)
    cf[:, CF_ONE:CF_ONE + 128] = 1.0
    cf[:, CF_TRI:CF_TRI + 128] = np.triu(np.ones((128, 128), np.float32))
    for h in range(4):
        cf[h, CF_SEL + h * 128:CF_SEL + (h + 1) * 128] = 1.0
        cf[64 + h, CF_SEL + h * 128:CF_SEL + (h + 1) * 128] = 1.0
        cf[h, CF_SELA + h * 128:CF_SELA + (h + 1) * 128] = 1.0
        cf[64 + h, CF_SELB + h * 128:CF_SELB + (h + 1) * 128] = 1.0
    cb = np.zeros((128, NCB), np.float32)
    cb[:, CB_ID:CB_ID + 128] = np.eye(128, dtype=np.float32)
    cb[:, CB_ONE:CB_ONE + 128] = 1.0
    cb[0, CB_MROW + 64:CB_MROW + 128] = -30000.0
    return cf, cb.astype(ml_dtypes.bfloat16)


def _col(v, nchunk):
    return np.ascontiguousarray(np.asarray(v, np.float32).reshape(nchunk, 128).T)


def prep_shared(inp):
    f32 = np.float32
    sh = {}
    pp = np.zeros((128, NPP), f32)
    for l in range(DEPTH):
        pp[:, PP_MODB + l * 48:PP_MODB + (l + 1) * 48] = _col(inp["mod_b"][l], 48)
        for tap in range(3):
            pp[:, PP_CW + l * 66 + tap * 22:PP_CW + l * 66 + (tap + 1) * 22] = _col(inp["ffn_conv_w"][l, tap], 22)
        pp[:, PP_CB + l * 22:PP_CB + (l + 1) * 22] = _col(inp["ffn_conv_b"][l], 22)
    for a in range(2):
        pp[:, PP_QN + a * 4:PP_QN + (a + 1) * 4] = _col(inp["mla_q_norm"][a], 4)
        pp[:, PP_KVN + a * 2:PP_KVN + (a + 1) * 2] = _col(inp["mla_kv_norm"][a], 2)
        pp[0:4, PP_GB + a * 2] = np.asarray(inp["ml_b_gates"][a][0:4], f32)
        pp[0:4, PP_GB + a * 2 + 1] = np.asarray(inp["ml_b_gates"][a][4:8], f32)
    pp[:, PP_FN:PP_FN + 8] = _col(inp["final_norm"], 8)
    sh["pp"] = pp
    cf, cb = _consts()
    sh["cf"] = cf
    sh["cb"] = cb
    mw = np.asarray(inp["mod_w"], f32)
    sh["modw"] = np.ascontiguousarray(mw.reshape(DEPTH, 8, 128, 12, 512).transpose(0, 3, 2, 1, 4))
    hn = np.asarray(inp["ml_head_norm"], f32)
    sh["hn"] = np.ascontiguousarray(np.broadcast_to(hn[None, :, :], (128, 2, 1024)))
    wu = np.asarray(inp["ffn_w_up"], f32)
    wa = wu[:, :, :DFF].reshape(DEPTH, 8, 128, NFC, 128)
    wg = wu[:, :, DFF:].reshape(DEPTH, 8, 128, NFC, 128)
    fup = np.concatenate([wa, wg], axis=-1)
    sh["fup"] = np.ascontiguousarray(fup.transpose(0, 3, 2, 1, 4))
    wd = np.asarray(inp["ffn_w_down"], f32)
    wd = wd.reshape(DEPTH, 2, 11, 128, 8, 128)
    sh["fdn"] = np.ascontiguousarray(wd.transpose(0, 1, 4, 3, 2, 5))
    jj = np.arange(0, 64, 2, dtype=f32) / f32(64)
    invf = (f32(1.0) / (f32(10000.0) ** jj)).astype(f32)
    pp[0:64, PP_IF] = np.concatenate([invf, invf])
    win = np.asarray(inp["mla_w_in"], f32)
    win = np.concatenate([win, win[:, :, 800:832], win[:, :, 768:800]], axis=-1)
    sh["mwin"] = np.ascontiguousarray(win.reshape(2, 8, 128, 896).transpose(0, 2, 1, 3).reshape(2, 128, 8 * 896))
    wq = np.asarray(inp["mla_w_q_up"], f32).reshape(2, 4, 128, 8, 192)
    wq = np.concatenate([wq, wq[..., 160:192], wq[..., 128:160]], axis=-1)
    sh["mwq"] = np.ascontiguousarray(wq.transpose(0, 3, 2, 1, 4).reshape(2, 8, 128, 4 * 256))
    wkv = np.asarray(inp["mla_w_kv_up"], f32).reshape(2, 2, 128, 8, 256)
    sh["mwkv"] = np.ascontiguousarray(wkv.transpose(0, 3, 2, 1, 4).reshape(2, 8, 128, 2 * 256))
    wo = np.asarray(inp["mla_w_out"], f32).reshape(2, 8, 128, 1024)
    sh["mwo"] = np.ascontiguousarray(wo.transpose(0, 2, 1, 3).reshape(2, 128, 8 * 1024))
    mw_ = np.asarray(inp["ml_w_in"], f32).reshape(2, 8, 128, 3080)
    heads = []
    for h in range(4):
        heads.append(np.concatenate([mw_[..., h * 128:(h + 1) * 128], mw_[..., 512 + h * 128:512 + (h + 1) * 128],
                                     mw_[..., 512 + h * 128:512 + (h + 1) * 128],
                                     mw_[..., 1024 + h * 256:1024 + (h + 1) * 256],
                                     mw_[..., 2048 + h * 256:2048 + (h + 1) * 256]], axis=-1))
    mlw = np.stack(heads, axis=1)
    sh["mlw"] = np.ascontiguousarray(mlw.transpose(0, 1, 3, 2, 4).reshape(2, 4, 128, 8 * 896))
    mlg = np.zeros((2, 2, 128, 8, 128), f32)
    for g in range(2):
        mlg[:, g, :, :, 0:4] = mw_[..., 3072 + 4 * g:3076 + 4 * g].transpose(0, 2, 1, 3)
    sh["mlg"] = np.ascontiguousarray(mlg.reshape(2, 2, 128, 8 * 128))
    mo = np.asarray(inp["ml_w_out"], f32).reshape(2, 8, 128, 1024)
    sh["mlo"] = np.ascontiguousarray(mo.transpose(0, 2, 1, 3).reshape(2, 128, 8 * 1024))
    return sh


def prep_core(inp, core, n_seq=2):
    b0 = core * n_seq
    x = np.asarray(inp["x"][b0:b0 + n_seq], np.float32)
    xT = np.ascontiguousarray(x.reshape(n_seq, S, 8, 128).transpose(0, 3, 2, 1))
    c = np.asarray(inp["c"][b0:b0 + n_seq], np.float32)
    cT = np.ascontiguousarray(c.reshape(n_seq, 8, 128).transpose(2, 1, 0))
    pos = np.ascontiguousarray(np.asarray(inp["positions"][b0:b0 + n_seq], np.int32))
    return {"xT": xT, "cT": cT, "pos": pos}


def unpack_out(outT):
    ns = outT.shape[0]
    return np.ascontiguousarray(outT.transpose(0, 3, 2, 1).reshape(ns, S, D))


_CACHE = {}


def get_nc(stages=None, dbg=False):
    key = (tuple(stages) if stages is not None else None, dbg)
    if key not in _CACHE:
        b = Builder(stages=stages, dbg=dbg)
        nc = b.build()
        _CACHE[key] = (nc, b)
    return _CACHE[key]


def run(inp, cores=range(8), stages=None, trace=False, dbg=False):
    nc, b = get_nc(stages, dbg)
    sh = prep_shared(inp)
    in_maps = []
    for c in cores:
        m = dict(sh)
        m.update(prep_core(inp, c))
        in_maps.append(m)
    res = run_bass_kernel_spmd(nc, in_maps, core_ids=list(range(len(in_maps))), trace=trace)
    outs = [unpack_out(r["outT"]) for r in res.results]
    return np.concatenate(outs, axis=0), res


def kernel(**inputs):
    out, _ = run(inputs)
    return out.astype(np.float32)
```

```python
import numpy as np
import ml_dtypes
from contextlib import ExitStack
import concourse.bass as bass
import concourse.mybir as mybir
from concourse.bass_utils import run_bass_kernel_spmd

F32 = mybir.dt.float32
BF16 = mybir.dt.bfloat16
F32R = mybir.dt.float32r
I32 = mybir.dt.int32
AF = mybir.ActivationFunctionType
ALU = mybir.AluOpType
AX = mybir.AxisListType

D = 1024
S = 2048
DEPTH = 4
DFF = 2816
NFC = 22
FF_SPLIT = (11, 11)
EPS = 1e-6
TB = 512
NTB = S // TB

PP_MODB = 0
PP_QN = 192
PP_KVN = 200
PP_CW = 204
PP_CB = 468
PP_FN = 556
PP_GB = 564
PP_IF = 568
NPP = 576
CF_ID = 0
CF_ONE = 128
CF_TRI = 256
CF_SEL = 384
CF_SELA = 896
CF_SELB = 1408
NCF = 1920
CB_ID = 0
CB_ONE = 128
CB_MROW = 256
NCB = 384


class Res:
    __slots__ = ("name", "writers", "readers")

    def __init__(self, name):
        self.name = name
        self.writers = []
        self.readers = []


class Prog:
    ENG = ("pe", "act", "dve", "pool", "sp")

    def __init__(self, nc, stack):
        self.nc = nc
        self.stack = stack
        self.q = {e: [] for e in self.ENG}
        self.n = {e: 0 for e in self.ENG}
        self.seen = {e: {} for e in self.ENG}
        self.needed = {e: set() for e in self.ENG}
        self.csem = {e: stack.enter_context(nc.semaphore("cs_" + e)) for e in self.ENG}
        self.dsem = {}
        self.dtot = {}

    def _prune(self, eng, waits, raw_set):
        best = {}
        for ev in waits:
            if ev[0] == "c":
                if ev[1] == eng and id(ev) not in raw_set:
                    continue
                key = ("c", ev[1])
            else:
                key = ("d", ev[1])
            if ev[2] > best.get(key, 0):
                best[key] = ev[2]
        out = []
        seen = self.seen[eng]
        for key, val in best.items():
            if val <= seen.get(key, 0):
                continue
            seen[key] = val
            out.append((key[0], key[1], val))
            if key[0] == "c":
                self.needed[key[1]].add(val)
        return out

    @staticmethod
    def _deps(reads, writes, cowrites):
        waits = []
        raw = set()
        for r in reads:
            for ev in r.writers:
                waits.append(ev)
                raw.add(id(ev))
        for r in writes:
            waits += r.writers
            waits += r.readers
        for r in cowrites:
            waits += r.readers
        return waits, raw

    @staticmethod
    def _update(ev, reads, writes, cowrites):
        for r in reads:
            r.readers.append(ev)
        for r in writes:
            r.writers = [ev]
            r.readers = []
        for r in cowrites:
            r.writers.append(ev)

    def op(self, eng, fn, reads=(), writes=(), cowrites=()):
        waits, raw = self._deps(reads, writes, cowrites)
        w = self._prune(eng, waits, raw)
        self.n[eng] += 1
        ev = ("c", eng, self.n[eng])
        self.q[eng].append((fn, w, ev))
        self._update(ev, reads, writes, cowrites)
        return ev

    def dma(self, eng, sem, out, in_, reads=(), writes=(), cowrites=(), **kw):
        if sem not in self.dsem:
            self.dsem[sem] = self.stack.enter_context(self.nc.semaphore("ds_" + sem))
            self.dtot[sem] = 0
        waits, raw = self._deps(reads, writes, cowrites)
        w = self._prune(eng, waits, raw)
        self.dtot[sem] += 16
        ev = ("d", sem, self.dtot[sem])
        self.q[eng].append((lambda e: e.dma_start(out=out, in_=in_, **kw), w, ev))
        self._update(ev, reads, writes, cowrites)
        return ev

    def barrier(self, clear=()):
        evs = []
        for e in self.ENG:
            if self.n[e] > 0:
                evs.append(("c", e, self.n[e]))
        for s, t in self.dtot.items():
            if t > 0:
                evs.append(("d", s, t))
        for e in self.ENG:
            w = self._prune(e, [ev for ev in evs if not (ev[0] == "c" and ev[1] == e)], set())
            if w:
                self.q[e].append((None, w, None))
        for r in clear:
            r.writers = []
            r.readers = []

    def finish(self, final_events):
        w = self._prune("sp", list(final_events), set(id(e) for e in final_events))
        self.q["sp"].append((None, w, None))
        nc = self.nc
        rank = {}
        for e in self.ENG:
            rank[e] = {idx: i + 1 for i, idx in enumerate(sorted(self.needed[e]))}
        self.stats = {e: len(self.q[e]) for e in self.ENG}
        self.stats["incs"] = {e: len(rank[e]) for e in self.ENG}

        def replay(ename, eobj):
            for fn, waits, ev in self.q[ename]:
                for (kind, key, val) in waits:
                    if kind == "c":
                        eobj.wait_ge(self.csem[key], rank[key][val])
                    else:
                        eobj.wait_ge(self.dsem[key], val)
                if fn is None:
                    continue
                ins = fn(eobj)
                if ev is None:
                    continue
                if ev[0] == "c":
                    if ev[2] in rank[ename]:
                        ins.then_inc(self.csem[ename], 1)
                else:
                    ins.then_inc(self.dsem[ev[1]], 16)

        with nc.Block() as block:
            @block.tensor
            def _(t):
                replay("pe", t)

            @block.scalar
            def _(t):
                replay("act", t)

            @block.vector
            def _(t):
                replay("dve", t)

            @block.gpsimd
            def _(t):
                replay("pool", t)

            @block.sync
            def _(t):
                replay("sp", t)


class Builder:
    def __init__(self, n_seq=2, stages=None, dbg=False):
        self.n_seq = n_seq
        self.stages = stages
        self.dbg = dbg
        self.nc = bass.Bass("TRN2", target_bir_lowering=False)
        self.stack = ExitStack()

    def dram_in(self, name, shape, dt):
        return self.nc.dram_tensor(name, list(shape), dt, kind="ExternalInput").ap()

    def dump(self, name, ap, shape, dt):
        if not self.dbg:
            return
        d = self.nc.dram_tensor("dbg_" + name, list(shape), dt, kind="ExternalOutput").ap()
        self.P.barrier()
        self.P.dma("sp", "dbg", d, ap)
        self.P.barrier()

    def sb(self, name, shape, dt):
        return self.stack.enter_context(self.nc.sbuf_tensor(name, list(shape), dt))

    def build(self):
        nc = self.nc
        with self.stack:
            self.P = Prog(nc, self.stack)
            self._declare()
            self._program()
        return nc

    def _declare(self):
        nc = self.nc
        ns = self.n_seq
        self.d_xT = self.dram_in("xT", [ns, 128, 8, S], F32)
        self.d_cT = self.dram_in("cT", [128, 8, ns], F32)
        self.d_pos = self.dram_in("pos", [ns, S], I32)
        self.d_modw = self.dram_in("modw", [DEPTH, 12, 128, 8, 512], F32)
        self.d_pp = self.dram_in("pp", [128, NPP], F32)
        self.d_cf = self.dram_in("cf", [128, NCF], F32)
        self.d_cb = self.dram_in("cb", [128, NCB], BF16)
        self.d_hn = self.dram_in("hn", [128, 2, 1024], F32)
        self.d_fup = self.dram_in("fup", [DEPTH, NFC, 128, 8, 256], F32)
        self.d_fdn = self.dram_in("fdn", [DEPTH, 2, 8, 128, 11, 128], F32)
        self.d_win = self.dram_in("mwin", [2, 128, 8 * 896], F32)
        self.d_wq = self.dram_in("mwq", [2, 8, 128, 4 * 256], F32)
        self.d_wkv = self.dram_in("mwkv", [2, 8, 128, 2 * 256], F32)
        self.d_wo = self.dram_in("mwo", [2, 128, 8 * 1024], F32)
        self.d_mlw = self.dram_in("mlw", [2, 4, 128, 8 * 896], F32)
        self.d_mlg = self.dram_in("mlg", [2, 2, 128, 8 * 128], F32)
        self.d_mlo = self.dram_in("mlo", [2, 128, 8 * 1024], F32)
        self.d_out = nc.dram_tensor("outT", [ns, 128, 8, S], F32, kind="ExternalOutput").ap()

        self.xT = self.sb("xT_sb", [128, 8, S], F32)
        self.hT = self.sb("hT_sb", [128, 8, S], BF16)
        self.pp = self.sb("pp_sb", [128, NPP], F32)
        self.cf = self.sb("cf_sb", [128, NCF], F32)
        self.cb = self.sb("cb_sb", [128, NCB], BF16)
        self.modT = self.sb("modT", [128, DEPTH, 48, 2], F32)
        self.cact = self.sb("cact", [128, 8, 2], F32)
        self.sqr = self.sb("sqr", [128, 2, 512], BF16)
        self.tab = self.sb("ropetab", [128, S], BF16)
        self.r_tab = Res("tab")
        self.WB = self.sb("WB", [128, 12288], BF16)
        self.AR = self.sb("AR", [128, 17792], F32)
        self.ps = [self.stack.enter_context(nc.psum_tensor("ps%d" % i, [128, 512], F32)) for i in range(8)]
        self.r_ps = [Res("ps%d" % i) for i in range(8)]
        self.r_x = [[Res("x%d_%d" % (k, t)) for t in range(NTB)] for k in range(8)]
        self.r_h = [Res("h%d" % t) for t in range(NTB)]
        self.r_const = Res("const")
        self.r_mod = Res("mod")
        self.ident_f = self.cf[:, CF_ID:CF_ID + 128]
        self.ones_f = self.cf[:, CF_ONE:CF_ONE + 128]
        self.ident_b = self.cb[:, CB_ID:CB_ID + 128]
        self.ones_b = self.cb[:, CB_ONE:CB_ONE + 128]

    def ar_f32(self, off_kib, shape):
        n = int(np.prod(shape[1:]))
        o = int(off_kib * 256)
        ap = self.AR[0:shape[0], o:o + n]
        if len(shape) == 3:
            ap = ap.rearrange("p (a b) -> p a b", a=shape[1])
        return ap

    def ar_bf16(self, off_kib, shape):
        n = int(np.prod(shape[1:]))
        o = int(off_kib * 512)
        ap = self.AR.bitcast(BF16)[0:shape[0], o:o + n]
        if len(shape) == 3:
            ap = ap.rearrange("p (a b) -> p a b", a=shape[1])
        return ap

    def areset(self):
        self._acur = 0

    def af(self, shape):
        n = int(np.prod(shape[1:]))
        n = (n + 7) // 8 * 8
        o = self._acur
        self._acur += n
        assert self._acur <= 17792, self._acur
        ap = self.AR[0:shape[0], o:o + int(np.prod(shape[1:]))]
        if len(shape) == 3:
            ap = ap.rearrange("p (a b) -> p a b", a=shape[1])
        return ap

    def ab(self, shape):
        n = int(np.prod(shape[1:]))
        nf = (n + 15) // 16 * 8
        o = self._acur * 2
        self._acur += nf
        assert self._acur <= 17792, self._acur
        ap = self.AR.bitcast(BF16)[0:shape[0], o:o + n]
        if len(shape) == 3:
            ap = ap.rearrange("p (a b) -> p a b", a=shape[1])
        return ap

    def mod_ap(self, l, kind, i, s):
        return self.modT[:, l, kind * 8 + i, s:s + 1]

    def _program(self):
        P = self.P
        st = self.stages
        self.load_x(0)
        self.prologue()
        P.barrier()
        last = []
        for s in range(self.n_seq):
            if s > 0:
                self.load_x(s)
                P.barrier()
            for l in range(DEPTH):
                if st is None or ("mix%d" % l) in st:
                    self.norm_mod(s, l, 1, 0)
                    P.barrier()
                    if l % 2 == 0:
                        self.mla(s, l)
                    else:
                        self.mlstm(s, l)
                    P.barrier()
                if st is None or ("ffn%d" % l) in st:
                    self.norm_mod(s, l, 4, 3)
                    P.barrier()
                    self.ffn(s, l)
                    P.barrier()
            last += self.final_store(s)
            P.barrier()
        P.finish(last)

    def prologue(self):
        P = self.P
        rc = self.r_const
        P.dma("sp", "c0", self.pp[:, :], self.d_pp[:, :], cowrites=[rc])
        P.dma("sp", "c0", self.cf[:, :], self.d_cf[:, :], cowrites=[rc])
        P.dma("sp", "c0", self.cb[:, :], self.d_cb[:, :], cowrites=[rc])
        P.dma("sp", "c0", self.cact[:, :, :], self.d_cT[:, :, 0:2], cowrites=[rc])
        r_ca = Res("cact")
        cact2 = self.cact[:, :, :]
        P.op("act", lambda e: e.activation(out=cact2, in_=cact2, func=AF.Silu), reads=[rc], writes=[r_ca])
        NSTG = 4
        stg = [self.ar_bf16(8 * i, [128, 8, 512]) for i in range(NSTG)]
        r_stg = [Res("stg%d" % i) for i in range(NSTG)]
        cact_b = self.ar_bf16(40, [128, 8, 2])
        P.op("dve", lambda e: e.tensor_copy(out=cact_b, in_=self.cact[:, :, :]), reads=[r_ca], writes=[r_ca])
        it = 0
        for l in range(DEPTH):
            pst = self.ps[l % 2]
            r_p = self.r_ps[l % 2]
            for cbk in range(12):
                sl = it % NSTG
                it += 1
                for k2 in range(2):
                    kw = dict(writes=[r_stg[sl]]) if k2 == 0 else dict(cowrites=[r_stg[sl]])
                    P.dma("pool", "stg%d" % sl, stg[sl][:, k2 * 4:(k2 + 1) * 4, :], self.d_modw[l, cbk, :, k2 * 4:(k2 + 1) * 4, :], **kw)

                def mm(e, sl=sl, cbk=cbk, pst=pst):
                    ins = None
                    for j in range(4):
                        col = (cbk * 4 + j) * 2
                        for k in range(8):
                            ins = e.matmul(pst[:, col:col + 2], lhsT=stg[sl][:, k, j * 128:(j + 1) * 128],
                                           rhs=cact_b[:, k, 0:2], start=(k == 0), stop=(k == 7))
                    return ins
                if cbk == 0:
                    P.op("pe", mm, reads=[r_stg[sl], r_ca], writes=[r_p])
                else:
                    P.op("pe", mm, reads=[r_stg[sl], r_ca], cowrites=[r_p])
            mo = self.modT[:, l, :, :]
            pin = pst[:, 0:96].rearrange("p (a b) -> p a b", a=48)
            bb = self.pp[:, PP_MODB + l * 48:PP_MODB + (l + 1) * 48].unsqueeze(2).broadcast_to([128, 48, 2])
            P.op("dve", lambda e, mo=mo, pin=pin, bb=bb: e.tensor_tensor(out=mo, in0=pin, in1=bb, op=ALU.add),
                 reads=[r_p, rc], cowrites=[self.r_mod])
            for kind in (1, 4):
                m1 = self.modT[:, l, kind * 8:(kind + 1) * 8, :]
                P.op("dve", lambda e, m1=m1: e.tensor_scalar(out=m1, in0=m1, scalar1=1.0, scalar2=None, op0=ALU.add),
                     reads=[self.r_mod], cowrites=[self.r_mod])

    def load_x(self, s):
        P = self.P
        for k in range(8):
            P.dma("sp", "xld%d" % k, self.xT[:, k, :], self.d_xT[s, :, k, :],
                  writes=[self.r_x[k][t] for t in range(NTB)])

    def final_store(self, s):
        P = self.P
        evs = []
        self._norm_core(s, final=True)
        return self._final_evs

    def norm_mod(self, s, l, kind_sc, kind_sh):
        self._norm_core(s, l=l, kind_sc=kind_sc, kind_sh=kind_sh)

    def _norm_core(self, s, l=None, kind_sc=None, kind_sh=None, final=False):
        P = self.P
        sq = [self.sqr[:, 0, :], self.sqr[:, 1, :]]
        r_sq = [Res("sq0"), Res("sq1")]
        lnv = [self.ar_f32(4, [128, 512]), self.ar_f32(6, [128, 512])]
        r_ln = [Res("ln0"), Res("ln1")]
        rstd = [self.ar_f32(8, [128, 512]), self.ar_f32(10, [128, 512])]
        r_rs = [Res("rs0"), Res("rs1")]
        tmp = [self.ar_f32(12 + 2 * i, [128, 512]) for i in range(4)]
        r_tmp = [Res("tmp%d" % i) for i in range(4)]
        ones_r = self.ones_b
        self._final_evs = []
        it = 0
        for tb in range(NTB):
            tsl = slice(tb * TB, (tb + 1) * TB)
            pb = tb % 2
            pst = self.ps[pb]
            r_p = self.r_ps[pb]
            for k in range(8):
                b = it % 2
                it += 1
                xin = self.xT[:, k, tsl]
                P.op("act", lambda e, o=sq[b], i=xin: e.activation(out=o, in_=i, func=AF.Square),
                     reads=[self.r_x[k][tb]], writes=[r_sq[b]])
                kw = dict(reads=[r_sq[b], self.r_const])
                if k == 0:
                    kw["writes"] = [r_p]
                else:
                    kw["cowrites"] = [r_p]
                P.op("pe", lambda e, pst=pst, i=sq[b], k=k: e.matmul(pst[:, :], lhsT=ones_r, rhs=i,
                                                                  start=(k == 0), stop=(k == 7)), **kw)
            P.op("act", lambda e, o=lnv[pb], pst=pst: e.activation(out=o, in_=pst[:, :], func=AF.Ln, scale=1.0 / D, bias=EPS),
                 reads=[r_p], writes=[r_ln[pb]])
            P.op("act", lambda e, o=rstd[pb], i=lnv[pb]: e.activation(out=o, in_=i, func=AF.Exp, scale=-0.5),
                 reads=[r_ln[pb]], writes=[r_rs[pb]])
            for k in range(8):
                tbuf = (tb * 8 + k) % 4
                xin = self.xT[:, k, tsl]
                if not final:
                    sc = self.mod_ap(l, kind_sc, k, s)
                    sh = self.mod_ap(l, kind_sh, k, s)
                    P.op("dve", lambda e, o=tmp[tbuf], x=xin, r=rstd[pb], sc=sc: e.scalar_tensor_tensor(
                        out=o, in0=x, scalar=sc, in1=r, op0=ALU.mult, op1=ALU.mult),
                        reads=[self.r_x[k][tb], r_rs[pb], self.r_mod], writes=[r_tmp[tbuf]])
                    ho = self.hT[:, k, tsl]
                    kw = dict(reads=[r_tmp[tbuf], self.r_mod])
                    if k == 0:
                        kw["writes"] = [self.r_h[tb]]
                    else:
                        kw["cowrites"] = [self.r_h[tb]]
                    if k % 2 == 0:
                        P.op("dve", lambda e, o=ho, i=tmp[tbuf], sh=sh: e.tensor_scalar(
                            out=o, in0=i, scalar1=sh, scalar2=None, op0=ALU.add), **kw)
                    else:
                        P.op("act", lambda e, o=ho, i=tmp[tbuf], sh=sh: e.activation(
                            out=o, in_=i, func=AF.Identity, bias=sh), **kw)
                else:
                    g = self.pp[:, PP_FN + k:PP_FN + k + 1]
                    P.op("dve", lambda e, o=tmp[tbuf], x=xin, r=rstd[pb], g=g: e.scalar_tensor_tensor(
                        out=o, in0=x, scalar=g, in1=r, op0=ALU.mult, op1=ALU.mult),
                        reads=[self.r_x[k][tb], r_rs[pb], self.r_const], writes=[r_tmp[tbuf]])
                    ev = P.dma("sp", "ost%d" % tbuf, self.d_out[s, :, k, tsl], tmp[tbuf], reads=[r_tmp[tbuf]])
                    self._final_evs.append(ev)

    def ffn(self, s, l):
        P = self.P
        u = self.ar_bf16(0, [128, 11, S])
        r_u = [[Res("u%d_%d" % (c, t)) for t in range(NTB)] for c in range(11)]
        AF_W = 2 + S + 2
        a_full = [self.ar_f32(44, [128, AF_W]), self.ar_f32(44 + 8.25, [128, AF_W])]
        r_a = [[Res("a%d_%d" % (b, t)) for t in range(NTB)] for b in range(2)]
        r_az = Res("az")
        tt = [self.ar_f32(61, [128, 512]), self.ar_f32(63, [128, 512])]
        r_t = [Res("t0"), Res("t1")]
        ge = [self.ar_f32(65, [128, 512]), self.ar_f32(67, [128, 512])]
        r_g = [Res("g0"), Res("g1")]
        for b in range(2):
            P.op("dve", lambda e, o=a_full[b][:, 0:2]: e.memset(o, 0.0), cowrites=[r_az])
        NWS = 6
        wsl = [self.WB[:, i * 2048:(i + 1) * 2048] for i in range(NWS)]
        r_w = [Res("w%d" % i) for i in range(NWS)]
        wi = 0
        cwb = PP_CW + l * 66
        cbb = PP_CB + l * 22
        it = 0
        c_glob = 0
        for half, nch in enumerate(FF_SPLIT):
            for cl in range(nch):
                c = c_glob + cl
                sl = wi % NWS
                wi += 1
                wv = wsl[sl].rearrange("p (k n) -> p k n", k=8)
                P.dma("pool", "w%d" % sl, wsl[sl], self.d_fup[l, c].rearrange("p k n -> p (k n)"), writes=[r_w[sl]])
                ab = c % 2
                w0 = self.pp[:, cwb + 0 * 22 + c:cwb + 0 * 22 + c + 1]
                w1 = self.pp[:, cwb + 1 * 22 + c:cwb + 1 * 22 + c + 1]
                w2 = self.pp[:, cwb + 2 * 22 + c:cwb + 2 * 22 + c + 1]
                bia = self.pp[:, cbb + c:cbb + c + 1]
                for tb in range(NTB):
                    tsl = slice(tb * TB, (tb + 1) * TB)
                    pa = self.ps[(it % 2) * 2]
                    pg = self.ps[(it % 2) * 2 + 1]
                    r_pa = self.r_ps[(it % 2) * 2]
                    r_pg = self.r_ps[(it % 2) * 2 + 1]
                    tbuf = it % 2
                    it += 1

                    def mm_a(e, wv=wv, pa=pa, tsl=tsl):
                        ins = None
                        for k in range(8):
                            ins = e.matmul(pa[:, :], lhsT=wv[:, k, 0:128], rhs=self.hT[:, k, tsl], start=(k == 0), stop=(k == 7))
                        return ins

                    def mm_g(e, wv=wv, pg=pg, tsl=tsl):
                        ins = None
                        for k in range(8):
                            ins = e.matmul(pg[:, :], lhsT=wv[:, k, 128:256], rhs=self.hT[:, k, tsl], start=(k == 0), stop=(k == 7))
                        return ins
                    P.op("pe", mm_a, reads=[r_w[sl], self.r_h[tb]], writes=[r_pa])
                    P.op("pe", mm_g, reads=[r_w[sl], self.r_h[tb]], writes=[r_pg])
                    af = a_full[ab]
                    o0 = 2 + tb * TB
                    P.op("act", lambda e, o=af[:, o0:o0 + TB], pa=pa: e.activation(out=o, in_=pa[:, :], func=AF.Copy),
                         reads=[r_pa], writes=[r_a[ab][tb]])
                    P.op("act", lambda e, o=tt[tbuf], pa=pa, w2=w2, bia=bia: e.activation(
                        out=o, in_=pa[:, :], func=AF.Identity, scale=w2, bias=bia),
                        reads=[r_pa, self.r_const], writes=[r_t[tbuf]])
                    rd = [r_a[ab][tb], r_az, self.r_const, r_t[tbuf]]
                    if tb > 0:
                        rd.append(r_a[ab][tb - 1])
                    P.op("dve", lambda e, o=tt[tbuf], a1=af[:, o0 - 1:o0 - 1 + TB], w1=w1: e.scalar_tensor_tensor(
                        out=o, in0=a1, scalar=w1, in1=o, op0=ALU.mult, op1=ALU.add), reads=rd, writes=[r_t[tbuf]])
                    P.op("dve", lambda e, o=tt[tbuf], a0=af[:, o0 - 2:o0 - 2 + TB], w0=w0: e.scalar_tensor_tensor(
                        out=o, in0=a0, scalar=w0, in1=o, op0=ALU.mult, op1=ALU.add), reads=rd, writes=[r_t[tbuf]])
                    P.op("act", lambda e, o=ge[tbuf], i=tt[tbuf]: e.activation(out=o, in_=i, func=AF.Gelu),
                         reads=[r_t[tbuf]], writes=[r_g[tbuf]])
                    P.op("dve", lambda e, o=u[:, cl, tsl], g=ge[tbuf], pg=pg: e.tensor_tensor(out=o, in0=g, in1=pg[:, :], op=ALU.mult),
                         reads=[r_g[tbuf], r_pg], writes=[r_u[cl][tb]])
            for d in range(8):
                sl = wi % NWS
                wi += 1
                wv = wsl[sl][:, 0:nch * 128].rearrange("p (c n) -> p c n", c=nch)
                P.dma("pool", "w%d" % sl, wsl[sl][:, 0:nch * 128], self.d_fdn[l, half, d].rearrange("p c n -> p (c n)"),
                      writes=[r_w[sl]])
                gf = self.mod_ap(l, 5, d, s)
                for tb in range(NTB):
                    tsl = slice(tb * TB, (tb + 1) * TB)
                    po = self.ps[4 + it % 2]
                    r_po = self.r_ps[4 + it % 2]
                    it += 1

                    def mm_d(e, wv=wv, po=po, tsl=tsl, nch=nch):
                        ins = None
                        for cc in range(nch):
                            ins = e.matmul(po[:, :], lhsT=wv[:, cc, :], rhs=u[:, cc, tsl], start=(cc == 0), stop=(cc == nch - 1))
                        return ins
                    P.op("pe", mm_d, reads=[r_w[sl]] + [r_u[cc][tb] for cc in range(nch)], writes=[r_po])
                    xo = self.xT[:, d, tsl]
                    P.op("dve", lambda e, xo=xo, po=po, gf=gf: e.scalar_tensor_tensor(
                        out=xo, in0=po[:, :], scalar=gf, in1=xo, op0=ALU.mult, op1=ALU.add),
                        reads=[r_po, self.r_mod], writes=[self.r_x[d][tb]])
            c_glob += nch


    def mla(self, s, l):
        P = self.P
        a = l // 2
        SCALE = 192.0 ** -0.5
        tab = self.tab[:, :]
        cosT = tab[0:64, :]
        sinT = tab[64:128, :]
        cqn = self.ar_bf16(8, [128, 4, S])
        ckvn = self.ar_bf16(24, [128, 2, S])
        krT = self.ar_bf16(32, [128, S])
        knT2 = [self.ar_bf16(36, [128, S]), self.ar_bf16(40, [128, S])]
        Vh2 = [self.ar_bf16(44, [128, 16, 128]), self.ar_bf16(48, [128, 16, 128])]
        qnT2 = [self.ar_bf16(52, [128, 512]), self.ar_bf16(53, [128, 512])]
        qrT2 = [self.ar_bf16(54, [128, 512]), self.ar_bf16(55, [128, 512])]
        PT = [self.ar_bf16(56, [128, 512]), self.ar_bf16(57, [128, 512]), self.ar_bf16(4, [128, 512])]
        scr = [self.ar_f32(58 + 2 * i, [128, 512]) for i in range(4)]
        rinv = self.ar_f32(66, [128, 512])
        sqb = [self.ar_bf16(0, [128, 512]), self.ar_bf16(1, [128, 512])]
        small = self.ar_f32(5, [128, 64])
        r_cos = self.r_tab; r_sin = self.r_tab
        r_cqn = [Res("cqn%d" % t) for t in range(NTB)]
        r_ckvn = [Res("ckvn%d" % t) for t in range(NTB)]
        r_kr = [Res("kr%d" % t) for t in range(NTB)]
        r_scr = [Res("scr%d" % i) for i in range(4)]
        r_sq = [Res("sqA"), Res("sqB")]
        r_small = Res("small")
        rc = self.r_const
        w_in = self.WB[:, 0:8 * 896].rearrange("p (k n) -> p k n", k=8)
        r_wbig = Res("wbig")
        for k2 in range(4):
            P.dma("pool", "wbig", self.WB[:, k2 * 1792:(k2 + 1) * 1792], self.d_win[a, :, k2 * 1792:(k2 + 1) * 1792],
                  cowrites=[r_wbig])
        if l == 0:
            pos_i = self.ar_f32(36, [64, S]).bitcast(I32)
            tf = self.ar_f32(44, [64, S])
            tg = self.ar_f32(52, [64, S])
            r_pi = Res("posi"); r_tf = Res("tf"); r_tg = Res("tg")
            P.dma("sp", "pos", pos_i, self.d_pos[s:s + 1, :].broadcast_to([64, S]), writes=[r_pi])
            invf = self.pp[0:64, PP_IF:PP_IF + 1]
            TWO_PI = 2.0 * np.pi
            c1 = 6.28125
            rem = TWO_PI - c1
            c2 = float(np.frombuffer(np.array([np.frombuffer(np.float32(rem).tobytes(), np.uint32)[0] & 0xFFFFF000], np.uint32).tobytes(), np.float32)[0])
            c3 = float(np.float32(rem - c2))
            P.op("dve", lambda e: e.tensor_copy(out=tf, in_=pos_i), reads=[r_pi], writes=[r_tf])
            P.op("dve", lambda e: e.tensor_scalar(out=tf, in0=tf, scalar1=invf, scalar2=None, op0=ALU.mult),
                 reads=[r_tf, rc], writes=[r_tf])
            P.op("dve", lambda e: e.tensor_scalar(out=tg, in0=tf, scalar1=float(1.0 / TWO_PI), scalar2=None, op0=ALU.mult),
                 reads=[r_tf], writes=[r_tg])
            P.op("dve", lambda e: e.tensor_copy(out=pos_i, in_=tg), reads=[r_tg], writes=[r_pi])
            P.op("dve", lambda e: e.tensor_copy(out=tg, in_=pos_i), reads=[r_pi], writes=[r_tg])
            for cc in (c1, c2, c3):
                P.op("dve", lambda e, cc=cc: e.scalar_tensor_tensor(out=tf, in0=tg, scalar=-float(cc), in1=tf, op0=ALU.mult, op1=ALU.add),
                     reads=[r_tf, r_tg], writes=[r_tf])
            PI = float(np.pi)
            th = self.ar_f32(60, [64, S])
            r_th = Res("th")
            for (shift, dstT, r_d) in ((0.0, sinT, r_sin), (PI / 2, cosT, r_cos)):
                P.op("dve", lambda e, shift=shift: e.tensor_scalar(out=tg, in0=tf, scalar1=float(shift), scalar2=None, op0=ALU.add),
                     reads=[r_tf], writes=[r_tg])
                P.op("dve", lambda e: e.tensor_scalar(out=th, in0=tg, scalar1=-PI, scalar2=TWO_PI, op0=ALU.is_lt, op1=ALU.mult),
                     reads=[r_tg], writes=[r_th])
                P.op("dve", lambda e: e.tensor_tensor(out=tg, in0=tg, in1=th, op=ALU.add), reads=[r_tg, r_th], writes=[r_tg])
                P.op("dve", lambda e: e.tensor_scalar(out=th, in0=tg, scalar1=PI, scalar2=-TWO_PI, op0=ALU.is_gt, op1=ALU.mult),
                     reads=[r_tg], writes=[r_th])
                P.op("dve", lambda e: e.tensor_tensor(out=tg, in0=tg, in1=th, op=ALU.add), reads=[r_tg, r_th], writes=[r_tg])
                P.op("act", lambda e, dstT=dstT: e.activation(out=dstT, in_=tg, func=AF.Sin), reads=[r_tg], writes=[r_d])
            P.op("dve", lambda e: e.tensor_scalar(out=tab[64:96, :], in0=tab[64:96, :], scalar1=-1.0, scalar2=None, op0=ALU.mult),
                 reads=[r_sin], writes=[r_sin])
        P.barrier()
        r_zero = Res("zero")
        P.op("dve", lambda e: e.memset(krT[64:128, :], 0.0), cowrites=[r_zero])
        for qq in range(2):
            P.op("dve", lambda e, qq=qq: e.memset(qrT2[qq][64:128, :], 0.0), cowrites=[r_zero])

        ones_r = self.ones_b
        sqr = [self.sqr[:, 0, :], self.sqr[:, 1, :]]
        it = 0
        sqi = 0

        rope_i = [0]

        def rope(dst, tsl, nm):
            b2 = (rope_i[0] % 2) * 2
            rope_i[0] += 1
            s0 = scr[b2][0:64, :]
            s1 = scr[b2 + 1][0:64, :]
            P.op("dve", lambda e: e.tensor_tensor(out=s0, in0=self.ps[6][0:64, :], in1=cosT[:, tsl], op=ALU.mult),
                 reads=[self.r_ps[6], r_cos], writes=[r_scr[b2]])
            P.op("dve", lambda e: e.tensor_tensor(out=s1, in0=self.ps[6][64:128, :], in1=sinT[:, tsl], op=ALU.mult),
                 reads=[self.r_ps[6], r_sin], writes=[r_scr[b2 + 1]])
            P.op("dve", lambda e: e.tensor_tensor(out=dst, in0=s0, in1=s1, op=ALU.add),
                 reads=[r_scr[b2], r_scr[b2 + 1], r_zero], writes=[nm])

        for tb in range(NTB):
            tsl = slice(tb * TB, (tb + 1) * TB)
            for (nch, col0, dst, r_dst, gcol) in ((4, 0, cqn, r_cqn, PP_QN + a * 4), (2, 512, ckvn, r_ckvn, PP_KVN + a * 2)):
                pss = self.ps[4 + it % 2]
                r_pss = self.r_ps[4 + it % 2]
                it += 1
                for j in range(nch):
                    pj = self.ps[j]
                    r_pj = self.r_ps[j]

                    def mm(e, pj=pj, c0=col0 + j * 128, tsl=tsl):
                        ins = None
                        for k in range(8):
                            ins = e.matmul(pj[:, :], lhsT=w_in[:, k, c0:c0 + 128], rhs=self.hT[:, k, tsl], start=(k == 0), stop=(k == 7))
                        return ins
                    P.op("pe", mm, reads=[r_wbig, self.r_h[tb]], writes=[r_pj])
                    P.op("act", lambda e, o=scr[j], pj=pj: e.activation(out=o, in_=pj[:, :], func=AF.Copy),
                         reads=[r_pj], writes=[r_scr[j]])
                    b = sqi % 2
                    sqi += 1
                    P.op("act", lambda e, o=sqr[b], i=scr[j]: e.activation(out=o, in_=i, func=AF.Square),
                         reads=[r_scr[j]], writes=[r_sq[b]])
                    kw = dict(reads=[r_sq[b], rc])
                    if j == 0:
                        kw["writes"] = [r_pss]
                    else:
                        kw["cowrites"] = [r_pss]
                    P.op("pe", lambda e, pss=pss, i=sqr[b], j=j, nch=nch: e.matmul(pss[:, :], lhsT=ones_r, rhs=i, start=(j == 0), stop=(j == nch - 1)), **kw)
                P.op("act", lambda e, pss=pss, nch=nch: e.activation(out=rinv, in_=pss[:, :], func=AF.Ln, scale=1.0 / (nch * 128), bias=EPS),
                     reads=[r_pss], writes=[r_small])
                P.op("act", lambda e: e.activation(out=rinv, in_=rinv, func=AF.Exp, scale=-0.5), reads=[r_small], writes=[r_small])
                for j in range(nch):
                    g = self.pp[:, gcol + j:gcol + j + 1]
                    kw = dict(reads=[r_scr[j], r_small, rc])
                    if j == 0:
                        kw["writes"] = [r_dst[tb]]
                    else:
                        kw["cowrites"] = [r_dst[tb]]
                    P.op("dve", lambda e, o=dst[:, j, tsl], i=scr[j], g=g: e.scalar_tensor_tensor(
                        out=o, in0=i, scalar=g, in1=rinv, op0=ALU.mult, op1=ALU.mult), **kw)
            def mmk(e, tsl=tsl):
                ins = None
                for k in range(8):
                    ins = e.matmul(self.ps[6][:, :], lhsT=w_in[:, k, 768:896], rhs=self.hT[:, k, tsl], start=(k == 0), stop=(k == 7))
                return ins
            P.op("pe", mmk, reads=[r_wbig, self.r_h[tb]], writes=[self.r_ps[6]])
            rope(krT[0:64, tsl], tsl, r_kr[tb])
        P.barrier()
        if s == 0 and l == 0:
            self.dump("cos", cosT, [64, S], BF16)
            self.dump("sin", sinT, [64, S], BF16)
            self.dump("cqn", cqn, [128, 4, S], BF16)
            self.dump("ckvn", ckvn, [128, 2, S], BF16)
            self.dump("krT", krT[0:64, :], [64, S], BF16)

        wo = self.WB[:, 0:8192].rearrange("p (h n) -> p h n", h=8)
        r_wo = Res("wo")
        for h2 in range(4):
            P.dma("pool", "wbig", self.WB[:, h2 * 2048:(h2 + 1) * 2048], self.d_wo[a, :, h2 * 2048:(h2 + 1) * 2048], cowrites=[r_wo])

        r_qn = [Res("qn0"), Res("qn1")]
        r_qr = [Res("qr0"), Res("qr1")]
        r_kn = [[Res("kn%d_%d" % (b, t)) for t in range(NTB)] for b in range(2)]
        r_v = [[Res("v%d_%d" % (b, t)) for t in range(NTB)] for b in range(2)]
        r_pt = [Res("pt0"), Res("pt1"), Res("pt2")]
        r_rinv = Res("rinv")
        r_o = [[Res("o%d_%d" % (h, t)) for t in range(NTB)] for h in range(8)]
        r_hw = [Res("hw0"), Res("hw1")]
        r_mk = Res("mk")
        r_mq = Res("mq")
        mrow128 = self.cb[:, CB_MROW:CB_MROW + 128]

        def hw_views(hb):
            wq = self.WB[:, 8192 + hb * 1536:8192 + hb * 1536 + 1024].rearrange("p (k n) -> p k n", k=4)
            wkv = self.WB[:, 8192 + hb * 1536 + 1024:8192 + hb * 1536 + 1536].rearrange("p (k n) -> p k n", k=2)
            return wq, wkv

        def load_hw(h):
            hb = h % 2
            P.dma("pool", "hw%d" % hb, self.WB[:, 8192 + hb * 1536:8192 + hb * 1536 + 1024], self.d_wq[a, h], writes=[r_hw[hb]])
            P.dma("pool", "hw%d" % hb, self.WB[:, 8192 + hb * 1536 + 1024:8192 + hb * 1536 + 1536], self.d_wkv[a, h], cowrites=[r_hw[hb]])

        def sumsq_stages(srcA, rA, srcB, rB, outcol, r_out):
            def st_sq():
                P.op("dve", lambda e: e.tensor_tensor(out=sqb[0], in0=srcA, in1=srcA, op=ALU.mult), reads=[rA], writes=[r_sq[0]])
                P.op("dve", lambda e: e.tensor_tensor(out=sqb[1], in0=srcB, in1=srcB, op=ALU.mult), reads=[rB, r_zero], writes=[r_sq[1]])

            def st_mm():
                def mms(e):
                    e.matmul(self.ps[7][:, :], lhsT=self.ones_b, rhs=sqb[0], start=True, stop=False)
                    return e.matmul(self.ps[7][:, :], lhsT=self.ones_b, rhs=sqb[1], start=False, stop=True)
                P.op("pe", mms, reads=[r_sq[0], r_sq[1], rc], writes=[self.r_ps[7]])

            def st_max():
                P.op("dve", lambda e: e.reduce_max(out=outcol, in_=self.ps[7][:, :], axis=AX.X), reads=[self.r_ps[7]], cowrites=[r_out])
            return [st_sq, st_mm, st_max]

        def kprep(h, tbs):
            hb = h % 2
            wq, wkv = hw_views(hb)
            knT = knT2[hb]
            Vh = Vh2[hb]
            stages = []
            for tb in tbs:
                tsl = slice(tb * TB, (tb + 1) * TB)

                def s1(tb=tb, tsl=tsl):
                    def mmk2(e):
                        ins = None
                        for k in range(2):
                            ins = e.matmul(self.ps[6][:, :], lhsT=wkv[:, k, 0:128], rhs=ckvn[:, k, tsl], start=(k == 0), stop=(k == 1))
                        return ins
                    P.op("pe", mmk2, reads=[r_hw[hb], r_ckvn[tb]], writes=[self.r_ps[6]])

                def s2(tb=tb, tsl=tsl):
                    P.op("act", lambda e: e.activation(out=knT[:, tsl], in_=self.ps[6][:, :], func=AF.Copy),
                         reads=[self.r_ps[6]], writes=[r_kn[hb][tb]])

                def s3(tb=tb):
                    def mmv(e):
                        ins = None
                        for t4 in range(4):
                            t0 = tb * TB + t4 * 128
                            for k in range(2):
                                ins = e.matmul(self.ps[7][:, t4 * 128:(t4 + 1) * 128], lhsT=ckvn[:, k, t0:t0 + 128], rhs=wkv[:, k, 128:256],
                                               start=(k == 0), stop=(k == 1))
                        return ins
                    P.op("pe", mmv, reads=[r_hw[hb], r_ckvn[tb]], writes=[self.r_ps[7]])

                def s4(tb=tb):
                    P.op("dve", lambda e: e.tensor_copy(out=Vh[:, tb * 4:(tb + 1) * 4, :],
                                                        in_=self.ps[7][:, :].rearrange("p (a b) -> p a b", a=4)),
                         reads=[self.r_ps[7]], writes=[r_v[hb][tb]])
                stages += [s1, s2, s3, s4]
                stages += sumsq_stages(knT[:, tsl], r_kn[hb][tb], krT[:, tsl], r_kr[tb], small[:, hb * 4 + tb:hb * 4 + tb + 1], r_mk)
            if tbs[-1] == NTB - 1:
                def s_fin():
                    P.op("dve", lambda e: e.reduce_max(out=small[:, 8 + hb:9 + hb], in_=small[:, hb * 4:hb * 4 + 4], axis=AX.X),
                         reads=[r_mk], cowrites=[r_mk])
                stages.append(s_fin)
            return stages

        qcnt = [0]

        def qprep(h, qb):
            hb = h % 2
            wq, wkv = hw_views(hb)
            qi = qcnt[0] % 2
            qcnt[0] += 1
            tsl = slice(qb * TB, (qb + 1) * TB)
            qnT = qnT2[qi]
            qrT = qrT2[qi]
            mq = small[:, 10 + qi:11 + qi]
            negc = small[:, 16 + h * 4 + qb:17 + h * 4 + qb]

            def s1():
                def mmq(e):
                    ins = None
                    for k in range(4):
                        ins = e.matmul(self.ps[6][:, :], lhsT=wq[:, k, 0:128], rhs=cqn[:, k, tsl], start=(k == 0), stop=(k == 3))
                    return ins
                P.op("pe", mmq, reads=[r_hw[hb], r_cqn[qb]], writes=[self.r_ps[6]])

            def s2():
                P.op("act", lambda e: e.activation(out=qnT, in_=self.ps[6][:, :], func=AF.Copy), reads=[self.r_ps[6]], writes=[r_qn[qi]])

            def s3():
                def mmqr(e):
                    ins = None
                    for k in range(4):
                        ins = e.matmul(self.ps[6][:, :], lhsT=wq[:, k, 128:256], rhs=cqn[:, k, tsl], start=(k == 0), stop=(k == 3))
                    return ins
                P.op("pe", mmqr, reads=[r_hw[hb], r_cqn[qb]], writes=[self.r_ps[6]])

            def s4():
                rope(qrT[0:64, :], tsl, r_qr[qi])

            def s_neg():
                P.op("dve", lambda e: e.tensor_scalar(out=negc, in0=mq, scalar1=small[:, 8 + hb:9 + hb], scalar2=-0.5 * SCALE,
                                                      op0=ALU.add, op1=ALU.mult), reads=[r_mq, r_mk], cowrites=[r_mq])
            stages = [s1, s2, s3, s4] + sumsq_stages(qnT, r_qn[qi], qrT, r_qr[qi], mq, r_mq) + [s_neg]
            return qi, negc, stages

        def blk_ctx(h, qb, qi, negc):
            hb = h % 2
            return dict(h=h, qb=qb, hb=hb, knT=knT2[hb], Vh=Vh2[hb], qnT=qnT2[qi], qrT=qrT2[qi], qi=qi, negc=negc,
                        po=self.ps[3 + qb % 2], r_po=self.r_ps[3 + qb % 2], psm=self.ps[5], r_psm=self.r_ps[5],
                        nj=4 * qb + 4)

        def step_bufs(k):
            return self.ps[k % 3], self.r_ps[k % 3], PT[k % 3], r_pt[k % 3]

        def emit_mms(k, cx, j):
            pst, r_pst, ptb, r_ptb = step_bufs(k)
            r = j - 4 * cx["qb"]
            c0 = 128 * max(r, 0)
            ksl = slice(j * 128, (j + 1) * 128)
            knT, qnT, qrT = cx["knT"], cx["qnT"], cx["qrT"]

            def mms(e):
                e.matmul(pst[:, c0:512], lhsT=knT[:, ksl], rhs=qnT[:, c0:512], start=True, stop=False)
                ins = e.matmul(pst[:, c0:512], lhsT=krT[:, ksl], rhs=qrT[:, c0:512], start=False, stop=(r < 0))
                if r >= 0:
                    ins = e.matmul(pst[:, c0:c0 + 64], lhsT=mrow128, rhs=self.ones_b[:, 0:64], start=False, stop=True)
                return ins
            tbk = j // 4
            P.op("pe", mms, reads=[r_kn[cx["hb"]][tbk], r_kr[tbk], r_qn[cx["qi"]], r_qr[cx["qi"]], rc, r_zero], writes=[r_pst])

        def emit_exp_mmo(k, cx, j):
            pst, r_pst, ptb, r_ptb = step_bufs(k)
            r = j - 4 * cx["qb"]
            c0 = 128 * max(r, 0)
            negc, Vh, po, psm, nj = cx["negc"], cx["Vh"], cx["po"], cx["psm"], cx["nj"]
            P.op("act", lambda e: e.activation(out=ptb[:, c0:512], in_=pst[:, c0:512], func=AF.Exp, scale=SCALE, bias=negc),
                 reads=[r_pst, r_mq], writes=[r_ptb])

            def mmo(e):
                e.matmul(po[:, c0:512], lhsT=Vh[:, j, :], rhs=ptb[:, c0:512], start=(j == 0), stop=(j == nj - 1))
                return e.matmul(psm[:, c0:512], lhsT=self.ones_b, rhs=ptb[:, c0:512], start=(j == 0), stop=(j == nj - 1))
            tbk = j // 4
            kw = dict(reads=[r_v[cx["hb"]][tbk], r_ptb, rc])
            if j == 0:
                kw["writes"] = [cx["r_po"], cx["r_psm"]]
            else:
                kw["cowrites"] = [cx["r_po"], cx["r_psm"]]
            P.op("pe", mmo, **kw)
            if j == nj - 1:
                h, qb = cx["h"], cx["qb"]
                P.op("act", lambda e: e.activation(out=rinv, in_=psm[:, :], func=AF.Ln), reads=[cx["r_psm"]], writes=[r_rinv])
                P.op("act", lambda e: e.activation(out=rinv, in_=rinv, func=AF.Exp, scale=-1.0), reads=[r_rinv], writes=[r_rinv])
                P.op("dve", lambda e: e.tensor_tensor(out=self.hT[:, h, qb * TB:(qb + 1) * TB], in0=po[:, :], in1=rinv, op=ALU.mult),
                     reads=[cx["r_po"], r_rinv], writes=[r_o[h][qb]])

        load_hw(0)
        load_hw(1)
        for st_ in kprep(0, [0, 1, 2, 3]):
            st_()
        ctxs = {}
        qi0, negc0, st0 = qprep(0, 0)
        for st_ in st0:
            st_()
        ctxs[(0, 0)] = blk_ctx(0, 0, qi0, negc0)
        steps = [(h, qb, j) for h in range(8) for qb in range(4) for j in range(4 * qb + 4)]
        start_of = {}
        for k, (h, qb, j) in enumerate(steps):
            if j == 0:
                start_of[(h, qb)] = k
        LOOK = 2
        pending = []
        emit_mms(0, ctxs[(0, 0)], 0)
        emit_mms(1, ctxs[(0, 0)], 1)
        for k, (h, qb, j) in enumerate(steps):
            if j == 0:
                if qb < 3:
                    qi_, negc_, stg_ = qprep(h, qb + 1)
                    ctxs[(h, qb + 1)] = blk_ctx(h, qb + 1, qi_, negc_)
                    pending.append([stg_, 0, start_of[(h, qb + 1)] - LOOK])
                if h < 7:
                    if qb == 1:
                        pending.append([kprep(h + 1, [0, 1]), 0, start_of[(h, 3)] - 1])
                    elif qb == 2:
                        pending.append([kprep(h + 1, [2, 3]), 0, start_of[(h + 1, 0)] - LOOK - 6])
                    elif qb == 3:
                        qi_, negc_, stg_ = qprep(h + 1, 0)
                        ctxs[(h + 1, 0)] = blk_ctx(h + 1, 0, qi_, negc_)
                        pending.append([stg_, 0, start_of[(h + 1, 0)] - LOOK])
                if qb == 3 and h + 2 < 8:
                    load_hw(h + 2)
            pending.sort(key=lambda it: it[2])
            n_adv = 0
            cum = 0
            for it in pending:
                cum += len(it[0]) - it[1]
                n_adv = max(n_adv, -(-cum // max(1, it[2] - k + 1)))
            for _ in range(n_adv):
                it = pending[0]
                it[0][it[1]]()
                it[1] += 1
                if it[1] >= len(it[0]):
                    pending.pop(0)
                    if not pending:
                        break
            if k + LOOK < len(steps):
                h2, qb2, j2 = steps[k + LOOK]
                emit_mms(k + LOOK, ctxs[(h2, qb2)], j2)
            emit_exp_mmo(k, ctxs[(h, qb)], j)
        assert not pending
        it = 0
        for d in range(8):
            ga = self.mod_ap(l, 2, d, s)
            for tb in range(NTB):
                tsl = slice(tb * TB, (tb + 1) * TB)
                po = self.ps[6 + it % 2]
                r_po = self.r_ps[6 + it % 2]
                it += 1

                def mmd(e, po=po, d=d, tsl=tsl):
                    ins = None
                    for h in range(8):
                        ins = e.matmul(po[:, :], lhsT=wo[:, h, d * 128:(d + 1) * 128], rhs=self.hT[:, h, tsl], start=(h == 0), stop=(h == 7))
                    return ins
                P.op("pe", mmd, reads=[r_wo] + [r_o[h][tb] for h in range(8)], writes=[r_po])
                xo = self.xT[:, d, tsl]
                P.op("dve", lambda e, xo=xo, po=po, ga=ga: e.scalar_tensor_tensor(
                    out=xo, in0=po[:, :], scalar=ga, in1=xo, op0=ALU.mult, op1=ALU.add),
                    reads=[r_po, self.r_mod], writes=[self.r_x[d][tb]])


    def mlstm(self, s, l):
        P = self.P
        b = l // 2
        rc = self.r_const
        LNS = float(np.log(128.0 ** -0.5))
        self.areset()
        G1 = self.af([128, S])
        G2 = self.af([128, S])
        qpT = self.ab([128, S])
        kpT = self.ab([128, S])
        yT = self.ab([128, 8, S])
        tokS = self.af([128, 16, 8])
        dec_b = self.af([128, 64])
        hn_h = self.af([128, 256])
        small2 = self.af([128, 64])
        kws = [self.ab([128, 128]) for _ in range(2)]
        vaug = [self.ab([128, 264]) for _ in range(2)]
        eo = [self.af([128, 256]) for _ in range(2)]
        smT = [self.ab([128, 128]) for _ in range(2)]
        Cf = self.af([128, 264])
        Cb = [self.ab([128, 264]) for _ in range(2)]
        yh = [self.ab([128, 256]) for _ in range(2)]
        qs = [self.af([128, 512]) for _ in range(2)]
        junk = qs[0][:, 0:256]
        tri = self.cf[:, CF_TRI:CF_TRI + 128]

        def slot(G, j):
            return G[32 * j:32 * j + 4, :]
        A = [slot(G1, j) for j in range(4)]
        B = [slot(G2, j) for j in range(4)]
        r_A = [Res("A%d" % j) for j in range(4)]
        r_B = [Res("B%d" % j) for j in range(4)]
        wg = [self.WB[:, 8192 + g * 1024:8192 + (g + 1) * 1024].rearrange("p (k n) -> p k n", k=8) for g in range(2)]
        r_wg = Res("wg")
        for g in range(2):
            P.dma("pool", "wg", self.WB[:, 8192 + g * 1024:8192 + (g + 1) * 1024], self.d_mlg[b, g], cowrites=[r_wg])
        r_gz = Res("gz")
        P.op("dve", lambda e: e.memset(G1, 0.0), writes=[r_gz])
        P.op("dve", lambda e: e.memset(G2, 0.0), cowrites=[r_gz])
        r_wh = Res("wh")
        Wh = self.WB[:, 0:7168].rearrange("p (k n) -> p k n", k=8)

        def load_head(h):
            for k2 in range(4):
                kw = dict(writes=[r_wh]) if k2 == 0 else dict(cowrites=[r_wh])
                P.dma("pool", "wh", self.WB[:, k2 * 1792:(k2 + 1) * 1792], self.d_mlw[b, h, :, k2 * 1792:(k2 + 1) * 1792], **kw)
        load_head(0)
        ib = self.pp[0:4, PP_GB + 2 * b:PP_GB + 2 * b + 1]
        fb = self.pp[0:4, PP_GB + 2 * b + 1:PP_GB + 2 * b + 2]
        for tb in range(NTB):
            tsl = slice(tb * TB, (tb + 1) * TB)
            pi = self.ps[(tb % 2) * 2]
            pf = self.ps[(tb % 2) * 2 + 1]
            r_pi = self.r_ps[(tb % 2) * 2]
            r_pf = self.r_ps[(tb % 2) * 2 + 1]

            def mmg(e, pp_=pi, g=0, tsl=tsl):
                ins = None
                for k in range(8):
                    ins = e.matmul(pp_[:, :], lhsT=wg[g][:, k, :], rhs=self.hT[:, k, tsl], start=(k == 0), stop=(k == 7))
                return ins
            P.op("pe", mmg, reads=[r_wg, self.r_h[tb]], writes=[r_pi])
            P.op("pe", lambda e, f=mmg, pf=pf, tsl=tsl: f(e, pf, 1, tsl), reads=[r_wg, self.r_h[tb]], writes=[r_pf])
            kw = dict(writes=[r_A[0]]) if tb == 0 else dict(cowrites=[r_A[0]])
            P.op("act", lambda e, pi=pi, tsl=tsl: e.activation(out=A[0][:, tsl], in_=pi[0:4, :], func=AF.Identity, bias=ib),
                 reads=[r_pi, rc, r_gz], **kw)
            kw = dict(writes=[r_A[1]]) if tb == 0 else dict(cowrites=[r_A[1]])
            P.op("act", lambda e, pf=pf, tsl=tsl: e.activation(out=A[1][:, tsl], in_=pf[0:4, :], func=AF.Identity, bias=fb),
                 reads=[r_pf, rc, r_gz], **kw)
        P.op("act", lambda e: e.activation(out=A[1], in_=A[1], func=AF.Exp, scale=-1.0), reads=[r_A[1]], writes=[r_A[1]])
        P.op("act", lambda e: e.activation(out=A[1], in_=A[1], func=AF.Ln, bias=1.0), reads=[r_A[1]], writes=[r_A[1]])
        P.op("dve", lambda e: e.tensor_tensor_scan(out=B[0], data0=A[1], data1=A[1], initial=0.0, op0=ALU.add, op1=ALU.max),
             reads=[r_A[1]], writes=[r_B[0]])
        P.op("dve", lambda e: e.tensor_tensor(out=B[1], in0=A[0], in1=B[0], op=ALU.add), reads=[r_A[0], r_B[0]], writes=[r_B[1]])
        P.op("dve", lambda e: e.tensor_tensor_scan(out=A[1], data0=B[1], data1=B[1], initial=0.0, op0=ALU.max, op1=ALU.max),
             reads=[r_B[1]], writes=[r_A[1]])
        P.op("dve", lambda e: e.tensor_tensor_scan(out=A[0], data0=B[1], data1=B[1], initial=0.0, op0=ALU.max, op1=ALU.max),
             reads=[r_B[1]], writes=[r_A[0]])

        def v3(ap):
            return ap.rearrange("p (c n) -> p c n", c=16)
        Mxa3 = v3(A[1]); Mxb3 = v3(A[0]); a3 = v3(B[1]); Gn3 = v3(B[0])

        def ref_of(M3):
            return M3[:, 0:15, 127:128].broadcast_to([4, 15, 128])

        def end_of(M3):
            return M3[:, :, 127:128].broadcast_to([4, 16, 128])
        o3 = v3(A[2])
        P.op("dve", lambda e: e.tensor_tensor(out=o3[:, 1:16, :], in0=ref_of(Mxb3), in1=Mxb3[:, 1:16, :], op=ALU.subtract),
             reads=[r_A[0]], writes=[r_A[2]])
        P.op("dve", lambda e: e.tensor_scalar(out=o3[:, 0:1, :], in0=Mxb3[:, 0:1, :], scalar1=-1.0, scalar2=None, op0=ALU.mult),
             reads=[r_A[0]], cowrites=[r_A[2]])
        P.op("act", lambda e: e.activation(out=A[2], in_=A[2], func=AF.Exp), reads=[r_A[2]], writes=[r_A[2]])
        o3b = v3(B[2])
        P.op("dve", lambda e: e.tensor_tensor(out=o3b[:, 1:16, :], in0=a3[:, 1:16, :], in1=ref_of(Mxa3), op=ALU.subtract),
             reads=[r_B[1], r_A[1]], writes=[r_B[2]])
        P.op("dve", lambda e: e.tensor_copy(out=o3b[:, 0:1, :], in_=a3[:, 0:1, :]), reads=[r_B[1]], cowrites=[r_B[2]])
        P.op("act", lambda e: e.activation(out=B[2], in_=B[2], func=AF.Exp, bias=LNS), reads=[r_B[2]], writes=[r_B[2]])
        P.op("dve", lambda e: e.tensor_tensor(out=v3(A[3]), in0=a3, in1=end_of(Mxa3), op=ALU.subtract),
             reads=[r_B[1], r_A[1]], writes=[r_A[3]])
        P.op("act", lambda e: e.activation(out=A[3], in_=A[3], func=AF.Exp, bias=LNS), reads=[r_A[3]], writes=[r_A[3]])
        P.op("dve", lambda e: e.tensor_tensor(out=B[3], in0=B[0], in1=A[0], op=ALU.subtract), reads=[r_B[0], r_A[0]], writes=[r_B[3]])
        P.op("act", lambda e: e.activation(out=B[3], in_=B[3], func=AF.Exp, scale=2.0), reads=[r_B[3]], writes=[r_B[3]])
        r_s2 = Res("small2")
        dr = small2[0:4, 0:16]
        P.op("dve", lambda e: e.memset(small2[:, 0:16], 0.0), writes=[r_s2])
        P.op("dve", lambda e: e.tensor_tensor(out=small2[0:4, 1:16].unsqueeze(2), in0=Mxb3[:, 0:15, 127:128], in1=Mxb3[:, 1:16, 127:128],
                                              op=ALU.subtract), reads=[r_A[0]], cowrites=[r_s2])
        P.op("dve", lambda e: e.tensor_scalar(out=small2[0:4, 0:1].unsqueeze(2), in0=Mxb3[:, 0:1, 127:128], scalar1=-1.0, scalar2=None,
                                              op0=ALU.mult), reads=[r_A[0]], cowrites=[r_s2])
        P.op("act", lambda e: e.activation(out=dr, in_=dr, func=AF.Exp), reads=[r_s2], writes=[r_s2])
        r_dec = Res("dec")
        for h in range(4):
            sel0 = self.cf[:, CF_SELA + h * 128:CF_SELA + (h + 1) * 128]
            P.op("pe", lambda e, sel0=sel0: e.matmul(self.ps[4][:, 0:16], lhsT=sel0, rhs=small2[:, 0:16], start=True, stop=True),
                 reads=[r_s2, rc], writes=[self.r_ps[4]])
            kw = dict(writes=[r_dec]) if h == 0 else dict(cowrites=[r_dec])
            P.op("act", lambda e, h=h: e.activation(out=dec_b[:, h * 16:(h + 1) * 16], in_=self.ps[4][:, 0:16], func=AF.Copy),
                 reads=[self.r_ps[4]], **kw)
        r_tok = Res("tok")
        for c in range(16):
            csl = slice(c * 128, (c + 1) * 128)
            for gi, (G, rG, col) in enumerate(((G1, r_A, 0), (G2, r_B, 4))):
                bk = (2 * c + gi) % 8
                P.op("pe", lambda e, G=G, csl=csl, bk=bk: e.transpose(out=self.ps[bk][:, 0:128], in_=G[:, csl], identity=self.ident_f),
                     reads=[rG[0], rG[1], rG[2], rG[3], rc], writes=[self.r_ps[bk]])
                P.op("act", lambda e, c=c, col=col, bk=bk: e.activation(out=tokS[:, c, col:col + 4], in_=self.ps[bk][:, 96:100], func=AF.Copy),
                     reads=[self.r_ps[bk]], cowrites=[r_tok])
        if s == 0 and l == 1 and self.dbg:
            self.dump("G1", G1, [128, S], F32)
            self.dump("G2", G2, [128, S], F32)
            self.dump("tokS", tokS, [128, 16, 8], F32)
            self.dump("decb", dec_b, [128, 64], F32)
        r_qp = [Res("qp%d" % t) for t in range(NTB)]
        r_kp = [Res("kp%d" % t) for t in range(NTB)]
        r_qs = [Res("qs0"), Res("qs1")]
        r_hn = Res("hn")
        r_kws = [Res("kws0"), Res("kws1")]
        r_va = [Res("va0"), Res("va1")]
        r_eo = [Res("eo0"), Res("eo1")]
        r_sm = [Res("sm0"), Res("sm1")]
        r_Cf = Res("Cf")
        r_Cb = [Res("Cb0"), Res("Cb1")]
        r_yh = [Res("yh0"), Res("yh1")]
        r_junk = r_qs[0]
        r_yT = [[Res("yT%d_%d" % (j, t)) for t in range(NTB)] for j in range(8)]
        r_st = Res("st")
        for vb in range(2):
            P.op("dve", lambda e, vb=vb: e.memset(vaug[vb][:, 256:264], 1.0), cowrites=[r_va[vb]])
        qi = 0
        for h in range(4):
            P.dma("sp", "hn", hn_h, self.d_hn[:, b, h * 256:(h + 1) * 256], writes=[r_hn])
            selh = self.cf[:, CF_SELB + h * 128:CF_SELB + (h + 1) * 128]
            for tb in range(NTB):
                tsl = slice(tb * TB, (tb + 1) * TB)
                for (c0, Gsl, rG, dstT, r_d) in ((0, G1[:, tsl], r_A[2], qpT, r_qp), (128, G2[:, tsl], r_B[2], kpT, r_kp)):
                    pb = self.ps[(qi % 2) * 2]
                    pq = self.ps[(qi % 2) * 2 + 1]
                    r_pb = self.r_ps[(qi % 2) * 2]
                    r_pq = self.r_ps[(qi % 2) * 2 + 1]
                    sc = qs[qi % 2]
                    r_sc = r_qs[qi % 2]
                    qi += 1
                    P.op("pe", lambda e, pb=pb, Gsl=Gsl, selh=selh: e.matmul(pb[:, :], lhsT=selh, rhs=Gsl, start=True, stop=True),
                         reads=[rG, rc], writes=[r_pb])

                    def mmq(e, pq=pq, c0=c0, tsl=tsl):
                        ins = None
                        for k in range(8):
                            ins = e.matmul(pq[:, :], lhsT=Wh[:, k, c0:c0 + 128], rhs=self.hT[:, k, tsl], start=(k == 0), stop=(k == 7))
                        return ins
                    P.op("pe", mmq, reads=[r_wh, self.r_h[tb]], writes=[r_pq])
                    P.op("act", lambda e, sc=sc, pq=pq: e.activation(out=sc, in_=pq[:, :], func=AF.Copy), reads=[r_pq], writes=[r_sc])
                    P.op("dve", lambda e, o=dstT[:, tsl], sc=sc, pb=pb: e.tensor_tensor(out=o, in0=sc, in1=pb[:, :], op=ALU.mult),
                         reads=[r_sc, r_pb], writes=[r_d[tb]])
            def proj(c, part, h=h):
                t0 = c * 128
                ia = c % 2
                io = 2
                pa = self.ps[ia]
                po = self.ps[io]

                def mma(e, pa=pa, t0=t0):
                    ins = None
                    for k in range(8):
                        ins = e.matmul(pa[:, 0:384], lhsT=self.hT[:, k, t0:t0 + 128], rhs=Wh[:, k, 256:640], start=(k == 0), stop=(k == 7))
                    return ins

                def mmo(e, po=po, t0=t0):
                    ins = None
                    for k in range(8):
                        ins = e.matmul(po[:, 0:256], lhsT=self.hT[:, k, t0:t0 + 128], rhs=Wh[:, k, 640:896], start=(k == 0), stop=(k == 7))
                    return ins
                if part == "pe":
                    P.op("pe", mma, reads=[r_wh, self.r_h[c // 4]], writes=[self.r_ps[ia]])
                    P.op("pe", mmo, reads=[r_wh, self.r_h[c // 4]], writes=[self.r_ps[io]])
                    return
                cb_ = c % 2
                wsc = tokS[:, c, h:h + 1]
                P.op("act", lambda e, pa=pa, cb_=cb_, wsc=wsc: e.activation(out=kws[cb_], in_=pa[:, 0:128], func=AF.Copy, scale=wsc),
                     reads=[self.r_ps[ia], r_tok], writes=[r_kws[cb_]])
                P.op("act", lambda e, pa=pa, cb_=cb_: e.activation(out=vaug[cb_][:, 0:256], in_=pa[:, 128:384], func=AF.Copy),
                     reads=[self.r_ps[ia]], cowrites=[r_va[cb_]])
                P.op("act", lambda e, po=po, cb_=cb_: e.activation(out=eo[cb_], in_=po[:, 0:256], func=AF.Exp, scale=-1.0),
                     reads=[self.r_ps[io]], writes=[r_eo[cb_]])
                P.op("act", lambda e, cb_=cb_: e.activation(out=eo[cb_], in_=eo[cb_], func=AF.Ln, bias=1.0),
                     reads=[r_eo[cb_]], writes=[r_eo[cb_]])
                P.op("act", lambda e, cb_=cb_: e.activation(out=eo[cb_], in_=eo[cb_], func=AF.Exp, scale=-1.0),
                     reads=[r_eo[cb_]], writes=[r_eo[cb_]])
                P.op("dve", lambda e, cb_=cb_: e.tensor_tensor(out=eo[cb_], in0=eo[cb_], in1=hn_h, op=ALU.mult),
                     reads=[r_eo[cb_], r_hn], writes=[r_eo[cb_]])

            pT = self.ps[3].bitcast(BF16)

            def recur(c, part, h=h):
                t0 = c * 128
                csl = slice(t0, t0 + 128)
                tbq = c // 4
                cb_ = c % 2
                pn = self.ps[4 + cb_]
                r_pn = self.r_ps[4 + cb_]
                if part == "a":
                    P.op("pe", lambda e: e.matmul(self.ps[3][:, 0:128], lhsT=kpT[:, csl], rhs=qpT[:, csl], start=True, stop=True),
                         reads=[r_kp[tbq], r_qp[tbq]], writes=[self.r_ps[3]])
                    return
                if part == "b":
                    P.op("dve", lambda e: e.tensor_tensor(out=smT[cb_], in0=self.ps[3][:, 0:128], in1=tri, op=ALU.mult),
                         reads=[self.r_ps[3], rc], writes=[r_sm[cb_]])
                Cprev = Cb[(c + 1) % 2]

                def mmn(e):
                    if c > 0:
                        e.matmul(pn[:, 0:257], lhsT=qpT[:, csl], rhs=Cprev[:, 0:257], start=True, stop=False)
                    return e.matmul(pn[:, 0:257], lhsT=smT[cb_], rhs=vaug[cb_][:, 0:257], start=(c == 0), stop=True)
                rd = [r_qp[tbq], r_sm[cb_], r_va[cb_]]
                if c > 0:
                    rd.append(r_Cb[(c + 1) % 2])
                if part == "b":
                    P.op("pe", mmn, reads=rd, writes=[r_pn])
                if part == "b" and c < 15:
                    P.op("pe", lambda e: e.matmul(self.ps[6][:, 0:257], lhsT=kws[cb_], rhs=vaug[cb_][:, 0:257], start=True, stop=True),
                         reads=[r_kws[cb_], r_va[cb_]], writes=[self.r_ps[6]])
                    if c == 0:
                        P.op("dve", lambda e: e.tensor_copy(out=Cb[cb_][:, 0:257], in_=self.ps[6][:, 0:257]), reads=[self.r_ps[6]], writes=[r_Cb[cb_]])
                        P.op("dve", lambda e: e.tensor_copy(out=Cf[:, 0:257], in_=self.ps[6][:, 0:257]), reads=[self.r_ps[6]], writes=[r_Cf])
                    else:
                        dsc = dec_b[:, h * 16 + c:h * 16 + c + 1]
                        P.op("dve", lambda e: e.scalar_tensor_tensor(out=Cb[cb_][:, 0:257], in0=Cf[:, 0:257], scalar=dsc, in1=self.ps[6][:, 0:257],
                                                                     op0=ALU.mult, op1=ALU.add),
                             reads=[self.r_ps[6], r_Cf, r_dec], writes=[r_Cb[cb_]])
                        if c < 14:
                            P.op("dve", lambda e: e.scalar_tensor_tensor(out=Cf[:, 0:257], in0=Cf[:, 0:257], scalar=dsc, in1=self.ps[6][:, 0:257],
                                                                         op0=ALU.mult, op1=ALU.add),
                                 reads=[self.r_ps[6], r_Cf, r_dec], writes=[r_Cf])
                if part == "b":
                    return
                st = small2[:, 16 + 8 * cb_:24 + 8 * cb_]
                emt2 = tokS[:, c, 4 + h:5 + h]
                P.op("act", lambda e: e.activation(out=junk, in_=pn[:, 0:256], func=AF.Square, accum_out=st[:, 2:3]),
                     reads=[r_pn], writes=[r_junk], cowrites=[r_st])
                P.op("act", lambda e: e.activation(out=st[:, 6:7], in_=pn[:, 256:257], func=AF.Square),
                     reads=[r_pn], cowrites=[r_st])
                P.op("dve", lambda e: e.tensor_scalar(out=st[:, 0:1], in0=st[:, 6:7], scalar1=emt2, scalar2=EPS, op0=ALU.max, op1=ALU.mult),
                     reads=[r_st, r_tok], cowrites=[r_st])
                P.op("dve", lambda e: e.scalar_tensor_tensor(out=st[:, 1:2], in0=st[:, 2:3], scalar=1.0 / 256, in1=st[:, 0:1],
                                                             op0=ALU.mult, op1=ALU.add), reads=[r_st], cowrites=[r_st])
                P.op("act", lambda e: e.activation(out=st[:, 3:4], in_=st[:, 1:2], func=AF.Ln), reads=[r_st], cowrites=[r_st])
                P.op("act", lambda e: e.activation(out=st[:, 5:6], in_=st[:, 3:4], func=AF.Exp, scale=-0.5), reads=[r_st], cowrites=[r_st])
                P.op("dve", lambda e: e.scalar_tensor_tensor(out=yh[cb_], in0=pn[:, 0:256], scalar=st[:, 5:6], in1=eo[cb_],
                                                             op0=ALU.mult, op1=ALU.mult),
                     reads=[r_pn, r_st, r_eo[cb_]], writes=[r_yh[cb_]])

            def ytrans(c, h=h):
                t0 = c * 128
                csl = slice(t0, t0 + 128)
                tbq = c // 4
                cb_ = c % 2
                for j in range(2):
                    P.op("pe", lambda e, j=j: e.matmul(self.ps[7][:, 128 + j * 128:128 + (j + 1) * 128], lhsT=yh[cb_][:, j * 128:(j + 1) * 128],
                                                       rhs=self.ident_b, start=True, stop=True),
                         reads=[r_yh[cb_], rc], **(dict(writes=[self.r_ps[7]]) if j == 0 else dict(cowrites=[self.r_ps[7]])))
                P.op("act", lambda e: e.activation(out=yT[:, 2 * h:2 * h + 2, csl], in_=self.ps[7][:, 128:384].rearrange("p (a b) -> p a b", a=2), func=AF.Copy),
                     reads=[self.r_ps[7]], cowrites=[r_yT[2 * h][tbq], r_yT[2 * h + 1][tbq]])
            proj(0, "pe")
            proj(0, "ev")
            for c in range(16):
                recur(c, "a")
                if c + 1 < 16:
                    proj(c + 1, "pe")
                recur(c, "b")
                if c + 1 < 16:
                    proj(c + 1, "ev")
                recur(c, "c")
                if c > 0:
                    ytrans(c - 1)
            ytrans(15)
            if h + 1 < 4:
                load_head(h + 1)
        if s == 0 and l == 1 and self.dbg == 2:
            self.dump("qpT", qpT, [128, S], BF16)
            self.dump("kpT", kpT, [128, S], BF16)
            for j in range(8):
                self.dump("yT%d" % j, yT[:, j, :], [128, S], BF16)
            self.dump("small2", small2, [128, 64], F32)
        wo = self.WB[:, 0:8192].rearrange("p (h n) -> p h n", h=8)
        r_wo = Res("wo")
        for h2 in range(4):
            kw = dict(writes=[r_wh, r_wo]) if h2 == 0 else dict(cowrites=[r_wo])
            P.dma("pool", "wh", self.WB[:, h2 * 2048:(h2 + 1) * 2048], self.d_mlo[b, :, h2 * 2048:(h2 + 1) * 2048], **kw)
        it = 0
        for d in range(8):
            ga = self.mod_ap(l, 2, d, s)
            for tb in range(NTB):
                tsl = slice(tb * TB, (tb + 1) * TB)
                po = self.ps[it % 2]
                r_po = self.r_ps[it % 2]
                it += 1

                def mmd(e, po=po, d=d, tsl=tsl):
                    ins = None
                    for j in range(8):
                        ins = e.matmul(po[:, :], lhsT=wo[:, j, d * 128:(d + 1) * 128], rhs=yT[:, j, tsl], start=(j == 0), stop=(j == 7))
                    return ins
                P.op("pe", mmd, reads=[r_wo] + [r_yT[j][tb] for j in range(8)], writes=[r_po])
                xo = self.xT[:, d, tsl]
                P.op("dve", lambda e, xo=xo, po=po, ga=ga: e.scalar_tensor_tensor(
                    out=xo, in0=po[:, :], scalar=ga, in1=xo, op0=ALU.mult, op1=ALU.add),
                    reads=[r_po, self.r_mod], writes=[self.r_x[d][tb]])


def _consts():
    cf = np.zeros((128, NCF), np.float32)
    cf[:, CF_ID:CF_ID + 128] = np.eye(128, dtype=np.float32)
    cf[:, CF_ONE:CF_ONE + 128] = 1.0
    cf[:, CF_TRI:CF_TRI + 128] = np.triu(np.ones((128, 128), np.float32))
    for h in range(4):
        cf[h, CF_SEL + h * 128:CF_SEL + (h + 1) * 128] = 1.0
        cf[64 + h, CF_SEL + h * 128:CF_SEL + (h + 1) * 128] = 1.0
        cf[h, CF_SELA + h * 128:CF_SELA + (h + 1) * 128] = 1.0
        cf[64 + h, CF_SELB + h * 128:CF_SELB + (h + 1) * 128] = 1.0
    cb = np.zeros((128, NCB), np.float32)
    cb[:, CB_ID:CB_ID + 128] = np.eye(128, dtype=np.float32)
    cb[:, CB_ONE:CB_ONE + 128] = 1.0
    cb[0, CB_MROW + 64:CB_MROW + 128] = -30000.0
    return cf, cb.astype(ml_dtypes.bfloat16)


def _col(v, nchunk):
    return np.ascontiguousarray(np.asarray(v, np.float32).reshape(nchunk, 128).T)


def prep_shared(inp):
    f32 = np.float32
    sh = {}
    pp = np.zeros((128, NPP), f32)
    for l in range(DEPTH):
        pp[:, PP_MODB + l * 48:PP_MODB + (l + 1) * 48] = _col(inp["mod_b"][l], 48)
        for tap in range(3):
            pp[:, PP_CW + l * 66 + tap * 22:PP_CW + l * 66 + (tap + 1) * 22] = _col(inp["ffn_conv_w"][l, tap], 22)
        pp[:, PP_CB + l * 22:PP_CB + (l + 1) * 22] = _col(inp["ffn_conv_b"][l], 22)
    for a in range(2):
        pp[:, PP_QN + a * 4:PP_QN + (a + 1) * 4] = _col(inp["mla_q_norm"][a], 4)
        pp[:, PP_KVN + a * 2:PP_KVN + (a + 1) * 2] = _col(inp["mla_kv_norm"][a], 2)
        pp[0:4, PP_GB + a * 2] = np.asarray(inp["ml_b_gates"][a][0:4], f32)
        pp[0:4, PP_GB + a * 2 + 1] = np.asarray(inp["ml_b_gates"][a][4:8], f32)
    pp[:, PP_FN:PP_FN + 8] = _col(inp["final_norm"], 8)
    sh["pp"] = pp
    cf, cb = _consts()
    sh["cf"] = cf
    sh["cb"] = cb
    mw = np.asarray(inp["mod_w"], f32)
    sh["modw"] = np.ascontiguousarray(mw.reshape(DEPTH, 8, 128, 12, 512).transpose(0, 3, 2, 1, 4))
    hn = np.asarray(inp["ml_head_norm"], f32)
    sh["hn"] = np.ascontiguousarray(np.broadcast_to(hn[None, :, :], (128, 2, 1024)))
    wu = np.asarray(inp["ffn_w_up"], f32)
    wa = wu[:, :, :DFF].reshape(DEPTH, 8, 128, NFC, 128)
    wg = wu[:, :, DFF:].reshape(DEPTH, 8, 128, NFC, 128)
    fup = np.concatenate([wa, wg], axis=-1)
    sh["fup"] = np.ascontiguousarray(fup.transpose(0, 3, 2, 1, 4))
    wd = np.asarray(inp["ffn_w_down"], f32)
    wd = wd.reshape(DEPTH, 2, 11, 128, 8, 128)
    sh["fdn"] = np.ascontiguousarray(wd.transpose(0, 1, 4, 3, 2, 5))
    jj = np.arange(0, 64, 2, dtype=f32) / f32(64)
    invf = (f32(1.0) / (f32(10000.0) ** jj)).astype(f32)
    pp[0:64, PP_IF] = np.concatenate([invf, invf])
    win = np.asarray(inp["mla_w_in"], f32)
    win = np.concatenate([win, win[:, :, 800:832], win[:, :, 768:800]], axis=-1)
    sh["mwin"] = np.ascontiguousarray(win.reshape(2, 8, 128, 896).transpose(0, 2, 1, 3).reshape(2, 128, 8 * 896))
    wq = np.asarray(inp["mla_w_q_up"], f32).reshape(2, 4, 128, 8, 192)
    wq = np.concatenate([wq, wq[..., 160:192], wq[..., 128:160]], axis=-1)
    sh["mwq"] = np.ascontiguousarray(wq.transpose(0, 3, 2, 1, 4).reshape(2, 8, 128, 4 * 256))
    wkv = np.asarray(inp["mla_w_kv_up"], f32).reshape(2, 2, 128, 8, 256)
    sh["mwkv"] = np.ascontiguousarray(wkv.transpose(0, 3, 2, 1, 4).reshape(2, 8, 128, 2 * 256))
    wo = np.asarray(inp["mla_w_out"], f32).reshape(2, 8, 128, 1024)
    sh["mwo"] = np.ascontiguousarray(wo.transpose(0, 2, 1, 3).reshape(2, 128, 8 * 1024))
    mw_ = np.asarray(inp["ml_w_in"], f32).reshape(2, 8, 128, 3080)
    heads = []
    for h in range(4):
        heads.append(np.concatenate([mw_[..., h * 128:(h + 1) * 128], mw_[..., 512 + h * 128:512 + (h + 1) * 128],
                                     mw_[..., 512 + h * 128:512 + (h + 1) * 128],
                                     mw_[..., 1024 + h * 256:1024 + (h + 1) * 256],
                                     mw_[..., 2048 + h * 256:2048 + (h + 1) * 256]], axis=-1))
    mlw = np.stack(heads, axis=1)
    sh["mlw"] = np.ascontiguousarray(mlw.transpose(0, 1, 3, 2, 4).reshape(2, 4, 128, 8 * 896))
    mlg = np.zeros((2, 2, 128, 8, 128), f32)
    for g in range(2):
        mlg[:, g, :, :, 0:4] = mw_[..., 3072 + 4 * g:3076 + 4 * g].transpose(0, 2, 1, 3)
    sh["mlg"] = np.ascontiguousarray(mlg.reshape(2, 2, 128, 8 * 128))
    mo = np.asarray(inp["ml_w_out"], f32).reshape(2, 8, 128, 1024)
    sh["mlo"] = np.ascontiguousarray(mo.transpose(0, 2, 1, 3).reshape(2, 128, 8 * 1024))
    return sh


def prep_core(inp, core, n_seq=2):
    b0 = core * n_seq
    x = np.asarray(inp["x"][b0:b0 + n_seq], np.float32)
    xT = np.ascontiguousarray(x.reshape(n_seq, S, 8, 128).transpose(0, 3, 2, 1))
    c = np.asarray(inp["c"][b0:b0 + n_seq], np.float32)
    cT = np.ascontiguousarray(c.reshape(n_seq, 8, 128).transpose(2, 1, 0))
    pos = np.ascontiguousarray(np.asarray(inp["positions"][b0:b0 + n_seq], np.int32))
    return {"xT": xT, "cT": cT, "pos": pos}


def unpack_out(outT):
    ns = outT.shape[0]
    return np.ascontiguousarray(outT.transpose(0, 3, 2, 1).reshape(ns, S, D))


_CACHE = {}


def get_nc(stages=None, dbg=False):
    key = (tuple(stages) if stages is not None else None, dbg)
    if key not in _CACHE:
        b = Builder(stages=stages, dbg=dbg)
        nc = b.build()
        _CACHE[key] = (nc, b)
    return _CACHE[key]


def run(inp, cores=range(8), stages=None, trace=False, dbg=False):
    nc, b = get_nc(stages, dbg)
    sh = prep_shared(inp)
    in_maps = []
    for c in cores:
        m = dict(sh)
        m.update(prep_core(inp, c))
        in_maps.append(m)
    res = run_bass_kernel_spmd(nc, in_maps, core_ids=list(range(len(in_maps))), trace=trace)
    outs = [unpack_out(r["outT"]) for r in res.results]
    return np.concatenate(outs, axis=0), res


def kernel(**inputs):
    out, _ = run(inputs)
    return out.astype(np.float32)
```

```python
import numpy as np
import ml_dtypes
from contextlib import ExitStack
import concourse.bass as bass
import concourse.mybir as mybir
from concourse.bass_utils import run_bass_kernel_spmd

F32 = mybir.dt.float32
BF16 = mybir.dt.bfloat16
F32R = mybir.dt.float32r
I32 = mybir.dt.int32
AF = mybir.ActivationFunctionType
ALU = mybir.AluOpType
AX = mybir.AxisListType

D = 1024
S = 2048
DEPTH = 4
DFF = 2816
NFC = 22
FF_SPLIT = (11, 11)
EPS = 1e-6
TB = 512
NTB = S // TB

PP_MODB = 0
PP_QN = 192
PP_KVN = 200
PP_CW = 204
PP_CB = 468
PP_FN = 556
PP_GB = 564
PP_IF = 568
NPP = 576
CF_ID = 0
CF_ONE = 128
CF_TRI = 256
CF_SEL = 384
CF_SELA = 896
CF_SELB = 1408
NCF = 1920
CB_ID = 0
CB_ONE = 128
CB_MROW = 256
NCB = 384


class Res:
    __slots__ = ("name", "writers", "readers")

    def __init__(self, name):
        self.name = name
        self.writers = []
        self.readers = []


class Prog:
    ENG = ("pe", "act", "dve", "pool", "sp")

    def __init__(self, nc, stack):
        self.nc = nc
        self.stack = stack
        self.q = {e: [] for e in self.ENG}
        self.n = {e: 0 for e in self.ENG}
        self.seen = {e: {} for e in self.ENG}
        self.needed = {e: set() for e in self.ENG}
        self.csem = {e: stack.enter_context(nc.semaphore("cs_" + e)) for e in self.ENG}
        self.dsem = {}
        self.dtot = {}

    def _prune(self, eng, waits, raw_set):
        best = {}
        for ev in waits:
            if ev[0] == "c":
                if ev[1] == eng and id(ev) not in raw_set:
                    continue
                key = ("c", ev[1])
            else:
                key = ("d", ev[1])
            if ev[2] > best.get(key, 0):
                best[key] = ev[2]
        out = []
        seen = self.seen[eng]
        for key, val in best.items():
            if val <= seen.get(key, 0):
                continue
            seen[key] = val
            out.append((key[0], key[1], val))
            if key[0] == "c":
                self.needed[key[1]].add(val)
        return out

    @staticmethod
    def _deps(reads, writes, cowrites):
        waits = []
        raw = set()
        for r in reads:
            for ev in r.writers:
                waits.append(ev)
                raw.add(id(ev))
        for r in writes:
            waits += r.writers
            waits += r.readers
        for r in cowrites:
            waits += r.readers
        return waits, raw

    @staticmethod
    def _update(ev, reads, writes, cowrites):
        for r in reads:
            r.readers.append(ev)
        for r in writes:
            r.writers = [ev]
            r.readers = []
        for r in cowrites:
            r.writers.append(ev)

    def op(self, eng, fn, reads=(), writes=(), cowrites=()):
        waits, raw = self._deps(reads, writes, cowrites)
        w = self._prune(eng, waits, raw)
        self.n[eng] += 1
        ev = ("c", eng, self.n[eng])
        self.q[eng].append((fn, w, ev))
        self._update(ev, reads, writes, cowrites)
        return ev

    def dma(self, eng, sem, out, in_, reads=(), writes=(), cowrites=(), **kw):
        if sem not in self.dsem:
            self.dsem[sem] = self.stack.enter_context(self.nc.semaphore("ds_" + sem))
            self.dtot[sem] = 0
        waits, raw = self._deps(reads, writes, cowrites)
        w = self._prune(eng, waits, raw)
        self.dtot[sem] += 16
        ev = ("d", sem, self.dtot[sem])
        self.q[eng].append((lambda e: e.dma_start(out=out, in_=in_, **kw), w, ev))
        self._update(ev, reads, writes, cowrites)
        return ev

    def barrier(self, clear=()):
        evs = []
        for e in self.ENG:
            if self.n[e] > 0:
                evs.append(("c", e, self.n[e]))
        for s, t in self.dtot.items():
            if t > 0:
                evs.append(("d", s, t))
        for e in self.ENG:
            w = self._prune(e, [ev for ev in evs if not (ev[0] == "c" and ev[1] == e)], set())
            if w:
                self.q[e].append((None, w, None))
        for r in clear:
            r.writers = []
            r.readers = []

    def finish(self, final_events):
        w = self._prune("sp", list(final_events), set(id(e) for e in final_events))
        self.q["sp"].append((None, w, None))
        nc = self.nc
        rank = {}
        for e in self.ENG:
            rank[e] = {idx: i + 1 for i, idx in enumerate(sorted(self.needed[e]))}
        self.stats = {e: len(self.q[e]) for e in self.ENG}
        self.stats["incs"] = {e: len(rank[e]) for e in self.ENG}

        def replay(ename, eobj):
            for fn, waits, ev in self.q[ename]:
                for (kind, key, val) in waits:
                    if kind == "c":
                        eobj.wait_ge(self.csem[key], rank[key][val])
                    else:
                        eobj.wait_ge(self.dsem[key], val)
                if fn is None:
                    continue
                ins = fn(eobj)
                if ev is None:
                    continue
                if ev[0] == "c":
                    if ev[2] in rank[ename]:
                        ins.then_inc(self.csem[ename], 1)
                else:
                    ins.then_inc(self.dsem[ev[1]], 16)

        with nc.Block() as block:
            @block.tensor
            def _(t):
                replay("pe", t)

            @block.scalar
            def _(t):
                replay("act", t)

            @block.vector
            def _(t):
                replay("dve", t)

            @block.gpsimd
            def _(t):
                replay("pool", t)

            @block.sync
            def _(t):
                replay("sp", t)


class Builder:
    def __init__(self, n_seq=2, stages=None, dbg=False):
        self.n_seq = n_seq
        self.stages = stages
        self.dbg = dbg
        self.nc = bass.Bass("TRN2", target_bir_lowering=False)
        self.stack = ExitStack()

    def dram_in(self, name, shape, dt):
        return self.nc.dram_tensor(name, list(shape), dt, kind="ExternalInput").ap()

    def dump(self, name, ap, shape, dt):
        if not self.dbg:
            return
        d = self.nc.dram_tensor("dbg_" + name, list(shape), dt, kind="ExternalOutput").ap()
        self.P.barrier()
        self.P.dma("sp", "dbg", d, ap)
        self.P.barrier()

    def sb(self, name, shape, dt):
        return self.stack.enter_context(self.nc.sbuf_tensor(name, list(shape), dt))

    def build(self):
        nc = self.nc
        with self.stack:
            self.P = Prog(nc, self.stack)
            self._declare()
            self._program()
        return nc

    def _declare(self):
        nc = self.nc
        ns = self.n_seq
        self.d_xT = self.dram_in("xT", [ns, 128, 8, S], F32)
        self.d_cT = self.dram_in("cT", [128, 8, ns], F32)
        self.d_pos = self.dram_in("pos", [ns, S], I32)
        self.d_modw = self.dram_in("modw", [DEPTH, 12, 128, 8, 512], F32)
        self.d_pp = self.dram_in("pp", [128, NPP], F32)
        self.d_cf = self.dram_in("cf", [128, NCF], F32)
        self.d_cb = self.dram_in("cb", [128, NCB], BF16)
        self.d_hn = self.dram_in("hn", [128, 2, 1024], F32)
        self.d_fup = self.dram_in("fup", [DEPTH, NFC, 128, 8, 256], F32)
        self.d_fdn = self.dram_in("fdn", [DEPTH, 2, 8, 128, 11, 128], F32)
        self.d_win = self.dram_in("mwin", [2, 128, 8 * 896], F32)
        self.d_wq = self.dram_in("mwq", [2, 8, 128, 4 * 256], F32)
        self.d_wkv = self.dram_in("mwkv", [2, 8, 128, 2 * 256], F32)
        self.d_wo = self.dram_in("mwo", [2, 128, 8 * 1024], F32)
        self.d_mlw = self.dram_in("mlw", [2, 4, 128, 8 * 896], F32)
        self.d_mlg = self.dram_in("mlg", [2, 2, 128, 8 * 128], F32)
        self.d_mlo = self.dram_in("mlo", [2, 128, 8 * 1024], F32)
        self.d_out = nc.dram_tensor("outT", [ns, 128, 8, S], F32, kind="ExternalOutput").ap()

        self.xT = self.sb("xT_sb", [128, 8, S], F32)
        self.hT = self.sb("hT_sb", [128, 8, S], BF16)
        self.pp = self.sb("pp_sb", [128, NPP], F32)
        self.cf = self.sb("cf_sb", [128, NCF], F32)
        self.cb = self.sb("cb_sb", [128, NCB], BF16)
        self.modT = self.sb("modT", [128, DEPTH, 48, 2], F32)
        self.cact = self.sb("cact", [128, 8, 2], F32)
        self.sqr = self.sb("sqr", [128, 2, 512], BF16)
        self.tab = self.sb("ropetab", [128, S], BF16)
        self.r_tab = Res("tab")
        self.WB = self.sb("WB", [128, 12288], BF16)
        self.AR = self.sb("AR", [128, 17792], F32)
        self.ps = [self.stack.enter_context(nc.psum_tensor("ps%d" % i, [128, 512], F32)) for i in range(8)]
        self.r_ps = [Res("ps%d" % i) for i in range(8)]
        self.r_x = [[Res("x%d_%d" % (k, t)) for t in range(NTB)] for k in range(8)]
        self.r_h = [Res("h%d" % t) for t in range(NTB)]
        self.r_const = Res("const")
        self.r_mod = Res("mod")
        self.ident_f = self.cf[:, CF_ID:CF_ID + 128]
        self.ones_f = self.cf[:, CF_ONE:CF_ONE + 128]
        self.ident_b = self.cb[:, CB_ID:CB_ID + 128]
        self.ones_b = self.cb[:, CB_ONE:CB_ONE + 128]

    def ar_f32(self, off_kib, shape):
        n = int(np.prod(shape[1:]))
        o = int(off_kib * 256)
        ap = self.AR[0:shape[0], o:o + n]
        if len(shape) == 3:
            ap = ap.rearrange("p (a b) -> p a b", a=shape[1])
        return ap

    def ar_bf16(self, off_kib, shape):
        n = int(np.prod(shape[1:]))
        o = int(off_kib * 512)
        ap = self.AR.bitcast(BF16)[0:shape[0], o:o + n]
        if len(shape) == 3:
            ap = ap.rearrange("p (a b) -> p a b", a=shape[1])
        return ap

    def areset(self):
        self._acur = 0

    def af(self, shape):
        n = int(np.prod(shape[1:]))
        n = (n + 7) // 8 * 8
        o = self._acur
        self._acur += n
        assert self._acur <= 17792, self._acur
        ap = self.AR[0:shape[0], o:o + int(np.prod(shape[1:]))]
        if len(shape) == 3:
            ap = ap.rearrange("p (a b) -> p a b", a=shape[1])
        return ap

    def ab(self, shape):
        n = int(np.prod(shape[1:]))
        nf = (n + 15) // 16 * 8
        o = self._acur * 2
        self._acur += nf
        assert self._acur <= 17792, self._acur
        ap = self.AR.bitcast(BF16)[0:shape[0], o:o + n]
        if len(shape) == 3:
            ap = ap.rearrange("p (a b) -> p a b", a=shape[1])
        return ap

    def mod_ap(self, l, kind, i, s):
        return self.modT[:, l, kind * 8 + i, s:s + 1]

    def _program(self):
        P = self.P
        st = self.stages
        self.load_x(0)
        self.prologue()
        P.barrier()
        last = []
        for s in range(self.n_seq):
            if s > 0:
                self.load_x(s)
                P.barrier()
            for l in range(DEPTH):
                if st is None or ("mix%d" % l) in st:
                    self.norm_mod(s, l, 1, 0)
                    P.barrier()
                    if l % 2 == 0:
                        self.mla(s, l)
                    else:
                        self.mlstm(s, l)
                    P.barrier()
                if st is None or ("ffn%d" % l) in st:
                    self.norm_mod(s, l, 4, 3)
                    P.barrier()
                    self.ffn(s, l)
                    P.barrier()
            last += self.final_store(s)
            P.barrier()
        P.finish(last)

    def prologue(self):
        P = self.P
        rc = self.r_const
        P.dma("sp", "c0", self.pp[:, :], self.d_pp[:, :], cowrites=[rc])
        P.dma("sp", "c0", self.cf[:, :], self.d_cf[:, :], cowrites=[rc])
        P.dma("sp", "c0", self.cb[:, :], self.d_cb[:, :], cowrites=[rc])
        P.dma("sp", "c0", self.cact[:, :, :], self.d_cT[:, :, 0:2], cowrites=[rc])
        r_ca = Res("cact")
        cact2 = self.cact[:, :, :]
        P.op("act", lambda e: e.activation(out=cact2, in_=cact2, func=AF.Silu), reads=[rc], writes=[r_ca])
        NSTG = 4
        stg = [self.ar_bf16(8 * i, [128, 8, 512]) for i in range(NSTG)]
        r_stg = [Res("stg%d" % i) for i in range(NSTG)]
        cact_b = self.ar_bf16(40, [128, 8, 2])
        P.op("dve", lambda e: e.tensor_copy(out=cact_b, in_=self.cact[:, :, :]), reads=[r_ca], writes=[r_ca])
        it = 0
        for l in range(DEPTH):
            pst = self.ps[l % 2]
            r_p = self.r_ps[l % 2]
            for cbk in range(12):
                sl = it % NSTG
                it += 1
                for k2 in range(2):
                    kw = dict(writes=[r_stg[sl]]) if k2 == 0 else dict(cowrites=[r_stg[sl]])
                    P.dma("pool", "stg%d" % sl, stg[sl][:, k2 * 4:(k2 + 1) * 4, :], self.d_modw[l, cbk, :, k2 * 4:(k2 + 1) * 4, :], **kw)

                def mm(e, sl=sl, cbk=cbk, pst=pst):
                    ins = None
                    for j in range(4):
                        col = (cbk * 4 + j) * 2
                        for k in range(8):
                            ins = e.matmul(pst[:, col:col + 2], lhsT=stg[sl][:, k, j * 128:(j + 1) * 128],
                                           rhs=cact_b[:, k, 0:2], start=(k == 0), stop=(k == 7))
                    return ins
                if cbk == 0:
                    P.op("pe", mm, reads=[r_stg[sl], r_ca], writes=[r_p])
                else:
                    P.op("pe", mm, reads=[r_stg[sl], r_ca], cowrites=[r_p])
            mo = self.modT[:, l, :, :]
            pin = pst[:, 0:96].rearrange("p (a b) -> p a b", a=48)
            bb = self.pp[:, PP_MODB + l * 48:PP_MODB + (l + 1) * 48].unsqueeze(2).broadcast_to([128, 48, 2])
            P.op("dve", lambda e, mo=mo, pin=pin, bb=bb: e.tensor_tensor(out=mo, in0=pin, in1=bb, op=ALU.add),
                 reads=[r_p, rc], cowrites=[self.r_mod])
            for kind in (1, 4):
                m1 = self.modT[:, l, kind * 8:(kind + 1) * 8, :]
                P.op("dve", lambda e, m1=m1: e.tensor_scalar(out=m1, in0=m1, scalar1=1.0, scalar2=None, op0=ALU.add),
                     reads=[self.r_mod], cowrites=[self.r_mod])

    def load_x(self, s):
        P = self.P
        for k in range(8):
            P.dma("sp", "xld%d" % k, self.xT[:, k, :], self.d_xT[s, :, k, :],
                  writes=[self.r_x[k][t] for t in range(NTB)])

    def final_store(self, s):
        P = self.P
        evs = []
        self._norm_core(s, final=True)
        return self._final_evs

    def norm_mod(self, s, l, kind_sc, kind_sh):
        self._norm_core(s, l=l, kind_sc=kind_sc, kind_sh=kind_sh)

    def _norm_core(self, s, l=None, kind_sc=None, kind_sh=None, final=False):
        P = self.P
        sq = [self.sqr[:, 0, :], self.sqr[:, 1, :]]
        r_sq = [Res("sq0"), Res("sq1")]
        lnv = [self.ar_f32(4, [128, 512]), self.ar_f32(6, [128, 512])]
        r_ln = [Res("ln0"), Res("ln1")]
        rstd = [self.ar_f32(8, [128, 512]), self.ar_f32(10, [128, 512])]
        r_rs = [Res("rs0"), Res("rs1")]
        tmp = [self.ar_f32(12 + 2 * i, [128, 512]) for i in range(4)]
        r_tmp = [Res("tmp%d" % i) for i in range(4)]
        ones_r = self.ones_b
        self._final_evs = []
        it = [0]

        def stats(tb):
            tsl = slice(tb * TB, (tb + 1) * TB)
            pb = tb % 2
            pst = self.ps[pb]
            r_p = self.r_ps[pb]
            for k in range(8):
                b = it[0] % 2
                it[0] += 1
                xin = self.xT[:, k, tsl]
                P.op("act", lambda e, o=sq[b], i=xin: e.activation(out=o, in_=i, func=AF.Square),
                     reads=[self.r_x[k][tb]], writes=[r_sq[b]])
                kw = dict(reads=[r_sq[b], self.r_const])
                if k == 0:
                    kw["writes"] = [r_p]
                else:
                    kw["cowrites"] = [r_p]
                P.op("pe", lambda e, pst=pst, i=sq[b], k=k: e.matmul(pst[:, :], lhsT=ones_r, rhs=i,
                                                                  start=(k == 0), stop=(k == 7)), **kw)

        stats(0)
        for tb in range(NTB):
            tsl = slice(tb * TB, (tb + 1) * TB)
            pb = tb % 2
            pst = self.ps[pb]
            r_p = self.r_ps[pb]
            if tb + 1 < NTB:
                stats(tb + 1)
            P.op("act", lambda e, o=lnv[pb], pst=pst: e.activation(out=o, in_=pst[:, :], func=AF.Ln, scale=1.0 / D, bias=EPS),
                 reads=[r_p], writes=[r_ln[pb]])
            P.op("act", lambda e, o=rstd[pb], i=lnv[pb]: e.activation(out=o, in_=i, func=AF.Exp, scale=-0.5),
                 reads=[r_ln[pb]], writes=[r_rs[pb]])
            for k in range(8):
                tbuf = (tb * 8 + k) % 4
                xin = self.xT[:, k, tsl]
                if not final:
                    sc = self.mod_ap(l, kind_sc, k, s)
                    sh = self.mod_ap(l, kind_sh, k, s)
                    P.op("dve", lambda e, o=tmp[tbuf], x=xin, r=rstd[pb], sc=sc: e.scalar_tensor_tensor(
                        out=o, in0=x, scalar=sc, in1=r, op0=ALU.mult, op1=ALU.mult),
                        reads=[self.r_x[k][tb], r_rs[pb], self.r_mod], writes=[r_tmp[tbuf]])
                    ho = self.hT[:, k, tsl]
                    kw = dict(reads=[r_tmp[tbuf], self.r_mod])
                    if k == 0:
                        kw["writes"] = [self.r_h[tb]]
                    else:
                        kw["cowrites"] = [self.r_h[tb]]
                    if k % 2 == 0:
                        P.op("dve", lambda e, o=ho, i=tmp[tbuf], sh=sh: e.tensor_scalar(
                            out=o, in0=i, scalar1=sh, scalar2=None, op0=ALU.add), **kw)
                    else:
                        P.op("act", lambda e, o=ho, i=tmp[tbuf], sh=sh: e.activation(
                            out=o, in_=i, func=AF.Identity, bias=sh), **kw)
                else:
                    g = self.pp[:, PP_FN + k:PP_FN + k + 1]
                    P.op("dve", lambda e, o=tmp[tbuf], x=xin, r=rstd[pb], g=g: e.scalar_tensor_tensor(
                        out=o, in0=x, scalar=g, in1=r, op0=ALU.mult, op1=ALU.mult),
                        reads=[self.r_x[k][tb], r_rs[pb], self.r_const], writes=[r_tmp[tbuf]])
                    ev = P.dma("sp", "ost%d" % tbuf, self.d_out[s, :, k, tsl], tmp[tbuf], reads=[r_tmp[tbuf]])
                    self._final_evs.append(ev)

    def ffn(self, s, l):
        P = self.P
        u = self.ar_bf16(0, [128, 11, S])
        r_u = [[Res("u%d_%d" % (c, t)) for t in range(NTB)] for c in range(11)]
        AF_W = 2 + S + 2
        a_full = [self.ar_f32(44, [128, AF_W]), self.ar_f32(44 + 8.25, [128, AF_W])]
        r_a = [[Res("a%d_%d" % (b, t)) for t in range(NTB)] for b in range(2)]
        r_az = Res("az")
        tt = [self.ar_f32(61, [128, 512]), self.ar_f32(63, [128, 512])]
        r_t = [Res("t0"), Res("t1")]
        ge = [self.ar_f32(65, [128, 512]), self.ar_f32(67, [128, 512])]
        r_g = [Res("g0"), Res("g1")]
        for b in range(2):
            P.op("dve", lambda e, o=a_full[b][:, 0:2]: e.memset(o, 0.0), cowrites=[r_az])
        NWS = 6
        wsl = [self.WB[:, i * 2048:(i + 1) * 2048] for i in range(NWS)]
        r_w = [Res("w%d" % i) for i in range(NWS)]
        wi = 0
        cwb = PP_CW + l * 66
        cbb = PP_CB + l * 22
        it = 0
        c_glob = 0
        for half, nch in enumerate(FF_SPLIT):
            for cl in range(nch):
                c = c_glob + cl
                sl = wi % NWS
                wi += 1
                wv = wsl[sl].rearrange("p (k n) -> p k n", k=8)
                P.dma("pool", "w%d" % sl, wsl[sl], self.d_fup[l, c].rearrange("p k n -> p (k n)"), writes=[r_w[sl]])
                ab = c % 2
                w0 = self.pp[:, cwb + 0 * 22 + c:cwb + 0 * 22 + c + 1]
                w1 = self.pp[:, cwb + 1 * 22 + c:cwb + 1 * 22 + c + 1]
                w2 = self.pp[:, cwb + 2 * 22 + c:cwb + 2 * 22 + c + 1]
                bia = self.pp[:, cbb + c:cbb + c + 1]
                for tb in range(NTB):
                    tsl = slice(tb * TB, (tb + 1) * TB)
                    pa = self.ps[(it % 2) * 2]
                    pg = self.ps[(it % 2) * 2 + 1]
                    r_pa = self.r_ps[(it % 2) * 2]
                    r_pg = self.r_ps[(it % 2) * 2 + 1]
                    tbuf = it % 2
                    it += 1

                    def mm_a(e, wv=wv, pa=pa, tsl=tsl):
                        ins = None
                        for k in range(8):
                            ins = e.matmul(pa[:, :], lhsT=wv[:, k, 0:128], rhs=self.hT[:, k, tsl], start=(k == 0), stop=(k == 7))
                        return ins

                    def mm_g(e, wv=wv, pg=pg, tsl=tsl):
                        ins = None
                        for k in range(8):
                            ins = e.matmul(pg[:, :], lhsT=wv[:, k, 128:256], rhs=self.hT[:, k, tsl], start=(k == 0), stop=(k == 7))
                        return ins
                    P.op("pe", mm_a, reads=[r_w[sl], self.r_h[tb]], writes=[r_pa])
                    P.op("pe", mm_g, reads=[r_w[sl], self.r_h[tb]], writes=[r_pg])
                    af = a_full[ab]
                    o0 = 2 + tb * TB
                    P.op("act", lambda e, o=af[:, o0:o0 + TB], pa=pa: e.activation(out=o, in_=pa[:, :], func=AF.Copy),
                         reads=[r_pa], writes=[r_a[ab][tb]])
                    P.op("act", lambda e, o=tt[tbuf], pa=pa, w2=w2, bia=bia: e.activation(
                        out=o, in_=pa[:, :], func=AF.Identity, scale=w2, bias=bia),
                        reads=[r_pa, self.r_const], writes=[r_t[tbuf]])
                    rd = [r_a[ab][tb], r_az, self.r_const, r_t[tbuf]]
                    if tb > 0:
                        rd.append(r_a[ab][tb - 1])
                    P.op("dve", lambda e, o=tt[tbuf], a1=af[:, o0 - 1:o0 - 1 + TB], w1=w1: e.scalar_tensor_tensor(
                        out=o, in0=a1, scalar=w1, in1=o, op0=ALU.mult, op1=ALU.add), reads=rd, writes=[r_t[tbuf]])
                    P.op("dve", lambda e, o=tt[tbuf], a0=af[:, o0 - 2:o0 - 2 + TB], w0=w0: e.scalar_tensor_tensor(
                        out=o, in0=a0, scalar=w0, in1=o, op0=ALU.mult, op1=ALU.add), reads=rd, writes=[r_t[tbuf]])
                    P.op("act", lambda e, o=ge[tbuf], i=tt[tbuf]: e.activation(out=o, in_=i, func=AF.Gelu),
                         reads=[r_t[tbuf]], writes=[r_g[tbuf]])
                    P.op("dve", lambda e, o=u[:, cl, tsl], g=ge[tbuf], pg=pg: e.tensor_tensor(out=o, in0=g, in1=pg[:, :], op=ALU.mult),
                         reads=[r_g[tbuf], r_pg], writes=[r_u[cl][tb]])
            for d in range(8):
                sl = wi % NWS
                wi += 1
                wv = wsl[sl][:, 0:nch * 128].rearrange("p (c n) -> p c n", c=nch)
                P.dma("pool", "w%d" % sl, wsl[sl][:, 0:nch * 128], self.d_fdn[l, half, d].rearrange("p c n -> p (c n)"),
                      writes=[r_w[sl]])
                gf = self.mod_ap(l, 5, d, s)
                for tb in range(NTB):
                    tsl = slice(tb * TB, (tb + 1) * TB)
                    po = self.ps[4 + it % 2]
                    r_po = self.r_ps[4 + it % 2]
                    it += 1

                    def mm_d(e, wv=wv, po=po, tsl=tsl, nch=nch):
                        ins = None
                        for cc in range(nch):
                            ins = e.matmul(po[:, :], lhsT=wv[:, cc, :], rhs=u[:, cc, tsl], start=(cc == 0), stop=(cc == nch - 1))
                        return ins
                    P.op("pe", mm_d, reads=[r_w[sl]] + [r_u[cc][tb] for cc in range(nch)], writes=[r_po])
                    xo = self.xT[:, d, tsl]
                    P.op("dve", lambda e, xo=xo, po=po, gf=gf: e.scalar_tensor_tensor(
                        out=xo, in0=po[:, :], scalar=gf, in1=xo, op0=ALU.mult, op1=ALU.add),
                        reads=[r_po, self.r_mod], writes=[self.r_x[d][tb]])
            c_glob += nch


    def mla(self, s, l):
        P = self.P
        a = l // 2
        SCALE = 192.0 ** -0.5
        tab = self.tab[:, :]
        cosT = tab[0:64, :]
        sinT = tab[64:128, :]
        cqn = self.ar_bf16(8, [128, 4, S])
        ckvn = self.ar_bf16(24, [128, 2, S])
        krT = self.ar_bf16(32, [128, S])
        knT2 = [self.ar_bf16(36, [128, S]), self.ar_bf16(40, [128, S])]
        Vh2 = [self.ar_bf16(44, [128, 16, 128]), self.ar_bf16(48, [128, 16, 128])]
        qnT2 = [self.ar_bf16(52, [128, 512]), self.ar_bf16(53, [128, 512])]
        qrT2 = [self.ar_bf16(54, [128, 512]), self.ar_bf16(55, [128, 512])]
        PT = [self.ar_bf16(56, [128, 512]), self.ar_bf16(57, [128, 512]), self.ar_bf16(4, [128, 512])]
        scr = [self.ar_f32(58 + 2 * i, [128, 512]) for i in range(4)]
        rinv = self.ar_f32(66, [128, 512])
        sqb = [self.ar_bf16(0, [128, 512]), self.ar_bf16(1, [128, 512])]
        small = self.ar_f32(5, [128, 64])
        r_cos = self.r_tab; r_sin = self.r_tab
        r_cqn = [Res("cqn%d" % t) for t in range(NTB)]
        r_ckvn = [Res("ckvn%d" % t) for t in range(NTB)]
        r_kr = [Res("kr%d" % t) for t in range(NTB)]
        r_scr = [Res("scr%d" % i) for i in range(4)]
        r_sq = [Res("sqA"), Res("sqB")]
        r_small = Res("small")
        rc = self.r_const
        w_in = self.WB[:, 0:8 * 896].rearrange("p (k n) -> p k n", k=8)
        r_wbig = Res("wbig")
        for k2 in range(4):
            P.dma("pool", "wbig", self.WB[:, k2 * 1792:(k2 + 1) * 1792], self.d_win[a, :, k2 * 1792:(k2 + 1) * 1792],
                  cowrites=[r_wbig])
        if l == 0:
            pos_i = self.ar_f32(36, [64, S]).bitcast(I32)
            tf = self.ar_f32(44, [64, S])
            tg = self.ar_f32(52, [64, S])
            r_pi = Res("posi"); r_tf = Res("tf"); r_tg = Res("tg")
            P.dma("sp", "pos", pos_i, self.d_pos[s:s + 1, :].broadcast_to([64, S]), writes=[r_pi])
            invf = self.pp[0:64, PP_IF:PP_IF + 1]
            TWO_PI = 2.0 * np.pi
            c1 = 6.28125
            rem = TWO_PI - c1
            c2 = float(np.frombuffer(np.array([np.frombuffer(np.float32(rem).tobytes(), np.uint32)[0] & 0xFFFFF000], np.uint32).tobytes(), np.float32)[0])
            c3 = float(np.float32(rem - c2))
            P.op("dve", lambda e: e.tensor_copy(out=tf, in_=pos_i), reads=[r_pi], writes=[r_tf])
            P.op("dve", lambda e: e.tensor_scalar(out=tf, in0=tf, scalar1=invf, scalar2=None, op0=ALU.mult),
                 reads=[r_tf, rc], writes=[r_tf])
            P.op("dve", lambda e: e.tensor_scalar(out=tg, in0=tf, scalar1=float(1.0 / TWO_PI), scalar2=None, op0=ALU.mult),
                 reads=[r_tf], writes=[r_tg])
            P.op("dve", lambda e: e.tensor_copy(out=pos_i, in_=tg), reads=[r_tg], writes=[r_pi])
            P.op("dve", lambda e: e.tensor_copy(out=tg, in_=pos_i), reads=[r_pi], writes=[r_tg])
            for cc in (c1, c2, c3):
                P.op("dve", lambda e, cc=cc: e.scalar_tensor_tensor(out=tf, in0=tg, scalar=-float(cc), in1=tf, op0=ALU.mult, op1=ALU.add),
                     reads=[r_tf, r_tg], writes=[r_tf])
            PI = float(np.pi)
            th = self.ar_f32(60, [64, S])
            r_th = Res("th")
            for (shift, dstT, r_d) in ((0.0, sinT, r_sin), (PI / 2, cosT, r_cos)):
                P.op("dve", lambda e, shift=shift: e.tensor_scalar(out=tg, in0=tf, scalar1=float(shift), scalar2=None, op0=ALU.add),
                     reads=[r_tf], writes=[r_tg])
                P.op("dve", lambda e: e.tensor_scalar(out=th, in0=tg, scalar1=-PI, scalar2=TWO_PI, op0=ALU.is_lt, op1=ALU.mult),
                     reads=[r_tg], writes=[r_th])
                P.op("dve", lambda e: e.tensor_tensor(out=tg, in0=tg, in1=th, op=ALU.add), reads=[r_tg, r_th], writes=[r_tg])
                P.op("dve", lambda e: e.tensor_scalar(out=th, in0=tg, scalar1=PI, scalar2=-TWO_PI, op0=ALU.is_gt, op1=ALU.mult),
                     reads=[r_tg], writes=[r_th])
                P.op("dve", lambda e: e.tensor_tensor(out=tg, in0=tg, in1=th, op=ALU.add), reads=[r_tg, r_th], writes=[r_tg])
                P.op("act", lambda e, dstT=dstT: e.activation(out=dstT, in_=tg, func=AF.Sin), reads=[r_tg], writes=[r_d])
            P.op("dve", lambda e: e.tensor_scalar(out=tab[64:96, :], in0=tab[64:96, :], scalar1=-1.0, scalar2=None, op0=ALU.mult),
                 reads=[r_sin], writes=[r_sin])
        P.barrier()
        r_zero = Res("zero")
        P.op("dve", lambda e: e.memset(krT[64:128, :], 0.0), cowrites=[r_zero])
        for qq in range(2):
            P.op("dve", lambda e, qq=qq: e.memset(qrT2[qq][64:128, :], 0.0), cowrites=[r_zero])

        ones_r = self.ones_b
        sqr = [self.sqr[:, 0, :], self.sqr[:, 1, :]]
        it = 0
        sqi = 0

        rope_i = [0]

        def rope(dst, tsl, nm, tmps=None):
            if tmps is None:
                b2 = (rope_i[0] % 2) * 2
                rope_i[0] += 1
                s0 = scr[b2][0:64, :]
                s1 = scr[b2 + 1][0:64, :]
                r0, r1 = r_scr[b2], r_scr[b2 + 1]
            else:
                s0, s1, r0, r1 = tmps
            P.op("dve", lambda e: e.tensor_tensor(out=s0, in0=self.ps[6][0:64, :], in1=cosT[:, tsl], op=ALU.mult),
                 reads=[self.r_ps[6], r_cos], writes=[r0])
            P.op("dve", lambda e: e.tensor_tensor(out=s1, in0=self.ps[6][64:128, :], in1=sinT[:, tsl], op=ALU.mult),
                 reads=[self.r_ps[6], r_sin], writes=[r1])
            P.op("dve", lambda e: e.tensor_tensor(out=dst, in0=s0, in1=s1, op=ALU.add),
                 reads=[r0, r1, r_zero], writes=[nm])

        scrB = [self.ar_f32(42 + 2 * i, [128, 512]) for i in range(4)]
        r_scrB = [Res("scrB%d" % i) for i in range(4)]
        rinvB = self.ar_f32(50, [128, 512])
        r_smallB = Res("smallB")
        scrC = [self.ar_f32(36, [128, 512]), self.ar_f32(38, [128, 512])]
        r_scrC = [Res("scrC0"), Res("scrC1")]
        rinvC = self.ar_f32(40, [128, 512])
        r_smallC = Res("smallC")
        ropeB = (self.ar_f32(52, [128, 512])[0:64, :], self.ar_f32(54, [128, 512])[0:64, :], Res("ropeB0"), Res("ropeB1"))
        for tb in range(NTB):
            tsl = slice(tb * TB, (tb + 1) * TB)
            for gi, (nch, col0, dst, r_dst, gcol) in enumerate(((4, 0, cqn, r_cqn, PP_QN + a * 4), (2, 512, ckvn, r_ckvn, PP_KVN + a * 2))):
                if gi == 0:
                    if tb % 2 == 0:
                        sc_, r_sc_, rv_, r_rv_ = scr, r_scr, rinv, r_small
                    else:
                        sc_, r_sc_, rv_, r_rv_ = scrB, r_scrB, rinvB, r_smallB
                    banks = (0, 1, 2, 3)
                else:
                    sc_, r_sc_, rv_, r_rv_ = scrC, r_scrC, rinvC, r_smallC
                    banks = (7, 0)
                pss = self.ps[4 + it % 2]
                r_pss = self.r_ps[4 + it % 2]
                it += 1
                for j in range(nch):
                    pj = self.ps[banks[j]]
                    r_pj = self.r_ps[banks[j]]

                    def mm(e, pj=pj, c0=col0 + j * 128, tsl=tsl):
                        ins = None
                        for k in range(8):
                            ins = e.matmul(pj[:, :], lhsT=w_in[:, k, c0:c0 + 128], rhs=self.hT[:, k, tsl], start=(k == 0), stop=(k == 7))
                        return ins
                    P.op("pe", mm, reads=[r_wbig, self.r_h[tb]], writes=[r_pj])
                    P.op("act", lambda e, o=sc_[j], pj=pj: e.activation(out=o, in_=pj[:, :], func=AF.Copy),
                         reads=[r_pj], writes=[r_sc_[j]])
                    b = sqi % 2
                    sqi += 1
                    P.op("act", lambda e, o=sqr[b], i=sc_[j]: e.activation(out=o, in_=i, func=AF.Square),
                         reads=[r_sc_[j]], writes=[r_sq[b]])
                    kw = dict(reads=[r_sq[b], rc])
                    if j == 0:
                        kw["writes"] = [r_pss]
                    else:
                        kw["cowrites"] = [r_pss]
                    P.op("pe", lambda e, pss=pss, i=sqr[b], j=j, nch=nch: e.matmul(pss[:, :], lhsT=ones_r, rhs=i, start=(j == 0), stop=(j == nch - 1)), **kw)
                P.op("act", lambda e, pss=pss, nch=nch, rv_=rv_: e.activation(out=rv_, in_=pss[:, :], func=AF.Ln, scale=1.0 / (nch * 128), bias=EPS),
                     reads=[r_pss], writes=[r_rv_])
                P.op("act", lambda e, rv_=rv_: e.activation(out=rv_, in_=rv_, func=AF.Exp, scale=-0.5), reads=[r_rv_], writes=[r_rv_])
                for j in range(nch):
                    g = self.pp[:, gcol + j:gcol + j + 1]
                    kw = dict(reads=[r_sc_[j], r_rv_, rc])
                    if j == 0:
                        kw["writes"] = [r_dst[tb]]
                    else:
                        kw["cowrites"] = [r_dst[tb]]
                    P.op("dve", lambda e, o=dst[:, j, tsl], i=sc_[j], g=g, rv_=rv_: e.scalar_tensor_tensor(
                        out=o, in0=i, scalar=g, in1=rv_, op0=ALU.mult, op1=ALU.mult), **kw)
            def mmk(e, tsl=tsl):
                ins = None
                for k in range(8):
                    ins = e.matmul(self.ps[6][:, :], lhsT=w_in[:, k, 768:896], rhs=self.hT[:, k, tsl], start=(k == 0), stop=(k == 7))
                return ins
            P.op("pe", mmk, reads=[r_wbig, self.r_h[tb]], writes=[self.r_ps[6]])
            rope(krT[0:64, tsl], tsl, r_kr[tb], tmps=ropeB)
        P.barrier()
        if s == 0 and l == 0:
            self.dump("cos", cosT, [64, S], BF16)
            self.dump("sin", sinT, [64, S], BF16)
            self.dump("cqn", cqn, [128, 4, S], BF16)
            self.dump("ckvn", ckvn, [128, 2, S], BF16)
            self.dump("krT", krT[0:64, :], [64, S], BF16)

        wo = self.WB[:, 0:8192].rearrange("p (h n) -> p h n", h=8)
        r_wo = Res("wo")
        for h2 in range(4):
            P.dma("pool", "wbig", self.WB[:, h2 * 2048:(h2 + 1) * 2048], self.d_wo[a, :, h2 * 2048:(h2 + 1) * 2048], cowrites=[r_wo])

        r_qn = [Res("qn0"), Res("qn1")]
        r_qr = [Res("qr0"), Res("qr1")]
        r_kn = [[Res("kn%d_%d" % (b, t)) for t in range(NTB)] for b in range(2)]
        r_v = [[Res("v%d_%d" % (b, t)) for t in range(NTB)] for b in range(2)]
        r_pt = [Res("pt0"), Res("pt1"), Res("pt2")]
        r_rinv = Res("rinv")
        r_o = [[Res("o%d_%d" % (h, t)) for t in range(NTB)] for h in range(8)]
        r_hw = [Res("hw0"), Res("hw1")]
        r_mk = Res("mk")
        r_mq = Res("mq")
        mrow128 = self.cb[:, CB_MROW:CB_MROW + 128]

        def hw_views(hb):
            wq = self.WB[:, 8192 + hb * 1536:8192 + hb * 1536 + 1024].rearrange("p (k n) -> p k n", k=4)
            wkv = self.WB[:, 8192 + hb * 1536 + 1024:8192 + hb * 1536 + 1536].rearrange("p (k n) -> p k n", k=2)
            return wq, wkv

        def load_hw(h):
            hb = h % 2
            P.dma("pool", "hw%d" % hb, self.WB[:, 8192 + hb * 1536:8192 + hb * 1536 + 1024], self.d_wq[a, h], writes=[r_hw[hb]])
            P.dma("pool", "hw%d" % hb, self.WB[:, 8192 + hb * 1536 + 1024:8192 + hb * 1536 + 1536], self.d_wkv[a, h], cowrites=[r_hw[hb]])

        def sumsq_stages(srcA, rA, srcB, rB, outcol, r_out):
            def st_sq():
                P.op("dve", lambda e: e.tensor_tensor(out=sqb[0], in0=srcA, in1=srcA, op=ALU.mult), reads=[rA], writes=[r_sq[0]])
                P.op("dve", lambda e: e.tensor_tensor(out=sqb[1], in0=srcB, in1=srcB, op=ALU.mult), reads=[rB, r_zero], writes=[r_sq[1]])

            def st_mm():
                def mms(e):
                    e.matmul(self.ps[7][:, :], lhsT=self.ones_b, rhs=sqb[0], start=True, stop=False)
                    return e.matmul(self.ps[7][:, :], lhsT=self.ones_b, rhs=sqb[1], start=False, stop=True)
                P.op("pe", mms, reads=[r_sq[0], r_sq[1], rc], writes=[self.r_ps[7]])

            def st_max():
                P.op("dve", lambda e: e.reduce_max(out=outcol, in_=self.ps[7][:, :], axis=AX.X), reads=[self.r_ps[7]], cowrites=[r_out])
            return [st_sq, st_mm, st_max]

        def kprep(h, tbs):
            hb = h % 2
            wq, wkv = hw_views(hb)
            knT = knT2[hb]
            Vh = Vh2[hb]
            stages = []
            for tb in tbs:
                tsl = slice(tb * TB, (tb + 1) * TB)

                def s1(tb=tb, tsl=tsl):
                    def mmk2(e):
                        ins = None
                        for k in range(2):
                            ins = e.matmul(self.ps[6][:, :], lhsT=wkv[:, k, 0:128], rhs=ckvn[:, k, tsl], start=(k == 0), stop=(k == 1))
                        return ins
                    P.op("pe", mmk2, reads=[r_hw[hb], r_ckvn[tb]], writes=[self.r_ps[6]])

                def s2(tb=tb, tsl=tsl):
                    P.op("act", lambda e: e.activation(out=knT[:, tsl], in_=self.ps[6][:, :], func=AF.Copy),
                         reads=[self.r_ps[6]], writes=[r_kn[hb][tb]])

                def s3(tb=tb):
                    def mmv(e):
                        ins = None
                        for t4 in range(4):
                            t0 = tb * TB + t4 * 128
                            for k in range(2):
                                ins = e.matmul(self.ps[7][:, t4 * 128:(t4 + 1) * 128], lhsT=ckvn[:, k, t0:t0 + 128], rhs=wkv[:, k, 128:256],
                                               start=(k == 0), stop=(k == 1))
                        return ins
                    P.op("pe", mmv, reads=[r_hw[hb], r_ckvn[tb]], writes=[self.r_ps[7]])

                def s4(tb=tb):
                    P.op("dve", lambda e: e.tensor_copy(out=Vh[:, tb * 4:(tb + 1) * 4, :],
                                                        in_=self.ps[7][:, :].rearrange("p (a b) -> p a b", a=4)),
                         reads=[self.r_ps[7]], writes=[r_v[hb][tb]])
                stages += [s1, s2, s3, s4]
                stages += sumsq_stages(knT[:, tsl], r_kn[hb][tb], krT[:, tsl], r_kr[tb], small[:, hb * 4 + tb:hb * 4 + tb + 1], r_mk)
            if tbs[-1] == NTB - 1:
                def s_fin():
                    P.op("dve", lambda e: e.reduce_max(out=small[:, 8 + hb:9 + hb], in_=small[:, hb * 4:hb * 4 + 4], axis=AX.X),
                         reads=[r_mk], cowrites=[r_mk])
                stages.append(s_fin)
            return stages

        qcnt = [0]

        def qprep(h, qb):
            hb = h % 2
            wq, wkv = hw_views(hb)
            qi = qcnt[0] % 2
            qcnt[0] += 1
            tsl = slice(qb * TB, (qb + 1) * TB)
            qnT = qnT2[qi]
            qrT = qrT2[qi]
            mq = small[:, 10 + qi:11 + qi]
            negc = small[:, 16 + h * 4 + qb:17 + h * 4 + qb]

            def s1():
                def mmq(e):
                    ins = None
                    for k in range(4):
                        ins = e.matmul(self.ps[6][:, :], lhsT=wq[:, k, 0:128], rhs=cqn[:, k, tsl], start=(k == 0), stop=(k == 3))
                    return ins
                P.op("pe", mmq, reads=[r_hw[hb], r_cqn[qb]], writes=[self.r_ps[6]])

            def s2():
                P.op("act", lambda e: e.activation(out=qnT, in_=self.ps[6][:, :], func=AF.Copy), reads=[self.r_ps[6]], writes=[r_qn[qi]])

            def s3():
                def mmqr(e):
                    ins = None
                    for k in range(4):
                        ins = e.matmul(self.ps[6][:, :], lhsT=wq[:, k, 128:256], rhs=cqn[:, k, tsl], start=(k == 0), stop=(k == 3))
                    return ins
                P.op("pe", mmqr, reads=[r_hw[hb], r_cqn[qb]], writes=[self.r_ps[6]])

            def s4():
                rope(qrT[0:64, :], tsl, r_qr[qi])

            def s_neg():
                P.op("dve", lambda e: e.tensor_scalar(out=negc, in0=mq, scalar1=small[:, 8 + hb:9 + hb], scalar2=-0.5 * SCALE,
                                                      op0=ALU.add, op1=ALU.mult), reads=[r_mq, r_mk], cowrites=[r_mq])
            stages = [s1, s2, s3, s4] + sumsq_stages(qnT, r_qn[qi], qrT, r_qr[qi], mq, r_mq) + [s_neg]
            return qi, negc, stages

        def blk_ctx(h, qb, qi, negc):
            hb = h % 2
            return dict(h=h, qb=qb, hb=hb, knT=knT2[hb], Vh=Vh2[hb], qnT=qnT2[qi], qrT=qrT2[qi], qi=qi, negc=negc,
                        po=self.ps[3 + qb % 2], r_po=self.r_ps[3 + qb % 2], psm=self.ps[5], r_psm=self.r_ps[5],
                        nj=4 * qb + 4)

        def step_bufs(k):
            return self.ps[k % 3], self.r_ps[k % 3], PT[k % 3], r_pt[k % 3]

        def emit_mms(k, cx, j):
            pst, r_pst, ptb, r_ptb = step_bufs(k)
            r = j - 4 * cx["qb"]
            c0 = 128 * max(r, 0)
            ksl = slice(j * 128, (j + 1) * 128)
            knT, qnT, qrT = cx["knT"], cx["qnT"], cx["qrT"]

            def mms(e):
                e.matmul(pst[:, c0:512], lhsT=knT[:, ksl], rhs=qnT[:, c0:512], start=True, stop=False)
                ins = e.matmul(pst[:, c0:512], lhsT=krT[:, ksl], rhs=qrT[:, c0:512], start=False, stop=(r < 0))
                if r >= 0:
                    ins = e.matmul(pst[:, c0:c0 + 64], lhsT=mrow128, rhs=self.ones_b[:, 0:64], start=False, stop=True)
                return ins
            tbk = j // 4
            P.op("pe", mms, reads=[r_kn[cx["hb"]][tbk], r_kr[tbk], r_qn[cx["qi"]], r_qr[cx["qi"]], rc, r_zero], writes=[r_pst])

        def emit_exp_mmo(k, cx, j):
            pst, r_pst, ptb, r_ptb = step_bufs(k)
            r = j - 4 * cx["qb"]
            c0 = 128 * max(r, 0)
            negc, Vh, po, psm, nj = cx["negc"], cx["Vh"], cx["po"], cx["psm"], cx["nj"]
            P.op("act", lambda e: e.activation(out=ptb[:, c0:512], in_=pst[:, c0:512], func=AF.Exp, scale=SCALE, bias=negc),
                 reads=[r_pst, r_mq], writes=[r_ptb])

            def mmo(e):
                e.matmul(po[:, c0:512], lhsT=Vh[:, j, :], rhs=ptb[:, c0:512], start=(j == 0), stop=(j == nj - 1))
                return e.matmul(psm[:, c0:512], lhsT=self.ones_b, rhs=ptb[:, c0:512], start=(j == 0), stop=(j == nj - 1))
            tbk = j // 4
            kw = dict(reads=[r_v[cx["hb"]][tbk], r_ptb, rc])
            if j == 0:
                kw["writes"] = [cx["r_po"], cx["r_psm"]]
            else:
                kw["cowrites"] = [cx["r_po"], cx["r_psm"]]
            P.op("pe", mmo, **kw)
            if j == nj - 1:
                h, qb = cx["h"], cx["qb"]
                P.op("act", lambda e: e.activation(out=rinv, in_=psm[:, :], func=AF.Ln), reads=[cx["r_psm"]], writes=[r_rinv])
                P.op("act", lambda e: e.activation(out=rinv, in_=rinv, func=AF.Exp, scale=-1.0), reads=[r_rinv], writes=[r_rinv])
                P.op("dve", lambda e: e.tensor_tensor(out=self.hT[:, h, qb * TB:(qb + 1) * TB], in0=po[:, :], in1=rinv, op=ALU.mult),
                     reads=[cx["r_po"], r_rinv], writes=[r_o[h][qb]])

        load_hw(0)
        load_hw(1)
        for st_ in kprep(0, [0, 1, 2, 3]):
            st_()
        ctxs = {}
        qi0, negc0, st0 = qprep(0, 0)
        for st_ in st0:
            st_()
        ctxs[(0, 0)] = blk_ctx(0, 0, qi0, negc0)
        steps = [(h, qb, j) for h in range(8) for qb in range(4) for j in range(4 * qb + 4)]
        start_of = {}
        for k, (h, qb, j) in enumerate(steps):
            if j == 0:
                start_of[(h, qb)] = k
        LOOK = 2
        pending = []
        emit_mms(0, ctxs[(0, 0)], 0)
        emit_mms(1, ctxs[(0, 0)], 1)
        for k, (h, qb, j) in enumerate(steps):
            if j == 0:
                if qb < 3:
                    qi_, negc_, stg_ = qprep(h, qb + 1)
                    ctxs[(h, qb + 1)] = blk_ctx(h, qb + 1, qi_, negc_)
                    pending.append([stg_, 0, start_of[(h, qb + 1)] - LOOK])
                if h < 7:
                    if qb == 1:
                        pending.append([kprep(h + 1, [0, 1]), 0, start_of[(h, 3)] - 1])
                    elif qb == 2:
                        pending.append([kprep(h + 1, [2, 3]), 0, start_of[(h + 1, 0)] - LOOK - 6])
                    elif qb == 3:
                        qi_, negc_, stg_ = qprep(h + 1, 0)
                        ctxs[(h + 1, 0)] = blk_ctx(h + 1, 0, qi_, negc_)
                        pending.append([stg_, 0, start_of[(h + 1, 0)] - LOOK])
                if qb == 3 and h + 2 < 8:
                    load_hw(h + 2)
            pending.sort(key=lambda it: it[2])
            n_adv = 0
            cum = 0
            for it in pending:
                cum += len(it[0]) - it[1]
                n_adv = max(n_adv, -(-cum // max(1, it[2] - k + 1)))
            for _ in range(n_adv):
                it = pending[0]
                it[0][it[1]]()
                it[1] += 1
                if it[1] >= len(it[0]):
                    pending.pop(0)
                    if not pending:
                        break
            if k + LOOK < len(steps):
                h2, qb2, j2 = steps[k + LOOK]
                emit_mms(k + LOOK, ctxs[(h2, qb2)], j2)
            emit_exp_mmo(k, ctxs[(h, qb)], j)
        assert not pending
        it = 0
        for d in range(8):
            ga = self.mod_ap(l, 2, d, s)
            for tb in range(NTB):
                tsl = slice(tb * TB, (tb + 1) * TB)
                po = self.ps[6 + it % 2]
                r_po = self.r_ps[6 + it % 2]
                it += 1

                def mmd(e, po=po, d=d, tsl=tsl):
                    ins = None
                    for h in range(8):
                        ins = e.matmul(po[:, :], lhsT=wo[:, h, d * 128:(d + 1) * 128], rhs=self.hT[:, h, tsl], start=(h == 0), stop=(h == 7))
                    return ins
                P.op("pe", mmd, reads=[r_wo] + [r_o[h][tb] for h in range(8)], writes=[r_po])
                xo = self.xT[:, d, tsl]
                P.op("dve", lambda e, xo=xo, po=po, ga=ga: e.scalar_tensor_tensor(
                    out=xo, in0=po[:, :], scalar=ga, in1=xo, op0=ALU.mult, op1=ALU.add),
                    reads=[r_po, self.r_mod], writes=[self.r_x[d][tb]])


    def mlstm(self, s, l):
        P = self.P
        b = l // 2
        rc = self.r_const
        LNS = float(np.log(128.0 ** -0.5))
        self.areset()
        G1 = self.af([128, S])
        G2 = self.af([128, S])
        qpT = self.ab([128, S])
        kpT = self.ab([128, S])
        yT = self.ab([128, 8, S])
        tokS = self.af([128, 16, 8])
        dec_b = self.af([128, 64])
        hn_h = self.af([128, 256])
        small2 = self.af([128, 64])
        kws = [self.ab([128, 128]) for _ in range(2)]
        vaug = [self.ab([128, 264]) for _ in range(2)]
        eo = [self.af([128, 256]) for _ in range(2)]
        smT = [self.ab([128, 128]) for _ in range(2)]
        Cf = self.af([128, 264])
        Cb = [self.ab([128, 264]) for _ in range(2)]
        yh = [self.ab([128, 256]) for _ in range(2)]
        qs = [self.af([128, 512]) for _ in range(2)]
        junk = qs[0][:, 0:256]
        tri = self.cf[:, CF_TRI:CF_TRI + 128]

        def slot(G, j):
            return G[32 * j:32 * j + 4, :]
        A = [slot(G1, j) for j in range(4)]
        B = [slot(G2, j) for j in range(4)]
        r_A = [Res("A%d" % j) for j in range(4)]
        r_B = [Res("B%d" % j) for j in range(4)]
        wg = [self.WB[:, 8192 + g * 1024:8192 + (g + 1) * 1024].rearrange("p (k n) -> p k n", k=8) for g in range(2)]
        r_wg = Res("wg")
        for g in range(2):
            P.dma("pool", "wg", self.WB[:, 8192 + g * 1024:8192 + (g + 1) * 1024], self.d_mlg[b, g], cowrites=[r_wg])
        r_gz = Res("gz")
        P.op("dve", lambda e: e.memset(G1, 0.0), writes=[r_gz])
        P.op("dve", lambda e: e.memset(G2, 0.0), cowrites=[r_gz])
        r_wh = Res("wh")
        Wh = self.WB[:, 0:7168].rearrange("p (k n) -> p k n", k=8)

        def load_head(h):
            for k2 in range(4):
                kw = dict(writes=[r_wh]) if k2 == 0 else dict(cowrites=[r_wh])
                P.dma("pool", "wh", self.WB[:, k2 * 1792:(k2 + 1) * 1792], self.d_mlw[b, h, :, k2 * 1792:(k2 + 1) * 1792], **kw)
        load_head(0)
        ib = self.pp[0:4, PP_GB + 2 * b:PP_GB + 2 * b + 1]
        fb = self.pp[0:4, PP_GB + 2 * b + 1:PP_GB + 2 * b + 2]
        for tb in range(NTB):
            tsl = slice(tb * TB, (tb + 1) * TB)
            pi = self.ps[(tb % 2) * 2]
            pf = self.ps[(tb % 2) * 2 + 1]
            r_pi = self.r_ps[(tb % 2) * 2]
            r_pf = self.r_ps[(tb % 2) * 2 + 1]

            def mmg(e, pp_=pi, g=0, tsl=tsl):
                ins = None
                for k in range(8):
                    ins = e.matmul(pp_[:, :], lhsT=wg[g][:, k, :], rhs=self.hT[:, k, tsl], start=(k == 0), stop=(k == 7))
                return ins
            P.op("pe", mmg, reads=[r_wg, self.r_h[tb]], writes=[r_pi])
            P.op("pe", lambda e, f=mmg, pf=pf, tsl=tsl: f(e, pf, 1, tsl), reads=[r_wg, self.r_h[tb]], writes=[r_pf])
            kw = dict(writes=[r_A[0]]) if tb == 0 else dict(cowrites=[r_A[0]])
            P.op("act", lambda e, pi=pi, tsl=tsl: e.activation(out=A[0][:, tsl], in_=pi[0:4, :], func=AF.Identity, bias=ib),
                 reads=[r_pi, rc, r_gz], **kw)
            kw = dict(writes=[r_A[1]]) if tb == 0 else dict(cowrites=[r_A[1]])
            P.op("act", lambda e, pf=pf, tsl=tsl: e.activation(out=A[1][:, tsl], in_=pf[0:4, :], func=AF.Identity, bias=fb),
                 reads=[r_pf, rc, r_gz], **kw)
        P.op("act", lambda e: e.activation(out=A[1], in_=A[1], func=AF.Exp, scale=-1.0), reads=[r_A[1]], writes=[r_A[1]])
        P.op("act", lambda e: e.activation(out=A[1], in_=A[1], func=AF.Ln, bias=1.0), reads=[r_A[1]], writes=[r_A[1]])
        P.op("dve", lambda e: e.tensor_tensor_scan(out=B[0], data0=A[1], data1=A[1], initial=0.0, op0=ALU.add, op1=ALU.max),
             reads=[r_A[1]], writes=[r_B[0]])
        P.op("dve", lambda e: e.tensor_tensor(out=B[1], in0=A[0], in1=B[0], op=ALU.add), reads=[r_A[0], r_B[0]], writes=[r_B[1]])
        P.op("dve", lambda e: e.tensor_tensor_scan(out=A[1], data0=B[1], data1=B[1], initial=0.0, op0=ALU.max, op1=ALU.max),
             reads=[r_B[1]], writes=[r_A[1]])
        P.op("dve", lambda e: e.tensor_tensor_scan(out=A[0], data0=B[1], data1=B[1], initial=0.0, op0=ALU.max, op1=ALU.max),
             reads=[r_B[1]], writes=[r_A[0]])

        def v3(ap):
            return ap.rearrange("p (c n) -> p c n", c=16)
        Mxa3 = v3(A[1]); Mxb3 = v3(A[0]); a3 = v3(B[1]); Gn3 = v3(B[0])

        def ref_of(M3):
            return M3[:, 0:15, 127:128].broadcast_to([4, 15, 128])

        def end_of(M3):
            return M3[:, :, 127:128].broadcast_to([4, 16, 128])
        o3 = v3(A[2])
        P.op("dve", lambda e: e.tensor_tensor(out=o3[:, 1:16, :], in0=ref_of(Mxb3), in1=Mxb3[:, 1:16, :], op=ALU.subtract),
             reads=[r_A[0]], writes=[r_A[2]])
        P.op("dve", lambda e: e.tensor_scalar(out=o3[:, 0:1, :], in0=Mxb3[:, 0:1, :], scalar1=-1.0, scalar2=None, op0=ALU.mult),
             reads=[r_A[0]], cowrites=[r_A[2]])
        P.op("act", lambda e: e.activation(out=A[2], in_=A[2], func=AF.Exp), reads=[r_A[2]], writes=[r_A[2]])
        o3b = v3(B[2])
        P.op("dve", lambda e: e.tensor_tensor(out=o3b[:, 1:16, :], in0=a3[:, 1:16, :], in1=ref_of(Mxa3), op=ALU.subtract),
             reads=[r_B[1], r_A[1]], writes=[r_B[2]])
        P.op("dve", lambda e: e.tensor_copy(out=o3b[:, 0:1, :], in_=a3[:, 0:1, :]), reads=[r_B[1]], cowrites=[r_B[2]])
        P.op("act", lambda e: e.activation(out=B[2], in_=B[2], func=AF.Exp, bias=LNS), reads=[r_B[2]], writes=[r_B[2]])
        P.op("dve", lambda e: e.tensor_tensor(out=v3(A[3]), in0=a3, in1=end_of(Mxa3), op=ALU.subtract),
             reads=[r_B[1], r_A[1]], writes=[r_A[3]])
        P.op("act", lambda e: e.activation(out=A[3], in_=A[3], func=AF.Exp, bias=LNS), reads=[r_A[3]], writes=[r_A[3]])
        P.op("dve", lambda e: e.tensor_tensor(out=B[3], in0=B[0], in1=A[0], op=ALU.subtract), reads=[r_B[0], r_A[0]], writes=[r_B[3]])
        P.op("act", lambda e: e.activation(out=B[3], in_=B[3], func=AF.Exp, scale=2.0), reads=[r_B[3]], writes=[r_B[3]])
        r_s2 = Res("small2")
        dr = small2[0:4, 0:16]
        P.op("dve", lambda e: e.memset(small2[:, 0:16], 0.0), writes=[r_s2])
        P.op("dve", lambda e: e.tensor_tensor(out=small2[0:4, 1:16].unsqueeze(2), in0=Mxb3[:, 0:15, 127:128], in1=Mxb3[:, 1:16, 127:128],
                                              op=ALU.subtract), reads=[r_A[0]], cowrites=[r_s2])
        P.op("dve", lambda e: e.tensor_scalar(out=small2[0:4, 0:1].unsqueeze(2), in0=Mxb3[:, 0:1, 127:128], scalar1=-1.0, scalar2=None,
                                              op0=ALU.mult), reads=[r_A[0]], cowrites=[r_s2])
        P.op("act", lambda e: e.activation(out=dr, in_=dr, func=AF.Exp), reads=[r_s2], writes=[r_s2])
        r_dec = Res("dec")
        for h in range(4):
            sel0 = self.cf[:, CF_SELA + h * 128:CF_SELA + (h + 1) * 128]
            P.op("pe", lambda e, sel0=sel0: e.matmul(self.ps[4][:, 0:16], lhsT=sel0, rhs=small2[:, 0:16], start=True, stop=True),
                 reads=[r_s2, rc], writes=[self.r_ps[4]])
            kw = dict(writes=[r_dec]) if h == 0 else dict(cowrites=[r_dec])
            P.op("act", lambda e, h=h: e.activation(out=dec_b[:, h * 16:(h + 1) * 16], in_=self.ps[4][:, 0:16], func=AF.Copy),
                 reads=[self.r_ps[4]], **kw)
        r_tok = Res("tok")
        for c in range(16):
            csl = slice(c * 128, (c + 1) * 128)
            for gi, (G, rG, col) in enumerate(((G1, r_A, 0), (G2, r_B, 4))):
                bk = (2 * c + gi) % 8
                P.op("pe", lambda e, G=G, csl=csl, bk=bk: e.transpose(out=self.ps[bk][:, 0:128], in_=G[:, csl], identity=self.ident_f),
                     reads=[rG[0], rG[1], rG[2], rG[3], rc], writes=[self.r_ps[bk]])
                P.op("act", lambda e, c=c, col=col, bk=bk: e.activation(out=tokS[:, c, col:col + 4], in_=self.ps[bk][:, 96:100], func=AF.Copy),
                     reads=[self.r_ps[bk]], cowrites=[r_tok])
        if s == 0 and l == 1 and self.dbg:
            self.dump("G1", G1, [128, S], F32)
            self.dump("G2", G2, [128, S], F32)
            self.dump("tokS", tokS, [128, 16, 8], F32)
            self.dump("decb", dec_b, [128, 64], F32)
        r_qp = [Res("qp%d" % t) for t in range(NTB)]
        r_kp = [Res("kp%d" % t) for t in range(NTB)]
        r_qs = [Res("qs0"), Res("qs1")]
        r_hn = Res("hn")
        r_kws = [Res("kws0"), Res("kws1")]
        r_va = [Res("va0"), Res("va1")]
        r_eo = [Res("eo0"), Res("eo1")]
        r_sm = [Res("sm0"), Res("sm1")]
        r_Cf = Res("Cf")
        r_Cb = [Res("Cb0"), Res("Cb1")]
        r_yh = [Res("yh0"), Res("yh1")]
        r_junk = r_qs[0]
        r_yT = [[Res("yT%d_%d" % (j, t)) for t in range(NTB)] for j in range(8)]
        r_st = Res("st")
        for vb in range(2):
            P.op("dve", lambda e, vb=vb: e.memset(vaug[vb][:, 256:264], 1.0), cowrites=[r_va[vb]])
        qi = 0
        for h in range(4):
            P.dma("sp", "hn", hn_h, self.d_hn[:, b, h * 256:(h + 1) * 256], writes=[r_hn])
            selh = self.cf[:, CF_SELB + h * 128:CF_SELB + (h + 1) * 128]
            for tb in range(NTB):
                tsl = slice(tb * TB, (tb + 1) * TB)
                for (c0, Gsl, rG, dstT, r_d) in ((0, G1[:, tsl], r_A[2], qpT, r_qp), (128, G2[:, tsl], r_B[2], kpT, r_kp)):
                    pb = self.ps[(qi % 2) * 2]
                    pq = self.ps[(qi % 2) * 2 + 1]
                    r_pb = self.r_ps[(qi % 2) * 2]
                    r_pq = self.r_ps[(qi % 2) * 2 + 1]
                    sc = qs[qi % 2]
                    r_sc = r_qs[qi % 2]
                    qi += 1
                    P.op("pe", lambda e, pb=pb, Gsl=Gsl, selh=selh: e.matmul(pb[:, :], lhsT=selh, rhs=Gsl, start=True, stop=True),
                         reads=[rG, rc], writes=[r_pb])

                    def mmq(e, pq=pq, c0=c0, tsl=tsl):
                        ins = None
                        for k in range(8):
                            ins = e.matmul(pq[:, :], lhsT=Wh[:, k, c0:c0 + 128], rhs=self.hT[:, k, tsl], start=(k == 0), stop=(k == 7))
                        return ins
                    P.op("pe", mmq, reads=[r_wh, self.r_h[tb]], writes=[r_pq])
                    P.op("act", lambda e, sc=sc, pq=pq: e.activation(out=sc, in_=pq[:, :], func=AF.Copy), reads=[r_pq], writes=[r_sc])
                    P.op("dve", lambda e, o=dstT[:, tsl], sc=sc, pb=pb: e.tensor_tensor(out=o, in0=sc, in1=pb[:, :], op=ALU.mult),
                         reads=[r_sc, r_pb], writes=[r_d[tb]])
            def proj(c, part, h=h):
                t0 = c * 128
                ia = c % 2
                io = 2
                pa = self.ps[ia]
                po = self.ps[io]

                def mma(e, pa=pa, t0=t0):
                    ins = None
                    for k in range(8):
                        ins = e.matmul(pa[:, 0:384], lhsT=self.hT[:, k, t0:t0 + 128], rhs=Wh[:, k, 256:640], start=(k == 0), stop=(k == 7))
                    return ins

                def mmo(e, po=po, t0=t0):
                    ins = None
                    for k in range(8):
                        ins = e.matmul(po[:, 0:256], lhsT=self.hT[:, k, t0:t0 + 128], rhs=Wh[:, k, 640:896], start=(k == 0), stop=(k == 7))
                    return ins
                if part == "pe":
                    P.op("pe", mma, reads=[r_wh, self.r_h[c // 4]], writes=[self.r_ps[ia]])
                    P.op("pe", mmo, reads=[r_wh, self.r_h[c // 4]], writes=[self.r_ps[io]])
                    return
                cb_ = c % 2
                wsc = tokS[:, c, h:h + 1]
                P.op("act", lambda e, pa=pa, cb_=cb_, wsc=wsc: e.activation(out=kws[cb_], in_=pa[:, 0:128], func=AF.Copy, scale=wsc),
                     reads=[self.r_ps[ia], r_tok], writes=[r_kws[cb_]])
                P.op("act", lambda e, pa=pa, cb_=cb_: e.activation(out=vaug[cb_][:, 0:256], in_=pa[:, 128:384], func=AF.Copy),
                     reads=[self.r_ps[ia]], cowrites=[r_va[cb_]])
                P.op("act", lambda e, po=po, cb_=cb_: e.activation(out=eo[cb_], in_=po[:, 0:256], func=AF.Exp, scale=-1.0),
                     reads=[self.r_ps[io]], writes=[r_eo[cb_]])
                P.op("act", lambda e, cb_=cb_: e.activation(out=eo[cb_], in_=eo[cb_], func=AF.Ln, bias=1.0),
                     reads=[r_eo[cb_]], writes=[r_eo[cb_]])
                P.op("act", lambda e, cb_=cb_: e.activation(out=eo[cb_], in_=eo[cb_], func=AF.Exp, scale=-1.0),
                     reads=[r_eo[cb_]], writes=[r_eo[cb_]])
                P.op("dve", lambda e, cb_=cb_: e.tensor_tensor(out=eo[cb_], in0=eo[cb_], in1=hn_h, op=ALU.mult),
                     reads=[r_eo[cb_], r_hn], writes=[r_eo[cb_]])

            pT = self.ps[3].bitcast(BF16)

            def recur(c, part, h=h):
                t0 = c * 128
                csl = slice(t0, t0 + 128)
                tbq = c // 4
                cb_ = c % 2
                pn = self.ps[4 + cb_]
                r_pn = self.r_ps[4 + cb_]
                if part == "a":
                    P.op("pe", lambda e: e.matmul(self.ps[3][:, 0:128], lhsT=kpT[:, csl], rhs=qpT[:, csl], start=True, stop=True),
                         reads=[r_kp[tbq], r_qp[tbq]], writes=[self.r_ps[3]])
                    return
                if part == "b":
                    P.op("dve", lambda e: e.tensor_tensor(out=smT[cb_], in0=self.ps[3][:, 0:128], in1=tri, op=ALU.mult),
                         reads=[self.r_ps[3], rc], writes=[r_sm[cb_]])
                Cprev = Cb[(c + 1) % 2]

                def mmn(e):
                    if c > 0:
                        e.matmul(pn[:, 0:257], lhsT=qpT[:, csl], rhs=Cprev[:, 0:257], start=True, stop=False)
                    return e.matmul(pn[:, 0:257], lhsT=smT[cb_], rhs=vaug[cb_][:, 0:257], start=(c == 0), stop=True)
                rd = [r_qp[tbq], r_sm[cb_], r_va[cb_]]
                if c > 0:
                    rd.append(r_Cb[(c + 1) % 2])
                if part == "b":
                    P.op("pe", mmn, reads=rd, writes=[r_pn])
                if part == "b" and c < 15:
                    P.op("pe", lambda e: e.matmul(self.ps[6][:, 0:257], lhsT=kws[cb_], rhs=vaug[cb_][:, 0:257], start=True, stop=True),
                         reads=[r_kws[cb_], r_va[cb_]], writes=[self.r_ps[6]])
                    if c == 0:
                        P.op("dve", lambda e: e.tensor_copy(out=Cb[cb_][:, 0:257], in_=self.ps[6][:, 0:257]), reads=[self.r_ps[6]], writes=[r_Cb[cb_]])
                        P.op("dve", lambda e: e.tensor_copy(out=Cf[:, 0:257], in_=self.ps[6][:, 0:257]), reads=[self.r_ps[6]], writes=[r_Cf])
                    else:
                        dsc = dec_b[:, h * 16 + c:h * 16 + c + 1]
                        P.op("dve", lambda e: e.scalar_tensor_tensor(out=Cb[cb_][:, 0:257], in0=Cf[:, 0:257], scalar=dsc, in1=self.ps[6][:, 0:257],
                                                                     op0=ALU.mult, op1=ALU.add),
                             reads=[self.r_ps[6], r_Cf, r_dec], writes=[r_Cb[cb_]])
                        if c < 14:
                            P.op("dve", lambda e: e.scalar_tensor_tensor(out=Cf[:, 0:257], in0=Cf[:, 0:257], scalar=dsc, in1=self.ps[6][:, 0:257],
                                                                         op0=ALU.mult, op1=ALU.add),
                                 reads=[self.r_ps[6], r_Cf, r_dec], writes=[r_Cf])
                if part == "b":
                    return
                st = small2[:, 16 + 8 * cb_:24 + 8 * cb_]
                emt2 = tokS[:, c, 4 + h:5 + h]
                P.op("act", lambda e: e.activation(out=junk, in_=pn[:, 0:256], func=AF.Square, accum_out=st[:, 2:3]),
                     reads=[r_pn], writes=[r_junk], cowrites=[r_st])
                P.op("act", lambda e: e.activation(out=st[:, 6:7], in_=pn[:, 256:257], func=AF.Square),
                     reads=[r_pn], cowrites=[r_st])
                P.op("dve", lambda e: e.tensor_scalar(out=st[:, 0:1], in0=st[:, 6:7], scalar1=emt2, scalar2=EPS, op0=ALU.max, op1=ALU.mult),
                     reads=[r_st, r_tok], cowrites=[r_st])
                P.op("dve", lambda e: e.scalar_tensor_tensor(out=st[:, 1:2], in0=st[:, 2:3], scalar=1.0 / 256, in1=st[:, 0:1],
                                                             op0=ALU.mult, op1=ALU.add), reads=[r_st], cowrites=[r_st])
                P.op("act", lambda e: e.activation(out=st[:, 3:4], in_=st[:, 1:2], func=AF.Ln), reads=[r_st], cowrites=[r_st])
                P.op("act", lambda e: e.activation(out=st[:, 5:6], in_=st[:, 3:4], func=AF.Exp, scale=-0.5), reads=[r_st], cowrites=[r_st])
                P.op("dve", lambda e: e.scalar_tensor_tensor(out=yh[cb_], in0=pn[:, 0:256], scalar=st[:, 5:6], in1=eo[cb_],
                                                             op0=ALU.mult, op1=ALU.mult),
                     reads=[r_pn, r_st, r_eo[cb_]], writes=[r_yh[cb_]])

            def ytrans(c, h=h):
                t0 = c * 128
                csl = slice(t0, t0 + 128)
                tbq = c // 4
                cb_ = c % 2
                for j in range(2):
                    P.op("pe", lambda e, j=j: e.matmul(self.ps[7][:, 128 + j * 128:128 + (j + 1) * 128], lhsT=yh[cb_][:, j * 128:(j + 1) * 128],
                                                       rhs=self.ident_b, start=True, stop=True),
                         reads=[r_yh[cb_], rc], **(dict(writes=[self.r_ps[7]]) if j == 0 else dict(cowrites=[self.r_ps[7]])))
                P.op("act", lambda e: e.activation(out=yT[:, 2 * h:2 * h + 2, csl], in_=self.ps[7][:, 128:384].rearrange("p (a b) -> p a b", a=2), func=AF.Copy),
                     reads=[self.r_ps[7]], cowrites=[r_yT[2 * h][tbq], r_yT[2 * h + 1][tbq]])
            proj(0, "pe")
            proj(0, "ev")
            for c in range(16):
                recur(c, "a")
                if c + 1 < 16:
                    proj(c + 1, "pe")
                recur(c, "b")
                if c + 1 < 16:
                    proj(c + 1, "ev")
                recur(c, "c")
                if c > 0:
                    ytrans(c - 1)
            ytrans(15)
            if h + 1 < 4:
                load_head(h + 1)
        if s == 0 and l == 1 and self.dbg == 2:
            self.dump("qpT", qpT, [128, S], BF16)
            self.dump("kpT", kpT, [128, S], BF16)
            for j in range(8):
                self.dump("yT%d" % j, yT[:, j, :], [128, S], BF16)
            self.dump("small2", small2, [128, 64], F32)
        wo = self.WB[:, 0:8192].rearrange("p (h n) -> p h n", h=8)
        r_wo = Res("wo")
        for h2 in range(4):
            kw = dict(writes=[r_wh, r_wo]) if h2 == 0 else dict(cowrites=[r_wo])
            P.dma("pool", "wh", self.WB[:, h2 * 2048:(h2 + 1) * 2048], self.d_mlo[b, :, h2 * 2048:(h2 + 1) * 2048], **kw)
        it = 0
        for d in range(8):
            ga = self.mod_ap(l, 2, d, s)
            for tb in range(NTB):
                tsl = slice(tb * TB, (tb + 1) * TB)
                po = self.ps[it % 2]
                r_po = self.r_ps[it % 2]
                it += 1

                def mmd(e, po=po, d=d, tsl=tsl):
                    ins = None
                    for j in range(8):
                        ins = e.matmul(po[:, :], lhsT=wo[:, j, d * 128:(d + 1) * 128], rhs=yT[:, j, tsl], start=(j == 0), stop=(j == 7))
                    return ins
                P.op("pe", mmd, reads=[r_wo] + [r_yT[j][tb] for j in range(8)], writes=[r_po])
                xo = self.xT[:, d, tsl]
                P.op("dve", lambda e, xo=xo, po=po, ga=ga: e.scalar_tensor_tensor(
                    out=xo, in0=po[:, :], scalar=ga, in1=xo, op0=ALU.mult, op1=ALU.add),
                    reads=[r_po, self.r_mod], writes=[self.r_x[d][tb]])


def _consts():
    cf = np.zeros((128, NCF), np.float32)
    cf[:, CF_ID:CF_ID + 128] = np.eye(128, dtype=np.float32)
    cf[:, CF_ONE:CF_ONE + 128] = 1.0
    cf[:, CF_TRI:CF_TRI + 128] = np.triu(np.ones((128, 128), np.float32))
    for h in range(4):
        cf[h, CF_SEL + h * 128:CF_SEL + (h + 1) * 128] = 1.0
        cf[64 + h, CF_SEL + h * 128:CF_SEL + (h + 1) * 128] = 1.0
        cf[h, CF_SELA + h * 128:CF_SELA + (h + 1) * 128] = 1.0
        cf[64 + h, CF_SELB + h * 128:CF_SELB + (h + 1) * 128] = 1.0
    cb = np.zeros((128, NCB), np.float32)
    cb[:, CB_ID:CB_ID + 128] = np.eye(128, dtype=np.float32)
    cb[:, CB_ONE:CB_ONE + 128] = 1.0
    cb[0, CB_MROW + 64:CB_MROW + 128] = -30000.0
    return cf, cb.astype(ml_dtypes.bfloat16)


def _col(v, nchunk):
    return np.ascontiguousarray(np.asarray(v, np.float32).reshape(nchunk, 128).T)


def prep_shared(inp):
    f32 = np.float32
    sh = {}
    pp = np.zeros((128, NPP), f32)
    for l in range(DEPTH):
        pp[:, PP_MODB + l * 48:PP_MODB + (l + 1) * 48] = _col(inp["mod_b"][l], 48)
        for tap in range(3):
            pp[:, PP_CW + l * 66 + tap * 22:PP_CW + l * 66 + (tap + 1) * 22] = _col(inp["ffn_conv_w"][l, tap], 22)
        pp[:, PP_CB + l * 22:PP_CB + (l + 1) * 22] = _col(inp["ffn_conv_b"][l], 22)
    for a in range(2):
        pp[:, PP_QN + a * 4:PP_QN + (a + 1) * 4] = _col(inp["mla_q_norm"][a], 4)
        pp[:, PP_KVN + a * 2:PP_KVN + (a + 1) * 2] = _col(inp["mla_kv_norm"][a], 2)
        pp[0:4, PP_GB + a * 2] = np.asarray(inp["ml_b_gates"][a][0:4], f32)
        pp[0:4, PP_GB + a * 2 + 1] = np.asarray(inp["ml_b_gates"][a][4:8], f32)
    pp[:, PP_FN:PP_FN + 8] = _col(inp["final_norm"], 8)
    sh["pp"] = pp
    cf, cb = _consts()
    sh["cf"] = cf
    sh["cb"] = cb
    mw = np.asarray(inp["mod_w"], f32)
    sh["modw"] = np.ascontiguousarray(mw.reshape(DEPTH, 8, 128, 12, 512).transpose(0, 3, 2, 1, 4))
    hn = np.asarray(inp["ml_head_norm"], f32)
    sh["hn"] = np.ascontiguousarray(np.broadcast_to(hn[None, :, :], (128, 2, 1024)))
    wu = np.asarray(inp["ffn_w_up"], f32)
    wa = wu[:, :, :DFF].reshape(DEPTH, 8, 128, NFC, 128)
    wg = wu[:, :, DFF:].reshape(DEPTH, 8, 128, NFC, 128)
    fup = np.concatenate([wa, wg], axis=-1)
    sh["fup"] = np.ascontiguousarray(fup.transpose(0, 3, 2, 1, 4))
    wd = np.asarray(inp["ffn_w_down"], f32)
    wd = wd.reshape(DEPTH, 2, 11, 128, 8, 128)
    sh["fdn"] = np.ascontiguousarray(wd.transpose(0, 1, 4, 3, 2, 5))
    jj = np.arange(0, 64, 2, dtype=f32) / f32(64)
    invf = (f32(1.0) / (f32(10000.0) ** jj)).astype(f32)
    pp[0:64, PP_IF] = np.concatenate([invf, invf])
    win = np.asarray(inp["mla_w_in"], f32)
    win = np.concatenate([win, win[:, :, 800:832], win[:, :, 768:800]], axis=-1)
    sh["mwin"] = np.ascontiguousarray(win.reshape(2, 8, 128, 896).transpose(0, 2, 1, 3).reshape(2, 128, 8 * 896))
    wq = np.asarray(inp["mla_w_q_up"], f32).reshape(2, 4, 128, 8, 192)
    wq = np.concatenate([wq, wq[..., 160:192], wq[..., 128:160]], axis=-1)
    sh["mwq"] = np.ascontiguousarray(wq.transpose(0, 3, 2, 1, 4).reshape(2, 8, 128, 4 * 256))
    wkv = np.asarray(inp["mla_w_kv_up"], f32).reshape(2, 2, 128, 8, 256)
    sh["mwkv"] = np.ascontiguousarray(wkv.transpose(0, 3, 2, 1, 4).reshape(2, 8, 128, 2 * 256))
    wo = np.asarray(inp["mla_w_out"], f32).reshape(2, 8, 128, 1024)
    sh["mwo"] = np.ascontiguousarray(wo.transpose(0, 2, 1, 3).reshape(2, 128, 8 * 1024))
    mw_ = np.asarray(inp["ml_w_in"], f32).reshape(2, 8, 128, 3080)
    heads = []
    for h in range(4):
        heads.append(np.concatenate([mw_[..., h * 128:(h + 1) * 128], mw_[..., 512 + h * 128:512 + (h + 1) * 128],
                                     mw_[..., 512 + h * 128:512 + (h + 1) * 128],
                                     mw_[..., 1024 + h * 256:1024 + (h + 1) * 256],
                                     mw_[..., 2048 + h * 256:2048 + (h + 1) * 256]], axis=-1))
    mlw = np.stack(heads, axis=1)
    sh["mlw"] = np.ascontiguousarray(mlw.transpose(0, 1, 3, 2, 4).reshape(2, 4, 128, 8 * 896))
    mlg = np.zeros((2, 2, 128, 8, 128), f32)
    for g in range(2):
        mlg[:, g, :, :, 0:4] = mw_[..., 3072 + 4 * g:3076 + 4 * g].transpose(0, 2, 1, 3)
    sh["mlg"] = np.ascontiguousarray(mlg.reshape(2, 2, 128, 8 * 128))
    mo = np.asarray(inp["ml_w_out"], f32).reshape(2, 8, 128, 1024)
    sh["mlo"] = np.ascontiguousarray(mo.transpose(0, 2, 1, 3).reshape(2, 128, 8 * 1024))
    return sh


def prep_core(inp, core, n_seq=2):
    b0 = core * n_seq
    x = np.asarray(inp["x"][b0:b0 + n_seq], np.float32)
    xT = np.ascontiguousarray(x.reshape(n_seq, S, 8, 128).transpose(0, 3, 2, 1))
    c = np.asarray(inp["c"][b0:b0 + n_seq], np.float32)
    cT = np.ascontiguousarray(c.reshape(n_seq, 8, 128).transpose(2, 1, 0))
    pos = np.ascontiguousarray(np.asarray(inp["positions"][b0:b0 + n_seq], np.int32))
    return {"xT": xT, "cT": cT, "pos": pos}


def unpack_out(outT):
    ns = outT.shape[0]
    return np.ascontiguousarray(outT.transpose(0, 3, 2, 1).reshape(ns, S, D))


_CACHE = {}


def get_nc(stages=None, dbg=False):
    key = (tuple(stages) if stages is not None else None, dbg)
    if key not in _CACHE:
        b = Builder(stages=stages, dbg=dbg)
        nc = b.build()
        _CACHE[key] = (nc, b)
    return _CACHE[key]


def run(inp, cores=range(8), stages=None, trace=False, dbg=False):
    nc, b = get_nc(stages, dbg)
    sh = prep_shared(inp)
    in_maps = []
    for c in cores:
        m = dict(sh)
        m.update(prep_core(inp, c))
        in_maps.append(m)
    res = run_bass_kernel_spmd(nc, in_maps, core_ids=list(range(len(in_maps))), trace=trace)
    outs = [unpack_out(r["outT"]) for r in res.results]
    return np.concatenate(outs, axis=0), res


def kernel(**inputs):
    out, _ = run(inputs)
    return out.astype(np.float32)
```
